# Optimizing a Trainium2 kernel written in Bass

```python
import jax, jax.numpy as jnp
from jax import lax
import numpy as np

D_MODEL = 1024
BATCH = 16
SEQ = 2048
DEPTH = 1

GRID_W = 64
Q_BLOCK = 128
EPS = 1e-6
ROPE_THETA = 10000.0

MLA_HEADS = 4
MLA_Q_RANK = 384
MLA_KV_RANK = 256
MLA_NOPE = 128
MLA_ROPE = 64
MLA_V = 128
MLA_WIDTH = MLA_HEADS * MLA_V

GQA_HEADS = 4
GQA_KV_HEADS = 2
GQA_HEAD_DIM = 128
GQA_WIDTH = GQA_HEADS * GQA_HEAD_DIM

MIX_WIDTH = MLA_WIDTH + GQA_WIDTH

OFF_MLA_Q = 0
OFF_MLA_KV = OFF_MLA_Q + MLA_Q_RANK
OFF_MLA_KPE = OFF_MLA_KV + MLA_KV_RANK
OFF_GQA_Q = OFF_MLA_KPE + MLA_ROPE
OFF_GQA_K = OFF_GQA_Q + GQA_HEADS * GQA_HEAD_DIM
OFF_GQA_V = OFF_GQA_K + GQA_KV_HEADS * GQA_HEAD_DIM
IN_WIDTH = OFF_GQA_V + GQA_KV_HEADS * GQA_HEAD_DIM

D_FF = 2816
CONV_W = 3

kernel_name = "hybrid_mla_axialgqa_convffn_block"


def rmsnorm(x, g):
    xf = x.astype(jnp.float32)
    y = xf * lax.rsqrt(jnp.mean(xf * xf, axis=-1, keepdims=True) + EPS)
    return (y * g.astype(jnp.float32)).astype(x.dtype)


def rope_tables(pos, dim):
    inv = ROPE_THETA ** (-jnp.arange(0, dim, 2, dtype=jnp.float32) / dim)
    ang = pos.astype(jnp.float32)[:, None] * inv[None, :]
    return jnp.cos(ang), jnp.sin(ang)


def apply_rope(x, cos, sin):
    half = x.shape[-1] // 2
    x1, x2 = x[..., :half], x[..., half:]
    c = cos[None, :, None, :].astype(x.dtype)
    s = sin[None, :, None, :].astype(x.dtype)
    return jnp.concatenate([x1 * c - x2 * s, x1 * s + x2 * c], axis=-1)


def blocked_attention(q, k, v):
    B, S, H, dq = q.shape
    Hkv, dv = k.shape[2], v.shape[-1]
    G = H // Hkv
    nb = S // Q_BLOCK
    scale = dq ** -0.5
    qb = q.reshape(B, nb, Q_BLOCK, Hkv, G, dq).transpose(1, 0, 2, 3, 4, 5)

    def one_block(q_blk):
        s = jnp.einsum('bqhgd,bkhd->bhgqk', q_blk, k,
                       preferred_element_type=jnp.float32) * scale
        p = jax.nn.softmax(s, axis=-1).astype(v.dtype)
        return jnp.einsum('bhgqk,bkhd->bqhgd', p, v)

    out = lax.map(one_block, qb)
    return out.transpose(1, 0, 2, 3, 4, 5).reshape(B, S, H, dv)


def token_mixer(u, w_in, mla_q_norm_g, mla_w_q_up, mla_kv_norm_g, mla_w_kv_up,
                gqa_q_norm_g, gqa_k_norm_g, group_norm_mla_g, group_norm_gqa_g,
                w_out, cos1, sin1, cos_r, sin_r, cos_c, sin_c):
    B, S, _ = u.shape
    z = u @ w_in

    c_q = rmsnorm(z[..., OFF_MLA_Q:OFF_MLA_KV], mla_q_norm_g)
    c_kv = rmsnorm(z[..., OFF_MLA_KV:OFF_MLA_KPE], mla_kv_norm_g)
    k_pe = z[..., OFF_MLA_KPE:OFF_GQA_Q][:, :, None, :]
    q = (c_q @ mla_w_q_up).reshape(B, S, MLA_HEADS, MLA_NOPE + MLA_ROPE)
    q_pe = apply_rope(q[..., MLA_NOPE:], cos1, sin1)
    q_m = jnp.concatenate([q[..., :MLA_NOPE], q_pe], axis=-1)
    kv = (c_kv @ mla_w_kv_up).reshape(B, S, MLA_HEADS, MLA_NOPE + MLA_V)
    k_pe = jnp.broadcast_to(apply_rope(k_pe, cos1, sin1), (B, S, MLA_HEADS, MLA_ROPE))
    k_m = jnp.concatenate([kv[..., :MLA_NOPE], k_pe], axis=-1)
    o_mla = blocked_attention(q_m, k_m, kv[..., MLA_NOPE:])

    half = GQA_HEAD_DIM // 2
    qg = z[..., OFF_GQA_Q:OFF_GQA_K].reshape(B, S, GQA_HEADS, GQA_HEAD_DIM)
    kg = z[..., OFF_GQA_K:OFF_GQA_V].reshape(B, S, GQA_KV_HEADS, GQA_HEAD_DIM)
    vg = z[..., OFF_GQA_V:IN_WIDTH].reshape(B, S, GQA_KV_HEADS, GQA_HEAD_DIM)
    qg = rmsnorm(qg, gqa_q_norm_g)
    kg = rmsnorm(kg, gqa_k_norm_g)

    def axial(t):
        return jnp.concatenate([apply_rope(t[..., :half], cos_r, sin_r),
                                apply_rope(t[..., half:], cos_c, sin_c)], axis=-1)

    o_gqa = blocked_attention(axial(qg), axial(kg), vg)

    o = jnp.concatenate([rmsnorm(o_mla.reshape(B, S, MLA_WIDTH), group_norm_mla_g),
                         rmsnorm(o_gqa.reshape(B, S, GQA_WIDTH), group_norm_gqa_g)], axis=-1)
    return o @ w_out


def conv_ffn(u, w_up, conv_w, conv_b, w_down):
    hu = u @ w_up
    C = hu.shape[-1]
    hc = lax.conv_general_dilated(
        hu, conv_w[:, None, :].astype(hu.dtype), window_strides=(1,),
        padding=((CONV_W // 2, CONV_W // 2),),
        dimension_numbers=('NWC', 'WIO', 'NWC'), feature_group_count=C) + conv_b
    gate, val = hc[..., :D_FF], hc[..., D_FF:]
    return (jax.nn.silu(gate) * val) @ w_down


def setup_inputs(seed: int = 0) -> dict:
    key = jax.random.key(seed)
    ks = jax.random.split(key, 20)
    f32 = jnp.float32

    def w(k, shape, fan_in):
        return jax.random.normal(k, shape, f32) * fan_in ** -0.5

    def gain(k, shape):
        return 1.0 + 0.02 * jax.random.normal(k, shape, f32)

    L = DEPTH
    return {
        "x": jax.random.normal(ks[0], (BATCH, SEQ, D_MODEL), f32),
        "norm1_g": gain(ks[1], (L, D_MODEL)),
        "w_in": w(ks[2], (L, D_MODEL, IN_WIDTH), D_MODEL),
        "mla_q_norm_g": gain(ks[3], (L, MLA_Q_RANK)),
        "mla_w_q_up": w(ks[4], (L, MLA_Q_RANK, MLA_HEADS * (MLA_NOPE + MLA_ROPE)), MLA_Q_RANK),
        "mla_kv_norm_g": gain(ks[5], (L, MLA_KV_RANK)),
        "mla_w_kv_up": w(ks[6], (L, MLA_KV_RANK, MLA_HEADS * (MLA_NOPE + MLA_V)), MLA_KV_RANK),
        "gqa_q_norm_g": gain(ks[7], (L, GQA_HEAD_DIM)),
        "gqa_k_norm_g": gain(ks[8], (L, GQA_HEAD_DIM)),
        "group_norm_mla_g": gain(ks[9], (L, MLA_WIDTH)),
        "group_norm_gqa_g": gain(ks[10], (L, GQA_WIDTH)),
        "w_out": w(ks[11], (L, MIX_WIDTH, D_MODEL), MIX_WIDTH),
        "norm2_g": gain(ks[12], (L, D_MODEL)),
        "ffn_w_up": w(ks[13], (L, D_MODEL, 2 * D_FF), D_MODEL),
        "ffn_conv_w": w(ks[14], (L, CONV_W, 2 * D_FF), CONV_W),
        "ffn_conv_b": 0.02 * jax.random.normal(ks[15], (L, 2 * D_FF), f32),
        "ffn_w_down": w(ks[16], (L, D_FF, D_MODEL), D_FF),
        "final_norm_g": gain(ks[17], (D_MODEL,)),
    }


def reference(x, norm1_g, w_in, mla_q_norm_g, mla_w_q_up, mla_kv_norm_g, mla_w_kv_up,
              gqa_q_norm_g, gqa_k_norm_g, group_norm_mla_g, group_norm_gqa_g, w_out,
              norm2_g, ffn_w_up, ffn_conv_w, ffn_conv_b, ffn_w_down, final_norm_g):
    S = x.shape[1]
    rows = S // GRID_W
    t = jnp.arange(S, dtype=jnp.int32)
    row = jnp.repeat(jnp.arange(rows, dtype=jnp.int32), GRID_W)
    col = jnp.tile(jnp.arange(GRID_W, dtype=jnp.int32), rows)
    cos1, sin1 = rope_tables(t, MLA_ROPE)
    cos_r, sin_r = rope_tables(row, GQA_HEAD_DIM // 2)
    cos_c, sin_c = rope_tables(col, GQA_HEAD_DIM // 2)

    h = x
    for l in range(DEPTH):
        u = rmsnorm(h, norm1_g[l])
        h = h + token_mixer(u, w_in[l], mla_q_norm_g[l], mla_w_q_up[l], mla_kv_norm_g[l],
                            mla_w_kv_up[l], gqa_q_norm_g[l], gqa_k_norm_g[l],
                            group_norm_mla_g[l], group_norm_gqa_g[l], w_out[l],
                            cos1, sin1, cos_r, sin_r, cos_c, sin_c)
        u = rmsnorm(h, norm2_g[l])
        h = h + conv_ffn(u, ffn_w_up[l], ffn_conv_w[l], ffn_conv_b[l], ffn_w_down[l])
    return rmsnorm(h, final_norm_g)
```

```python
import contextlib
import numpy as np
import ml_dtypes
import concourse.bass as bass
import concourse.mybir as mybir
from concourse.bass_utils import run_bass_kernel_spmd

F32 = mybir.dt.float32
BF16 = mybir.dt.bfloat16
AF = mybir.ActivationFunctionType
ALU = mybir.AluOpType

SB_LO = 16512
SB_HI = 229344
EPS = 1e-6
S = 2048
D = 1024
NB = 4
DFF = 2816
NJ = 22
GROUPS = [6, 6, 6, 4]

C_GQ, C_GKV, C_GQQ, C_GQK, C_GGM, C_GGG, C_CW, C_CB, NCOLS = 0, 3, 5, 6, 7, 11, 15, 147, 191


class Tile:
    __slots__ = ("w", "rs", "excl")

    def __init__(self, excl=False):
        self.w = None
        self.rs = {}
        self.excl = excl


class Ev:
    __slots__ = ("eng", "emit", "deps", "val", "sem", "dma", "needed", "pre")

    def __init__(self, eng, emit, dma):
        self.eng = eng
        self.emit = emit
        self.dma = dma
        self.deps = ()
        self.val = None
        self.sem = None
        self.needed = False
        self.pre = None


class Arena:
    def __init__(self, nc):
        self.nc = nc
        self.live = {}
        self.pending = []
        self.n = 0

    def alloc(self, name, shape, dtype):
        esz = 2 if dtype == BF16 else 4
        free = 1
        for d in shape[1:]:
            free *= d
        nbytes = ((free * esz + 63) // 64) * 64
        pos = SB_LO
        for lo, hi in sorted(self.live.values()):
            if pos + nbytes <= lo:
                break
            pos = max(pos, hi)
        if pos + nbytes > SB_HI:
            raise RuntimeError(f"SBUF arena full allocating {name} ({nbytes}B), used={self.used()}")
        assert name not in self.live, name
        self.live[name] = (pos, pos + nbytes)
        self.n += 1
        return self.nc.alloc_sbuf_tensor_at(f"{name}_{self.n}", list(shape), dtype, offset=pos)

    def free(self, *names):
        self.pending.extend(names)

    def release(self):
        for n in self.pending:
            del self.live[n]
        self.pending = []

    def used(self):
        return sum(hi - lo for lo, hi in self.live.values())


class Prog:
    ENGS = ("pe", "act", "dve", "pool", "sp")
    NRING = 8

    def __init__(self):
        self.q = {e: [] for e in self.ENGS}
        self.ndma = {e: 0 for e in self.ENGS}
        self.last_compute = {e: None for e in self.ENGS}
        self.recent_dma = {e: [] for e in self.ENGS}
        self.cap = None

    def capture(self):
        self.cap = []

    def end_capture(self):
        l, self.cap = self.cap, None
        return l

    def replay(self, *lists):
        idx = [0] * len(lists)
        total = sum(len(l) for l in lists)
        for _ in range(total):
            best, bk = None, None
            for k, l in enumerate(lists):
                if idx[k] < len(l):
                    frac = idx[k] / len(l)
                    if best is None or frac < best:
                        best, bk = frac, k
            a = lists[bk][idx[bk]]
            idx[bk] += 1
            self.op(*a)

    def op(self, eng, emit, reads=(), writes=(), dma=False):
        if self.cap is not None:
            self.cap.append((eng, emit, list(reads), list(writes), dma))
            return None
        ev = Ev(eng, emit, dma)
        deps = set()
        xr = [t for t in reads if t.excl]
        if xr:
            reads = [t for t in reads if not t.excl]
            writes = list(writes) + xr
        for t in reads:
            if t.w is not None:
                deps.add(t.w)
        for t in writes:
            if t.w is not None:
                deps.add(t.w)
            deps.update(t.rs.values())
        if dma:
            i = self.ndma[eng]
            self.ndma[eng] = i + 1
            ev.sem = (eng, i % self.NRING)
            ev.val = 16 * (i // self.NRING + 1)
            rd = self.recent_dma[eng]
            if len(rd) >= self.NRING:
                ev.pre = rd[-self.NRING]
            rd.append(ev)
            if len(rd) > 2 * self.NRING:
                del rd[0]
        else:
            self.last_compute[eng] = ev
        ev.deps = [d for d in deps
                   if not (d.eng == "pe" and eng == "pe" and not d.dma and not dma)]
        for t in reads:
            t.rs[("dma", id(ev)) if dma else eng] = ev
        for t in writes:
            t.w = ev
            t.rs = {}
        self.q[eng].append(ev)
        return ev

    def barrier(self):
        deps = [e for e in self.last_compute.values() if e is not None]
        for eng in self.ENGS:
            deps.extend(self.recent_dma[eng][-self.NRING:])
        for eng in self.ENGS:
            ev = Ev(eng, None, False)
            ev.deps = list(deps)
            self.q[eng].append(ev)

    def finalize(self, sems):
        for eng in self.ENGS:
            for ev in self.q[eng]:
                for d in ev.deps:
                    d.needed = True
        for eng in self.ENGS:
            cnt = 0
            for ev in self.q[eng]:
                if ev.dma or ev.emit is None:
                    continue
                if ev.needed:
                    cnt += 1
                    ev.val = cnt
                    ev.sem = eng
        self.sems = sems

    def emit_engine(self, eng, handle):
        seen = {}
        sems = self.sems
        for ev in self.q[eng]:
            need = {}
            for d in ev.deps:
                if need.get(d.sem, 0) < d.val:
                    need[d.sem] = d.val
            if ev.pre is not None:
                d = ev.pre
                if need.get(d.sem, 0) < d.val:
                    need[d.sem] = d.val
            for s, v in need.items():
                if seen.get(s, 0) < v:
                    handle.wait_ge(sems[s], v)
                    seen[s] = v
            if ev.emit is None:
                continue
            inst = ev.emit(handle)
            if ev.dma:
                inst.then_inc(sems[ev.sem], 16)
            elif ev.needed:
                inst.then_inc(sems[ev.sem], 1)


class _Stop(Exception):
    pass


def build_nc(stop=None, dbg=None):
    nc = bass.Bass("TRN2", target_bir_lowering=False)
    DBG = {}

    def din(name, shape, dt=F32):
        return nc.dram_tensor(name, list(shape), dt, kind="ExternalInput").ap()

    x = din("x", [2, S, D])
    w_in = din("w_in", [D, 1728])
    wq = din("wq", [384, 768])
    wkv = din("wkv", [256, 1024])
    w_out = din("w_out", [D, D])
    w_up = din("w_up", [D, 2 * DFF])
    w_down = din("w_down", [DFF, D])
    cst_d = din("cst", [128, 384], BF16)
    cols_d = din("cols", [128, NCOLS])
    grows_d = din("grows", [3, 128, D])
    rope1_d = din("rope1", [2, 128, S])
    ropeg_d = din("ropeg", [128, 128])
    y = nc.dram_tensor("y", [2, S, D], F32, kind="ExternalOutput").ap()

    A = Arena(nc)
    P = Prog()
    PS2 = [nc.alloc_psum_tensor(f"ps{i}", [128, 1024], F32) for i in range(4)]
    T_ps = [Tile(excl=True) for _ in range(8)]

    def bank(b):
        return PS2[b // 2][:, (b % 2) * 512:(b % 2 + 1) * 512]

    rr = [0]

    def nb():
        b = rr[0] % 8
        rr[0] += 1
        return b

    cst = A.alloc("cst", [128, 384], BF16)
    cols = A.alloc("cols", [128, NCOLS], F32)
    ropeg = A.alloc("ropeg", [128, 128], F32)
    st = A.alloc("st", [128, 16], F32)
    T_cst, T_cols, T_ropeg = Tile(), Tile(), Tile()
    T_st = [Tile() for _ in range(4)]
    st2 = A.alloc("st2", [128, 48], F32)
    ident, ones, perm = cst[:, 0:128], cst[:, 128:256], cst[:, 256:384]
    P.op("sp", lambda e: e.dma_start(out=cst[:], in_=cst_d), writes=[T_cst], dma=True)
    P.op("sp", lambda e: e.dma_start(out=cols[:], in_=cols_d), writes=[T_cols], dma=True)
    P.op("sp", lambda e: e.dma_start(out=ropeg[:], in_=ropeg_d), writes=[T_ropeg], dma=True)

    def col(c):
        return cols[:, c:c + 1]

    stcnt = [0]

    def mm_group(out_ap, pairs, first=True, last=True):
        def f(e):
            n = len(pairs)
            inst = None
            for i, (l, r) in enumerate(pairs):
                inst = e.matmul(out_ap, lhsT=l, rhs=r, start=(first and i == 0), stop=(last and i == n - 1))
            return inst
        return f

    def rstd_cols(eng_src_ap, width, n_feat, Tsrc, Tdst_list, junk, T_junk):
        pass

    def make_uT(src_aps, src_tiles, grow, T_grow, uT, T_uT, tis, junk, T_junk, ubf, T_ubf, evac_eng, rstd=None):
        for idx, ti in enumerate(tis):
            src, Ts = src_aps[idx], src_tiles[idx]
            k = stcnt[0] % 4
            stcnt[0] += 1
            sl = (stcnt[0] - 1) % len(ubf)
            c0 = 4 * k
            if rstd is None:
                P.op("act", lambda e, src=src, c0=c0: e.activation(out=junk[:], in_=src, func=AF.Square,
                                                                     accum_out=st[:, c0:c0 + 1]),
                     reads=[Ts], writes=[T_junk, T_st[k]])
                P.op("act", lambda e, c0=c0: e.activation(out=st[:, c0 + 1:c0 + 2], in_=st[:, c0:c0 + 1], func=AF.Ln,
                                                          scale=1.0 / D, bias=EPS),
                     reads=[T_st[k]], writes=[T_st[k]])
                P.op("act", lambda e, c0=c0: e.activation(out=st[:, c0 + 2:c0 + 3], in_=st[:, c0 + 1:c0 + 2], func=AF.Exp,
                                                          scale=-0.5),
                     reads=[T_st[k]], writes=[T_st[k]])
                rs_ap, T_rs = st[:, c0 + 2:c0 + 3], T_st[k]
            else:
                rs_ap, T_rs = rstd[idx]
            P.op("dve", lambda e, src=src, rs_ap=rs_ap, sl=sl: e.scalar_tensor_tensor(
                out=ubf[sl][:], in0=src, scalar=rs_ap, in1=grow[:], op0=ALU.mult, op1=ALU.mult),
                reads=[Ts, T_rs, T_grow], writes=[T_ubf[sl]])
            b = nb()
            psT = bank(b).bitcast(BF16)

            def tr(e, sl=sl, psT=psT):
                inst = None
                for c in range(8):
                    inst = e.transpose(out=psT[:, c * 128:(c + 1) * 128], in_=ubf[sl][:, c * 128:(c + 1) * 128],
                                       identity=ident)
                return inst
            P.op("pe", tr, reads=[T_ubf[sl], T_cst], writes=[T_ps[b]])
            t0 = ti * 128
            if evac_eng == "act":
                P.op("act", lambda e, psT=psT, t0=t0: e.activation(
                    out=uT[:, :, t0:t0 + 128], in_=psT.rearrange("p (c t) -> p c t", c=8), func=AF.Copy),
                    reads=[T_ps[b]], writes=[T_uT[ti]])
            else:
                P.op("dve", lambda e, psT=psT, t0=t0: e.tensor_copy(
                    out=uT[:, :, t0:t0 + 128], in_=psT.rearrange("p (c t) -> p c t", c=8)),
                    reads=[T_ps[b]], writes=[T_uT[ti]])

    def bstats(sq_aps, T_sq, nfeat, lnbias, lnt, T_lnt, rdst, T_rdst):
        b = nb()
        P.op("pe", mm_group(bank(b), [(ones, a) for a in sq_aps]), reads=T_sq + [T_cst], writes=[T_ps[b]])
        P.op("act", lambda e, b=b: e.activation(out=lnt[:], in_=bank(b), func=AF.Ln, scale=1.0 / nfeat, bias=EPS),
             reads=[T_ps[b]], writes=[T_lnt])
        P.op("act", lambda e: e.activation(out=rdst[:], in_=lnt[:], func=AF.Exp, scale=-0.5, bias=float(lnbias)),
             reads=[T_lnt], writes=[T_rdst])

    def chk(s_, k_):
        if stop is not None and (s_, k_) == tuple(stop):
            raise _Stop()

    def seq(s):
        w_in_sb = A.alloc("w_in", [128, 8, 1792], BF16)
        uT = A.alloc("uT", [128, 8, S], BF16)
        T_win, T_uT = Tile(), [Tile() for _ in range(16)]
        w_in_v = w_in.rearrange("(c p) n -> p c n", p=128)
        P.op("pool", lambda e: e.dma_start(out=w_in_sb[:, :, 0:704], in_=w_in_v[:, :, 0:704]), writes=[T_win], dma=True)
        P.op("pool", lambda e: e.dma_start(out=w_in_sb[:, :, 704:768], in_=w_in_v[:, :, 640:704]), writes=[T_win], dma=True)
        P.op("pool", lambda e: e.dma_start(out=w_in_sb[:, :, 768:1792], in_=w_in_v[:, :, 704:1728]), writes=[T_win], dma=True)

        wq_sb = A.alloc("wq", [128, 3, 768], BF16)
        wkv_sb = A.alloc("wkv", [128, 2, 1024], BF16)
        rope1 = A.alloc("rope1", [128, 2, S], F32)
        grow = A.alloc("grow", [128, D], F32)
        T_wq, T_wkv, T_rope1, T_grow = Tile(), Tile(), Tile(), Tile()
        P.op("pool", lambda e: e.dma_start(out=wq_sb[:], in_=wq.rearrange("(c p) n -> p c n", p=128)), writes=[T_wq], dma=True)
        P.op("pool", lambda e: e.dma_start(out=wkv_sb[:], in_=wkv.rearrange("(c p) n -> p c n", p=128)), writes=[T_wkv], dma=True)
        P.op("sp", lambda e: e.dma_start(out=rope1[:], in_=rope1_d.rearrange("a p t -> p a t")), writes=[T_rope1], dma=True)
        P.op("sp", lambda e: e.dma_start(out=grow[:], in_=grows_d[0]), writes=[T_grow], dma=True)
        Qn = A.alloc("Qn", [128, 4, S], BF16)
        Qpe = A.alloc("Qpe", [128, 2, S], BF16)
        Kn = A.alloc("Kn", [128, 4, S], BF16)
        Kpl = A.alloc("Kpl", [128, S], BF16)
        Kph = A.alloc("Kph", [128, S], BF16)
        Vm = A.alloc("Vm", [128, 16, 512], BF16)
        T_Q = [Tile() for _ in range(NB)]
        DBG.update(uT=uT, Qn=Qn, Qpe=Qpe, Kn=Kn, Kpe=Kpl, Vm=Vm, w_in_sb=w_in_sb)
        T_K, T_V = Tile(), Tile()
        T_Kpe = Tile()
        P.op("pool", lambda e: e.memset(Kpl[64:128, :], 0.0), writes=[T_Kpe])
        P.op("pool", lambda e: e.memset(Kph[0:64, :], 0.0), writes=[T_Kpe])
        T_Qpe = [Tile() for _ in range(NB)]
        xt = [A.alloc(f"xt{i}", [128, D], F32) for i in range(2)]
        T_xt = [Tile(), Tile()]
        ubf = [A.alloc(f"ubf{i}", [128, D], BF16) for i in range(2)]
        T_ubf = [Tile(), Tile()]
        junk = A.alloc("junk", [128, D], BF16)
        T_junk = Tile()
        cqs = [A.alloc(f"cq{i}", [128, 3, 512], BF16) for i in range(2)]
        sqs = [A.alloc(f"sq{i}", [128, 3, 512], BF16) for i in range(2)]
        ckvs = [A.alloc(f"ckv{i}", [128, 2, 512], BF16) for i in range(2)]
        sqkvs = [A.alloc(f"sqkv{i}", [128, 2, 512], BF16) for i in range(2)]
        T_cqs, T_sqs, T_ckvs, T_sqkvs = ([Tile(), Tile()] for _ in range(4))
        lnt = A.alloc("lnt", [128, 512], F32)
        rq = A.alloc("rq", [128, 512], F32)
        rkv = A.alloc("rkv", [128, 512], F32)
        T_lnt, T_rq, T_rkv = Tile(), Tile(), Tile()
        rkc = A.alloc("rkc", [128, 8], F32)
        T_rkc = Tile()
        qpfs = [A.alloc(f"qpf{i}", [128, 512], F32) for i in range(2)]
        qpbs = [A.alloc(f"qpb{i}", [128, 512], BF16) for i in range(2)]
        t1s = [A.alloc("t10", [128, 512], F32)] * 2
        t2s = [A.alloc("t20", [128, 512], F32)] * 2
        T_qpfs, T_qpbs = ([Tile(), Tile()] for _ in range(2))
        T_t1s, T_t2s = [Tile()] * 2, [Tile()] * 2
        xcnt = [0]
        chn = [0]

        def front(n):
            blk = slice(n * 512, (n + 1) * 512)
            cq, sq, ckv, sqkv = cqs[n % 2], sqs[n % 2], ckvs[n % 2], sqkvs[n % 2]
            T_cq, T_sq, T_ckv, T_sqkv = T_cqs[n % 2], T_sqs[n % 2], T_ckvs[n % 2], T_sqkvs[n % 2]
            for i in range(4):
                ti = n * 4 + i
                sl = xcnt[0] % 2
                xcnt[0] += 1
                P.op("sp", lambda e, sl=sl, ti=ti: e.dma_start(out=xt[sl][:], in_=x[s, ti * 128:(ti + 1) * 128, :]),
                     writes=[T_xt[sl]], dma=True)
                make_uT([xt[sl][:]], [T_xt[sl]], grow, T_grow, uT, T_uT, [ti], junk, T_junk, ubf, T_ubf, "act")
            uT_blk = [uT[:, k, blk] for k in range(8)]
            T_uTb = [T_uT[n * 4 + i] for i in range(4)]
            for c in range(3):
                b = nb()
                P.op("pe", mm_group(bank(b), [(w_in_sb[:, k, c * 128:(c + 1) * 128], uT_blk[k]) for k in range(8)]),
                     reads=T_uTb + [T_win], writes=[T_ps[b]])
                P.op("dve", lambda e, b=b, c=c: e.tensor_scalar(out=cq[:, c, :], in0=bank(b), scalar1=col(C_GQ + c),
                                                                 scalar2=None, op0=ALU.mult),
                     reads=[T_ps[b], T_cols], writes=[T_cq])
                P.op("act", lambda e, b=b, c=c: e.activation(out=sq[:, c, :], in_=bank(b), func=AF.Square),
                     reads=[T_ps[b]], writes=[T_sq])
            for c in range(2):
                b = nb()
                P.op("pe", mm_group(bank(b), [(w_in_sb[:, k, 384 + c * 128:384 + (c + 1) * 128], uT_blk[k]) for k in range(8)]),
                     reads=T_uTb + [T_win], writes=[T_ps[b]])
                P.op("dve", lambda e, b=b, c=c: e.tensor_scalar(out=ckv[:, c, :], in0=bank(b), scalar1=col(C_GKV + c),
                                                                 scalar2=None, op0=ALU.mult),
                     reads=[T_ps[b], T_cols], writes=[T_ckv])
                P.op("act", lambda e, b=b, c=c: e.activation(out=sqkv[:, c, :], in_=bank(b), func=AF.Square),
                     reads=[T_ps[b]], writes=[T_sqkv])

        def back(n):
            blk = slice(n * 512, (n + 1) * 512)
            cq, sq, ckv, sqkv = cqs[n % 2], sqs[n % 2], ckvs[n % 2], sqkvs[n % 2]
            T_cq, T_sq, T_ckv, T_sqkv = T_cqs[n % 2], T_sqs[n % 2], T_ckvs[n % 2], T_sqkvs[n % 2]
            uT_blk = [uT[:, k, blk] for k in range(8)]
            T_uTb = [T_uT[n * 4 + i] for i in range(4)]
            bstats([sq[:, c, :] for c in range(3)], [T_sq], 384, np.log(192.0 ** -0.5), lnt, T_lnt, rq, T_rq)
            bstats([sqkv[:, c, :] for c in range(2)], [T_sqkv], 256, 0.0, lnt, T_lnt, rkv, T_rkv)
            b = nb()

            def colstats(e, b=b):
                inst = None
                for i in range(4):
                    for c in range(2):
                        inst = e.matmul(bank(b)[:, i:i + 1], lhsT=sqkv[:, c, i * 128:(i + 1) * 128], rhs=ones[:, 0:1],
                                        start=(c == 0), stop=(c == 1))
                return inst
            P.op("pe", colstats, reads=[T_sqkv, T_cst], writes=[T_ps[b]])
            P.op("act", lambda e, b=b: e.activation(out=rkc[:, 4:8], in_=bank(b)[:, 0:4], func=AF.Ln, scale=1.0 / 256, bias=EPS),
                 reads=[T_ps[b]], writes=[T_rkc])
            P.op("act", lambda e: e.activation(out=rkc[:, 0:4], in_=rkc[:, 4:8], func=AF.Exp, scale=-0.5),
                 reads=[T_rkc], writes=[T_rkc])
            w = chn[0] % 2
            chn[0] += 1
            qpb, t1, t2 = qpbs[w], t1s[w], t2s[w]
            T_qpb, T_t1, T_t2 = T_qpbs[w], T_t1s[w], T_t2s[w]
            bk = nb()
            P.op("pe", mm_group(bank(bk), [(w_in_sb[:, k, 640:768], uT_blk[k]) for k in range(8)]),
                 reads=T_uTb + [T_win], writes=[T_ps[bk]])
            P.op("act", lambda e, bk=bk: e.activation(out=qpb[:], in_=bank(bk), func=AF.Copy), reads=[T_ps[bk]], writes=[T_qpb])
            b2 = nb()
            P.op("pe", mm_group(bank(b2), [(perm, qpb[:])]), reads=[T_qpb, T_cst], writes=[T_ps[b2]])
            P.op("dve", lambda e, bk=bk: e.tensor_tensor(out=t1[:], in0=bank(bk), in1=rope1[:, 0, blk], op=ALU.mult),
                 reads=[T_ps[bk], T_rope1], writes=[T_t1])
            P.op("dve", lambda e, b2=b2: e.tensor_tensor(out=t2[:], in0=bank(b2), in1=rope1[:, 1, blk], op=ALU.mult),
                 reads=[T_ps[b2], T_rope1], writes=[T_t2])
            P.op("pool", lambda e: e.tensor_tensor(out=Kpl[0:64, blk], in0=t1[0:64, :], in1=t2[0:64, :], op=ALU.add),
                 reads=[T_t1, T_t2], writes=[T_Kpe])
            P.op("pool", lambda e: e.tensor_tensor(out=Kph[64:128, blk], in0=t1[64:128, :], in1=t2[64:128, :], op=ALU.add),
                 reads=[T_t1, T_t2], writes=[T_Kpe])
            for h in range(4):
                b = nb()
                P.op("pe", mm_group(bank(b), [(wq_sb[:, c, h * 192:h * 192 + 128], cq[:, c, :]) for c in range(3)]),
                     reads=[T_cq, T_wq], writes=[T_ps[b]])
                P.op("dve", lambda e, b=b, h=h: e.tensor_tensor(out=Qn[:, h, blk], in0=bank(b), in1=rq[:], op=ALU.mult),
                     reads=[T_ps[b], T_rq], writes=[T_Q[n]])
            for pr in range(2):
                w = chn[0] % 2
                chn[0] += 1
                b = nb()

                def qpe_mm(e, b=b, pr=pr):
                    inst = None
                    for half in range(2):
                        h = 2 * pr + half
                        for c in range(3):
                            inst = e.matmul(bank(b)[half * 64:(half + 1) * 64, :],
                                            lhsT=wq_sb[:, c, h * 192 + 128:h * 192 + 192], rhs=cq[:, c, :],
                                            start=(c == 0), stop=(c == 2))
                    return inst
                P.op("pe", qpe_mm, reads=[T_cq, T_wq], writes=[T_ps[b]])
                P.op("dve", lambda e, b=b, w=w: e.tensor_tensor(out=qpfs[w][:], in0=bank(b), in1=rq[:], op=ALU.mult),
                     reads=[T_ps[b], T_rq], writes=[T_qpfs[w]])
                P.op("act", lambda e, w=w: e.activation(out=qpbs[w][:], in_=qpfs[w][:], func=AF.Copy),
                     reads=[T_qpfs[w]], writes=[T_qpbs[w]])
                b2 = nb()
                P.op("pe", mm_group(bank(b2), [(perm, qpbs[w][:])]), reads=[T_qpbs[w], T_cst], writes=[T_ps[b2]])
                P.op("pool", lambda e, w=w: e.tensor_tensor(out=t1s[w][:], in0=qpfs[w][:], in1=rope1[:, 0, blk], op=ALU.mult),
                     reads=[T_qpfs[w], T_rope1], writes=[T_t1s[w]])
                P.op("dve", lambda e, b2=b2, w=w: e.tensor_tensor(out=t2s[w][:], in0=bank(b2), in1=rope1[:, 1, blk], op=ALU.mult),
                     reads=[T_ps[b2], T_rope1], writes=[T_t2s[w]])
                P.op("pool", lambda e, pr=pr, w=w: e.tensor_tensor(out=Qpe[:, pr, blk], in0=t1s[w][:], in1=t2s[w][:], op=ALU.add),
                     reads=[T_t1s[w], T_t2s[w]], writes=[T_Qpe[n]])
            for h in range(4):
                b = nb()
                P.op("pe", mm_group(bank(b), [(wkv_sb[:, c, h * 256:h * 256 + 128], ckv[:, c, :]) for c in range(2)]),
                     reads=[T_ckv, T_wkv], writes=[T_ps[b]])
                P.op("dve", lambda e, b=b, h=h: e.tensor_tensor(out=Kn[:, h, blk], in0=bank(b), in1=rkv[:], op=ALU.mult),
                     reads=[T_ps[b], T_rkv], writes=[T_K])
            for i in range(4):
                b = nb()
                vr = [wkv_sb[:, c, :].rearrange("p (h two d) -> p h two d", h=4, two=2)[:, :, 1, :] for c in range(2)]
                P.op("pe", mm_group(bank(b), [(ckv[:, c, i * 128:(i + 1) * 128], vr[c]) for c in range(2)]),
                     reads=[T_ckv, T_wkv], writes=[T_ps[b]])
                P.op("act", lambda e, b=b, i=i: e.activation(out=Vm[:, n * 4 + i, :], in_=bank(b), func=AF.Copy,
                                                             scale=rkc[:, i:i + 1]),
                     reads=[T_ps[b], T_rkc], writes=[T_V])

        P.capture()
        front(0)
        P.replay(P.end_capture())
        for n in range(NB):
            P.capture()
            back(n)
            bl = P.end_capture()
            fl = []
            if n + 1 < NB:
                P.capture()
                front(n + 1)
                fl = P.end_capture()
            P.replay(bl, fl)
        A.free("wq", "wkv", "rope1", "grow", "xt0", "xt1", "ubf0", "ubf1", "junk", "cq0", "cq1", "sq0", "sq1",
               "ckv0", "ckv1", "sqkv0", "sqkv1", "lnt", "rq", "rkv", "rkc", "qpf0", "qpf1", "qpb0", "qpb1",
               "t10", "t20")
        P.barrier()
        A.release()
        chk(s, 1)

        def attention(score_pairs, v_of, gg_col0, o_dst, T_o_dst, after_block, extra_reads, recip_dve):
            PT = [A.alloc(f"PT{i}", [128, 1024], BF16) for i in range(4)]
            T_PT = [Tile() for _ in range(4)]
            on = A.alloc("on", [128, 4, 512], F32)
            osq = A.alloc("osq", [128, 4, 512], BF16)
            rinv = A.alloc("rinv", [128, 512], F32)
            glt = A.alloc("glt", [128, 512], F32)
            grs = A.alloc("grs", [128, 512], F32)
            T_on = [Tile() for _ in range(4)]
            T_osq = [Tile() for _ in range(4)]
            T_rinv, T_glt, T_grs = Tile(), Tile(), Tile()
            TS = [A.alloc(f"TS{i}", [128, 512], BF16) for i in range(4)]
            T_TS = [Tile() for _ in range(4)]
            steps = [(n, h, k2) for n in range(NB) for h in range(4) for k2 in range(8)]
            later = []

            def S_op(i):
                n, h, k2 = steps[i]
                sl = i % 2

                def f(e):
                    inst = None
                    for j in range(2):
                        prs = score_pairs(n, h, 2 * k2 + j)
                        for m, (l, r) in enumerate(prs):
                            inst = e.matmul(PS2[sl][:, j * 512:(j + 1) * 512], lhsT=l, rhs=r,
                                            start=(m == 0), stop=(m == len(prs) - 1))
                    return inst
                P.op("pe", f, reads=[T_Q[n], T_K] + extra_reads(n), writes=[T_ps[2 * sl], T_ps[2 * sl + 1]])

            def EXP_op(i):
                sl = i % 2
                P.op("act", lambda e: e.activation(out=PT[i % 4][:], in_=PS2[sl][:], func=AF.Exp),
                     reads=[T_ps[2 * sl], T_ps[2 * sl + 1]], writes=[T_PT[i % 4]])

            def PV_op(i):
                n, h, k2 = steps[i]
                par = h % 2
                ob = 4 + par
                P.op("dve", lambda e: e.tensor_tensor(out=TS[i % 4][:], in0=PT[i % 4][:, 0:512], in1=PT[i % 4][:, 512:1024],
                                                      op=ALU.add), reads=[T_PT[i % 4]], writes=[T_TS[i % 4]])

                def f(e):
                    inst = None
                    for j in range(2):
                        kc = 2 * k2 + j
                        inst = e.matmul(bank(ob), lhsT=v_of(h, kc), rhs=PT[i % 4][:, j * 512:(j + 1) * 512],
                                        start=(kc == 0), stop=(kc == 15))
                    return inst
                P.op("pe", f, reads=[T_PT[i % 4], T_V], writes=[T_ps[ob]])

            def SUM_op(i):
                n, h, k2 = steps[i]
                sb = 6 + h % 2
                P.op("pe", lambda e: e.matmul(bank(sb), lhsT=ones, rhs=TS[i % 4][:], start=(k2 == 0), stop=(k2 == 7)),
                     reads=[T_TS[i % 4], T_cst], writes=[T_ps[sb]])

            def finalize_head(i):
                n, h, k2 = steps[i]
                par = h % 2
                ob, sb = 4 + par, 6 + par
                def mult():
                    P.op("dve", lambda e: e.tensor_tensor(out=on[:, h, :], in0=bank(ob), in1=rinv[:], op=ALU.mult),
                         reads=[T_ps[ob], T_rinv], writes=[T_on[h]])
                if recip_dve:
                    def chunk(c):
                        P.op("dve", lambda e: e.reciprocal(out=rinv[:, c * 128:(c + 1) * 128], in_=bank(sb)[:, c * 128:(c + 1) * 128]),
                             reads=[T_ps[sb]], writes=[T_rinv])
                    chunk(0)
                    chunk(1)
                    later.append((i + 2, lambda: chunk(2)))
                    later.append((i + 2, lambda: chunk(3)))
                    later.append((i + 3, mult))
                    d0 = 3
                else:
                    P.op("act", lambda e: e.activation(out=glt[:], in_=bank(sb), func=AF.Ln), reads=[T_ps[sb]], writes=[T_glt])
                    P.op("act", lambda e: e.activation(out=rinv[:], in_=glt[:], func=AF.Exp, scale=-1.0), reads=[T_glt], writes=[T_rinv])
                    mult()
                    d0 = 1
                later.append((i + 3 + d0, lambda: P.op(
                    "act", lambda e: e.activation(out=osq[:, h, :], in_=on[:, h, :], func=AF.Square),
                    reads=[T_on[h]], writes=[T_osq[h]])))
                if h == 3:
                    def gss():
                        P.op("pe", mm_group(bank(7), [(ones, osq[:, hh, :]) for hh in range(4)]),
                             reads=T_osq + [T_cst], writes=[T_ps[7]])
                        P.op("act", lambda e: e.activation(out=glt[:], in_=bank(7), func=AF.Ln, scale=1.0 / 512, bias=EPS),
                             reads=[T_ps[7]], writes=[T_glt])
                        P.op("act", lambda e: e.activation(out=grs[:], in_=glt[:], func=AF.Exp, scale=-0.5),
                             reads=[T_glt], writes=[T_grs])
                        for hh in range(4):
                            P.op("dve", lambda e, hh=hh: e.scalar_tensor_tensor(
                                out=o_dst(n, hh), in0=on[:, hh, :], scalar=col(gg_col0 + hh), in1=grs[:],
                                op0=ALU.mult, op1=ALU.mult),
                                reads=[T_on[hh], T_cols, T_grs], writes=[T_o_dst(n)])
                    later.append((i + 4 + d0, gss))
                    if after_block is not None:
                        later.append((i + 8, lambda: after_block(n)))

            def run_due(i):
                due = [x for x in later if x[0] <= i]
                for x in due:
                    later.remove(x)
                    x[1]()

            S_op(0)
            for i in range(len(steps)):
                if i + 1 < len(steps):
                    S_op(i + 1)
                EXP_op(i)
                PV_op(i)
                if i >= 2:
                    SUM_op(i - 2)
                run_due(i)
                if steps[i][2] == 7:
                    later.append((i + 2, lambda i=i: finalize_head(i)))
            SUM_op(len(steps) - 2)
            SUM_op(len(steps) - 1)
            while later:
                later.sort(key=lambda x: x[0])
                x = later.pop(0)
                x[1]()
            A.free("PT0", "PT1", "PT2", "on", "osq", "rinv", "glt", "grs", "TS0", "TS1", "TS2", "TS3", "PT3")

        o_mla = A.alloc("o_mla", [128, 4, S], BF16)
        T_omla = [Tile() for _ in range(NB)]
        DBG.update(o_mla=o_mla)

        def mla_pairs(n, h, kc):
            blk = slice(n * 512, (n + 1) * 512)
            ks = slice(kc * 128, (kc + 1) * 128)
            kp = Kpl if h % 2 == 0 else Kph
            return [(Kn[:, h, ks], Qn[:, h, blk]), (kp[:, ks], Qpe[:, h // 2, blk])]

        attention(mla_pairs, lambda h, kc: Vm[:, kc, h * 128:(h + 1) * 128], C_GGM,
                  lambda n, hh: o_mla[:, hh, n * 512:(n + 1) * 512], lambda n: T_omla[n], None,
                  lambda n: [T_Qpe[n], T_Kpe], False)
        A.free("Qn", "Qpe", "Kn", "Kpl", "Kph", "Vm")
        P.barrier()
        A.release()
        chk(s, 2)

        Qg = A.alloc("Qg", [128, 4, S], BF16)
        Kg = A.alloc("Kg", [128, 2, S], BF16)
        Vg = A.alloc("Vg", [128, 16, 256], BF16)
        DBG.update(Qg=Qg, Kg=Kg, Vg=Vg)
        T_Q = [Tile() for _ in range(NB)]
        T_K, T_V = Tile(), Tile()
        w_out_sb = A.alloc("w_out", [128, 8, D], BF16)
        T_wout = Tile()
        P.op("pool", lambda e: e.dma_start(out=w_out_sb[:], in_=w_out.rearrange("(c p) n -> p c n", p=128)),
             writes=[T_wout], dma=True)
        qg = [A.alloc(f"qg{i}", [128, 512], BF16) for i in range(2)]
        sqg = [A.alloc(f"sqg{i}", [128, 512], BF16) for i in range(2)]
        lng = [A.alloc(f"lng{i}", [128, 512], F32) for i in range(2)]
        rg = [A.alloc(f"rg{i}", [128, 512], F32) for i in range(2)]
        g1 = [A.alloc(f"g1{i}", [128, 512], F32) for i in range(2)]
        g2 = [A.alloc(f"g2{i}", [128, 512], F32) for i in range(2)]
        T_qg, T_sqg, T_lng, T_rg, T_g1, T_g2 = ([Tile(), Tile()] for _ in range(6))
        rstep = ropeg[:].ap[0][0]
        hcnt = 0
        items = []
        for n in range(NB):
            blk = slice(n * 512, (n + 1) * 512)
            uT_blk = [uT[:, k, blk] for k in range(8)]
            T_uTb = [T_uT[n * 4 + i] for i in range(4)]
            cc_top = bass.AP(ropeg, 0 * rstep + n * 8, [[rstep, 64], [1, 8], [0, 64]])
            ss_top = bass.AP(ropeg, 0 * rstep + 64 + n * 8, [[rstep, 64], [1, 8], [0, 64]])
            cc_bot = bass.AP(ropeg, 64 * rstep, [[rstep, 64], [0, 8], [1, 64]])
            ss_bot = bass.AP(ropeg, 64 * rstep + 64, [[rstep, 64], [0, 8], [1, 64]])

            def v3(ap):
                return ap.rearrange("p (a b) -> p a b", a=8)
            for hd in range(6):
                isq = hd < 4
                c0 = 768 + hd * 128 if isq else 1280 + (hd - 4) * 128
                gcol = C_GQQ if isq else C_GQK
                dst = Qg[:, hd, blk] if isq else Kg[:, hd - 4, blk]
                Tdst = T_Q[n] if isq else T_K
                lnb = np.log(128.0 ** -0.5) if isq else 0.0
                u = hcnt % 2
                hcnt += 1
                P.capture()
                b = nb()
                P.op("pe", mm_group(bank(b), [(w_in_sb[:, k, c0:c0 + 128], uT_blk[k]) for k in range(8)]),
                     reads=T_uTb + [T_win], writes=[T_ps[b]])
                P.op("act", lambda e, b=b, u=u, gcol=gcol: e.activation(out=qg[u][:], in_=bank(b), func=AF.Copy, scale=col(gcol)),
                     reads=[T_ps[b], T_cols], writes=[T_qg[u]])
                P.op("act", lambda e, b=b, u=u: e.activation(out=sqg[u][:], in_=bank(b), func=AF.Square),
                     reads=[T_ps[b]], writes=[T_sqg[u]])
                P.op("dve", lambda e, b=b, u=u, gcol=gcol, cc_top=cc_top: e.scalar_tensor_tensor(
                    out=v3(g1[u][0:64, :]), in0=v3(bank(b)[0:64, :]), scalar=cols[0:64, gcol:gcol + 1], in1=cc_top,
                    op0=ALU.mult, op1=ALU.mult), reads=[T_ps[b], T_cols, T_ropeg], writes=[T_g1[u]])
                P.op("dve", lambda e, b=b, u=u, gcol=gcol, cc_bot=cc_bot: e.scalar_tensor_tensor(
                    out=v3(g1[u][64:128, :]), in0=v3(bank(b)[64:128, :]), scalar=cols[64:128, gcol:gcol + 1], in1=cc_bot,
                    op0=ALU.mult, op1=ALU.mult), reads=[T_ps[b], T_cols, T_ropeg], writes=[T_g1[u]])
                bs = nb()
                P.op("pe", mm_group(bank(bs), [(ones, sqg[u][:])]), reads=[T_sqg[u], T_cst], writes=[T_ps[bs]])
                bw = nb()
                P.op("pe", mm_group(bank(bw), [(perm, qg[u][:])]), reads=[T_qg[u], T_cst], writes=[T_ps[bw]])
                P.op("act", lambda e, bs=bs, u=u: e.activation(out=lng[u][:], in_=bank(bs), func=AF.Ln, scale=1.0 / 128, bias=EPS),
                     reads=[T_ps[bs]], writes=[T_lng[u]])
                P.op("act", lambda e, u=u, lnb=lnb: e.activation(out=rg[u][:], in_=lng[u][:], func=AF.Exp, scale=-0.5, bias=float(lnb)),
                     reads=[T_lng[u]], writes=[T_rg[u]])
                P.op("dve", lambda e, bw=bw, u=u, ss_top=ss_top: e.tensor_tensor(
                    out=v3(g2[u][0:64, :]), in0=v3(bank(bw)[0:64, :]), in1=ss_top, op=ALU.mult),
                    reads=[T_ps[bw], T_ropeg], writes=[T_g2[u]])
                P.op("dve", lambda e, bw=bw, u=u, ss_bot=ss_bot: e.tensor_tensor(
                    out=v3(g2[u][64:128, :]), in0=v3(bank(bw)[64:128, :]), in1=ss_bot, op=ALU.mult),
                    reads=[T_ps[bw], T_ropeg], writes=[T_g2[u]])
                P.op("pool", lambda e, u=u: e.tensor_tensor(out=g1[u][:], in0=g1[u][:], in1=g2[u][:], op=ALU.add),
                     reads=[T_g2[u]], writes=[T_g1[u]])
                P.op("pool", lambda e, u=u, dst=dst: e.tensor_tensor(out=dst, in0=g1[u][:], in1=rg[u][:], op=ALU.mult),
                     reads=[T_g1[u], T_rg[u]], writes=[Tdst])
                L = P.end_capture()
                items.append((L[:5], L[5:]))
            for i in range(4):
                P.capture()
                b = nb()
                t0 = (n * 4 + i) * 128
                P.op("pe", mm_group(bank(b)[:, 0:256], [(uT[:, k, t0:t0 + 128], w_in_sb[:, k, 1536:1792]) for k in range(8)]),
                     reads=[T_uT[n * 4 + i], T_win], writes=[T_ps[b]])
                P.op("act", lambda e, b=b, i=i, n=n: e.activation(out=Vg[:, n * 4 + i, :], in_=bank(b)[:, 0:256], func=AF.Copy),
                     reads=[T_ps[b]], writes=[T_V])
                items[-1 - i % 2] = (items[-1 - i % 2][0], items[-1 - i % 2][1] + P.end_capture())
        for i in range(len(items)):
            P.replay(items[i][0])
            P.replay(items[i][1])
        A.free("w_in", "uT", "qg0", "qg1", "sqg0", "sqg1", "lng0", "lng1", "rg0", "rg1", "g10", "g11", "g20", "g21")
        P.barrier()
        A.release()
        chk(s, 3)

        hres = A.alloc("h", [128, 16, D], F32)
        T_h = [Tile() for _ in range(16)]
        DBG.update(h=hres)
        og = A.alloc("og", [128, 4, 512], BF16)
        T_og = Tile()
        xr = [A.alloc(f"xr{i}", [128, D], F32) for i in range(4)]
        T_xr = [Tile() for _ in range(4)]
        junk2 = A.alloc("junk2", [128, D], BF16)
        T_junk2 = Tile()
        T_ss2 = [Tile() for _ in range(16)]
        T_r2 = Tile()

        def wout_block(n):
            for i in range(4):
                ti = n * 4 + i
                t0 = ti * 128
                P.op("sp", lambda e, i=i, t0=t0: e.dma_start(out=xr[i][:], in_=x[s, t0:t0 + 128, :]),
                     writes=[T_xr[i]], dma=True)
            for i in range(4):
                ti = n * 4 + i
                t0 = ti * 128
                for fh in range(2):
                    b = 5 if fh == 0 else 7
                    pairs = [(o_mla[:, c, t0:t0 + 128], w_out_sb[:, c, fh * 512:(fh + 1) * 512]) for c in range(4)]
                    pairs += [(og[:, c, i * 128:(i + 1) * 128], w_out_sb[:, 4 + c, fh * 512:(fh + 1) * 512]) for c in range(4)]
                    P.op("pe", mm_group(bank(b), pairs), reads=[T_omla[n], T_og, T_wout], writes=[T_ps[b]])
                    P.op("dve", lambda e, b=b, i=i, ti=ti, fh=fh: e.tensor_tensor(
                        out=hres[:, ti, fh * 512:(fh + 1) * 512], in0=bank(b), in1=xr[i][:, fh * 512:(fh + 1) * 512], op=ALU.add),
                        reads=[T_ps[b], T_xr[i]], writes=[T_h[ti]])
                P.op("act", lambda e, ti=ti: e.activation(out=junk2[:], in_=hres[:, ti, :], func=AF.Square,
                                                          accum_out=st2[:, ti:ti + 1]),
                     reads=[T_h[ti]], writes=[T_junk2, T_ss2[ti]])

        def gqa_pairs(n, h, kc):
            return [(Kg[:, h // 2, kc * 128:(kc + 1) * 128], Qg[:, h, n * 512:(n + 1) * 512])]

        attention(gqa_pairs, lambda h, kc: Vg[:, kc, (h // 2) * 128:(h // 2 + 1) * 128], C_GGG,
                  lambda n, hh: og[:, hh, :], lambda n: T_og, wout_block, lambda n: [], True)
        A.free("Qg", "Kg", "Vg", "w_out", "o_mla", "og", "xr0", "xr1", "xr2", "xr3", "junk2")
        P.barrier()
        A.release()
        chk(s, 4)

        u2T = A.alloc("u2T", [128, 8, S], BF16)
        T_u2T = [Tile() for _ in range(16)]
        DBG.update(u2T=u2T)
        DBGP3 = True
        grow2 = A.alloc("grow2", [128, D], F32)
        T_grow2 = Tile()
        P.op("sp", lambda e: e.dma_start(out=grow2[:], in_=grows_d[1]), writes=[T_grow2], dma=True)
        ubf = [A.alloc(f"ubf{i}", [128, D], BF16) for i in range(4)]
        T_ubf = [Tile() for _ in range(4)]
        junk = A.alloc("junk", [128, D], BF16)
        T_junk = Tile()
        NWU = 3
        wg = [A.alloc(f"wg{i}", [128, 8, 128], BF16) for i in range(NWU)]
        wv = [A.alloc(f"wv{i}", [128, 8, 128], BF16) for i in range(NWU)]
        T_wu = [Tile() for _ in range(NWU)]
        NWD = 8
        wd = [A.alloc(f"wd{i}", [128, D], BF16) for i in range(NWD)]
        T_wd = [Tile() for _ in range(NWD)]
        actT = [A.alloc(f"actT{i}", [128, S], BF16) for i in range(6)]
        T_actT = [Tile() for _ in range(6)]
        hug = [A.alloc(f"hug{i}", [128, S + 2], BF16) for i in range(2)]
        huv = [A.alloc(f"huv{i}", [128, S + 2], BF16) for i in range(2)]
        T_hu = [[Tile() for _ in range(NB)] for _ in range(2)]
        T_hv = [[Tile() for _ in range(NB)] for _ in range(2)]
        dg = [A.alloc(f"dg{i}", [128, 6, 128], BF16) for i in range(2)]
        T_dg = [Tile(), Tile()]
        sg = [A.alloc(f"sg{i}", [128, 512], BF16) for i in range(2)]
        T_sg = [Tile(), Tile()]
        DBG.update(actT0=actT[0], actT4=actT[4], hug0=hug[0], hug1=hug[1], huv0=huv[0], huv1=huv[1])
        for u in range(2):
            for buf, TT in ((hug[u], T_hu[u]), (huv[u], T_hv[u])):
                P.op("pool", lambda e, buf=buf: e.memset(buf[:, 0:1], 0.0), writes=[TT[0]])
                P.op("pool", lambda e, buf=buf: e.memset(buf[:, S + 1:S + 2], 0.0), writes=[TT[NB - 1]])
        w_up_v = w_up.rearrange("(c p) n -> p c n", p=128)
        jlist = []
        for gi, gsz in enumerate(GROUPS):
            j0 = sum(GROUPS[:gi])
            jlist.append(list(range(j0, j0 + gsz)))

        def load_w(j):
            su, sd = j % NWU, j % NWD
            P.op("pool", lambda e: e.dma_start(out=wg[su][:], in_=w_up_v[:, :, j * 128:(j + 1) * 128]), writes=[T_wu[su]], dma=True)
            P.op("pool", lambda e: e.dma_start(out=wv[su][:], in_=w_up_v[:, :, DFF + j * 128:DFF + (j + 1) * 128]),
                 writes=[T_wu[su]], dma=True)
            P.op("pool", lambda e: e.dma_start(out=wd[sd][:], in_=w_down[j * 128:(j + 1) * 128, :]), writes=[T_wd[sd]], dma=True)

        def dg_build(j):
            u = j % 2
            for k in range(3):
                P.op("pool", lambda e, k=k: e.tensor_scalar(out=dg[u][:, k, :], in0=ident, scalar1=col(C_CW + k * 44 + j),
                                                           scalar2=None, op0=ALU.mult),
                     reads=[T_cst, T_cols], writes=[T_dg[u]])
                P.op("pool", lambda e, k=k: e.tensor_scalar(out=dg[u][:, 3 + k, :], in0=ident, scalar1=col(C_CW + k * 44 + 22 + j),
                                                           scalar2=None, op0=ALU.mult),
                     reads=[T_cst, T_cols], writes=[T_dg[u]])

        def hu_block(j, n):
            su, u = j % NWU, j % 2
            blk = slice(n * 512, (n + 1) * 512)
            rd = [T_u2T[n * 4 + i] for i in range(4)] + [T_wu[su]]
            b = n % 2
            P.op("pe", mm_group(bank(b), [(wg[su][:, k, :], u2T[:, k, blk]) for k in range(8)]), reads=rd, writes=[T_ps[b]])
            P.op("act", lambda e, b=b: e.activation(out=hug[u][:, 1 + n * 512:1 + (n + 1) * 512], in_=bank(b), func=AF.Copy),
                 reads=[T_ps[b]], writes=[T_hu[u][n]])
            b = 2 + n % 2
            P.op("pe", mm_group(bank(b), [(wv[su][:, k, :], u2T[:, k, blk]) for k in range(8)]), reads=rd, writes=[T_ps[b]])
            P.op("dve", lambda e, b=b: e.tensor_copy(out=huv[u][:, 1 + n * 512:1 + (n + 1) * 512], in_=bank(b)),
                 reads=[T_ps[b]], writes=[T_hv[u][n]])

        def hu_ops(j):
            dg_build(j)
            for n in range(NB):
                hu_block(j, n)

        def conv_ops(j, jj):
            u = j % 2
            for n in range(NB):
                nbrs = [T_hu[u][m] for m in (n - 1, n, n + 1) if 0 <= m < NB]
                nbrv = [T_hv[u][m] for m in (n - 1, n, n + 1) if 0 <= m < NB]
                bg, bv = 4 + n % 2, 6 + n % 2
                P.op("pe", mm_group(bank(bg), [(dg[u][:, k, :], hug[u][:, n * 512 + k:n * 512 + k + 512]) for k in range(3)]),
                     reads=nbrs + [T_dg[u]], writes=[T_ps[bg]])
                P.op("pe", mm_group(bank(bv), [(dg[u][:, 3 + k, :], huv[u][:, n * 512 + k:n * 512 + k + 512]) for k in range(3)]),
                     reads=nbrv + [T_dg[u]], writes=[T_ps[bv]])
                v = n % 2
                P.op("act", lambda e, bg=bg, v=v: e.activation(out=sg[v][:], in_=bank(bg), func=AF.Silu, bias=col(C_CB + j)),
                     reads=[T_ps[bg], T_cols], writes=[T_sg[v]])
                P.op("dve", lambda e, bv=bv, v=v, n=n: e.scalar_tensor_tensor(
                    out=actT[jj][:, n * 512:(n + 1) * 512], in0=bank(bv), scalar=col(C_CB + 22 + j), in1=sg[v][:],
                    op0=ALU.add, op1=ALU.mult),
                    reads=[T_ps[bv], T_cols, T_sg[v]], writes=[T_actT[jj]])

        yt = None
        allj = [j for g in jlist for j in g]
        load_w(0)
        load_w(1)
        dg_build(0)
        dg_build(1)
        P.op("act", lambda e: e.activation(out=st2[:, 16:32], in_=st2[:, 0:16], func=AF.Ln, scale=1.0 / D, bias=EPS),
             reads=T_ss2, writes=[T_r2])
        P.op("act", lambda e: e.activation(out=st2[:, 32:48], in_=st2[:, 16:32], func=AF.Exp, scale=-0.5),
             reads=[T_r2], writes=[T_r2])
        for n in range(NB):
            for i in range(4):
                ti = n * 4 + i
                make_uT([hres[:, ti, :]], [T_h[ti]], grow2, T_grow2, u2T, T_u2T, [ti], junk, T_junk, ubf, T_ubf,
                        "act", rstd=[(st2[:, 32 + ti:33 + ti], T_r2)])
            hu_block(0, n)
            hu_block(1, n)
        def down_ops(gi):
            js = jlist[gi]
            lastg = gi == len(jlist) - 1
            if lastg:
                growf = A.alloc("growf", [128, D], F32)
                T_growf = Tile()
                P.op("sp", lambda e: e.dma_start(out=growf[:], in_=grows_d[2]), writes=[T_growf], dma=True)
                yt = [A.alloc(f"yt{i}", [128, D], F32) for i in range(2)]
                T_yt = [Tile(), Tile()]
            fin = []
            for ti in range(16):
                t0 = ti * 128
                for fh in range(2):
                    b = (2 * ti + fh) % 4
                    P.op("pe", mm_group(bank(b), [(actT[jj][:, t0:t0 + 128], wd[j % NWD][:, fh * 512:(fh + 1) * 512])
                                                  for jj, j in enumerate(js)]),
                         reads=[T_actT[jj] for jj in range(len(js))] + [T_wd[j % NWD] for j in js], writes=[T_ps[b]])
                    P.op("dve", lambda e, b=b, ti=ti, fh=fh: e.tensor_tensor(
                        out=hres[:, ti, fh * 512:(fh + 1) * 512], in0=bank(b), in1=hres[:, ti, fh * 512:(fh + 1) * 512], op=ALU.add),
                        reads=[T_ps[b], T_h[ti]], writes=[T_h[ti]])
                if lastg:
                    k = stcnt[0] % 4
                    stcnt[0] += 1
                    c0 = 4 * k
                    sl = ti % 2
                    P.op("act", lambda e, ti=ti, c0=c0: e.activation(out=junk[:], in_=hres[:, ti, :], func=AF.Square,
                                                                    accum_out=st[:, c0:c0 + 1]),
                         reads=[T_h[ti]], writes=[T_junk, T_st[k]])
                    P.op("act", lambda e, c0=c0: e.activation(out=st[:, c0 + 1:c0 + 2], in_=st[:, c0:c0 + 1], func=AF.Ln,
                                                              scale=1.0 / D, bias=EPS), reads=[T_st[k]], writes=[T_st[k]])
                    P.op("act", lambda e, c0=c0: e.activation(out=st[:, c0 + 2:c0 + 3], in_=st[:, c0 + 1:c0 + 2], func=AF.Exp,
                                                              scale=-0.5), reads=[T_st[k]], writes=[T_st[k]])
                    def finish(ti=ti, c0=c0, sl=sl, k=k, t0=t0):
                        P.op("dve", lambda e: e.scalar_tensor_tensor(
                            out=yt[sl][:], in0=hres[:, ti, :], scalar=st[:, c0 + 2:c0 + 3], in1=growf[:], op0=ALU.mult, op1=ALU.mult),
                            reads=[T_h[ti], T_st[k], T_growf], writes=[T_yt[sl]])
                        P.op("sp", lambda e: e.dma_start(out=y[s, t0:t0 + 128, :], in_=yt[sl][:]),
                             reads=[T_yt[sl]], dma=True)
                    fin.append(finish)
                    if len(fin) >= 2:
                        fin.pop(0)()
            for f_ in fin:
                f_()

        flat = [(gi, jj, j) for gi, js in enumerate(jlist) for jj, j in enumerate(js)]
        for idx, (gi, jj, j) in enumerate(flat):
            if j >= 2:
                hu_ops(j)
            if idx > 0:
                pgi, pjj, pj = flat[idx - 1]
                conv_ops(pj, pjj)
                if pjj == len(jlist[pgi]) - 1:
                    down_ops(pgi)
            if j + 2 < NJ:
                load_w(j + 2)
        pgi, pjj, pj = flat[-1]
        conv_ops(pj, pjj)
        down_ops(pgi)
        names = ["u2T", "grow2", "growf", "ubf0", "ubf1", "ubf2", "ubf3", "junk", "h", "yt0", "yt1"]
        names += [f"wg{i}" for i in range(NWU)] + [f"wv{i}" for i in range(NWU)] + [f"wd{i}" for i in range(NWD)]
        names += [f"actT{i}" for i in range(6)] + ["hug0", "hug1", "huv0", "huv1", "dg0", "dg1", "sg0", "sg1"]
        A.free(*names)
        P.barrier()
        A.release()
        chk(s, 5)

    try:
        for s_ in range(2):
            seq(s_)
    except _Stop:
        P.barrier()
        for name in (dbg or []):
            t = DBG[name]
            shp = list(t[:].shape)
            dd = nc.dram_tensor("dbg_" + name, shp, t[:].dtype, kind="ExternalOutput").ap()
            P.op("sp", lambda e, t=t, dd=dd: e.dma_start(out=dd, in_=t[:]), dma=True)
        P.barrier()

    semnames = list(Prog.ENGS) + [(e, i) for e in Prog.ENGS for i in range(Prog.NRING)]
    with contextlib.ExitStack() as es:
        sems = {k: es.enter_context(nc.semaphore("s_" + str(k).replace(" ", "").replace("'", "").replace("(", "")
                                                 .replace(")", "").replace(",", "_"))) for k in semnames}
        P.finalize(sems)
        block = es.enter_context(nc.Block())
        block.tensor(lambda e: P.emit_engine("pe", e))
        block.scalar(lambda e: P.emit_engine("act", e))
        block.vector(lambda e: P.emit_engine("dve", e))
        block.gpsimd(lambda e: P.emit_engine("pool", e))
        block.sync(lambda e: P.emit_engine("sp", e))
    return nc


def _rope_tables():
    inv = (10000.0 ** (-np.arange(0, 64, 2, dtype=np.float32) / 64.0)).astype(np.float32)
    p = np.arange(128)
    f = p % 32
    sign = np.where((p % 64) < 32, -1.0, 1.0).astype(np.float32)
    t = np.arange(S, dtype=np.float32)
    ang1 = (t[None, :] * inv[f][:, None]).astype(np.float32)
    rope1 = np.stack([np.cos(ang1), sign[:, None] * np.sin(ang1)], 0).astype(np.float32)
    pos = np.arange(64, dtype=np.float32)
    angg = (pos[None, :] * inv[f][:, None]).astype(np.float32)
    ropeg = np.concatenate([np.cos(angg), sign[:, None] * np.sin(angg)], 1).astype(np.float32)
    return rope1, ropeg


def _consts():
    ident = np.eye(128, dtype=np.float32)
    ones = np.ones((128, 128), np.float32)
    perm = np.zeros((128, 128), np.float32)
    for m in range(128):
        k = m + 32 if (m % 64) < 32 else m - 32
        perm[k, m] = 1.0
    return np.concatenate([ident, ones, perm], 1).astype(ml_dtypes.bfloat16)


_NC_CACHE = {}


def kernel(x, norm1_g, w_in, mla_q_norm_g, mla_w_q_up, mla_kv_norm_g, mla_w_kv_up,
           gqa_q_norm_g, gqa_k_norm_g, group_norm_mla_g, group_norm_gqa_g, w_out,
           norm2_g, ffn_w_up, ffn_conv_w, ffn_conv_b, ffn_w_down, final_norm_g):
    f32 = np.float32
    x = np.asarray(x, f32)

    def colify(v, n):
        return np.asarray(v, f32).reshape(n, 128).T

    cols = np.concatenate([
        colify(mla_q_norm_g[0], 3), colify(mla_kv_norm_g[0], 2),
        colify(gqa_q_norm_g[0], 1), colify(gqa_k_norm_g[0], 1),
        colify(group_norm_mla_g[0], 4), colify(group_norm_gqa_g[0], 4),
        colify(ffn_conv_w[0, 0], 44), colify(ffn_conv_w[0, 1], 44), colify(ffn_conv_w[0, 2], 44),
        colify(ffn_conv_b[0], 44)], 1)
    cols = np.ascontiguousarray(cols, f32)
    assert cols.shape == (128, NCOLS)
    grows = np.ascontiguousarray(np.stack([
        np.broadcast_to(np.asarray(norm1_g[0], f32), (128, D)),
        np.broadcast_to(np.asarray(norm2_g[0], f32), (128, D)),
        np.broadcast_to(np.asarray(final_norm_g, f32), (128, D))], 0))
    rope1, ropeg = _rope_tables()
    cst = _consts()
    if "nc" not in _NC_CACHE:
        _NC_CACHE["nc"] = build_nc()
    nc = _NC_CACHE["nc"]
    shared = {
        "w_in": np.ascontiguousarray(w_in[0], f32), "wq": np.ascontiguousarray(mla_w_q_up[0], f32),
        "wkv": np.ascontiguousarray(mla_w_kv_up[0], f32), "w_out": np.ascontiguousarray(w_out[0], f32),
        "w_up": np.ascontiguousarray(ffn_w_up[0], f32), "w_down": np.ascontiguousarray(ffn_w_down[0], f32),
        "cst": cst, "cols": cols, "grows": grows, "rope1": rope1, "ropeg": ropeg,
    }
    in_maps = []
    for c in range(8):
        m = dict(shared)
        m["x"] = np.ascontiguousarray(x[2 * c:2 * c + 2])
        in_maps.append(m)
    res = run_bass_kernel_spmd(nc, in_maps, core_ids=list(range(8)))
    out = np.concatenate([np.asarray(r["y"], f32) for r in res.results], 0)
    return out
```

```python
import contextlib
import numpy as np
import ml_dtypes
import concourse.bass as bass
import concourse.mybir as mybir
from concourse.bass_utils import run_bass_kernel_spmd

F32 = mybir.dt.float32
BF16 = mybir.dt.bfloat16
AF = mybir.ActivationFunctionType
ALU = mybir.AluOpType

SB_LO = 16512
SB_HI = 229344
EPS = 1e-6
S = 2048
D = 1024
NB = 4
DFF = 2816
NJ = 22
GROUPS = [6, 6, 6, 4]

C_GQ, C_GKV, C_GQQ, C_GQK, C_GGM, C_GGG, C_CW, C_CB, NCOLS = 0, 3, 5, 6, 7, 11, 15, 147, 191


class Tile:
    __slots__ = ("w", "rs", "excl")

    def __init__(self, excl=False):
        self.w = None
        self.rs = {}
        self.excl = excl


class Ev:
    __slots__ = ("eng", "emit", "deps", "val", "sem", "dma", "needed", "pre")

    def __init__(self, eng, emit, dma):
        self.eng = eng
        self.emit = emit
        self.dma = dma
        self.deps = ()
        self.val = None
        self.sem = None
        self.needed = False
        self.pre = None


class Arena:
    def __init__(self, nc):
        self.nc = nc
        self.live = {}
        self.pending = []
        self.n = 0

    def alloc(self, name, shape, dtype):
        esz = 2 if dtype == BF16 else 4
        free = 1
        for d in shape[1:]:
            free *= d
        nbytes = ((free * esz + 63) // 64) * 64
        pos = SB_LO
        for lo, hi in sorted(self.live.values()):
            if pos + nbytes <= lo:
                break
            pos = max(pos, hi)
        if pos + nbytes > SB_HI:
            raise RuntimeError(f"SBUF arena full allocating {name} ({nbytes}B), used={self.used()}")
        assert name not in self.live, name
        self.live[name] = (pos, pos + nbytes)
        self.n += 1
        return self.nc.alloc_sbuf_tensor_at(f"{name}_{self.n}", list(shape), dtype, offset=pos)

    def free(self, *names):
        self.pending.extend(names)

    def release(self):
        for n in self.pending:
            del self.live[n]
        self.pending = []

    def used(self):
        return sum(hi - lo for lo, hi in self.live.values())


class Prog:
    ENGS = ("pe", "act", "dve", "pool", "sp")
    NRING = 8

    def __init__(self):
        self.q = {e: [] for e in self.ENGS}
        self.ndma = {e: 0 for e in self.ENGS}
        self.last_compute = {e: None for e in self.ENGS}
        self.recent_dma = {e: [] for e in self.ENGS}
        self.cap = None

    def capture(self):
        self.cap = []

    def end_capture(self):
        l, self.cap = self.cap, None
        return l

    def replay(self, *lists):
        idx = [0] * len(lists)
        total = sum(len(l) for l in lists)
        for _ in range(total):
            best, bk = None, None
            for k, l in enumerate(lists):
                if idx[k] < len(l):
                    frac = idx[k] / len(l)
                    if best is None or frac < best:
                        best, bk = frac, k
            a = lists[bk][idx[bk]]
            idx[bk] += 1
            self.op(*a)

    def op(self, eng, emit, reads=(), writes=(), dma=False):
        if self.cap is not None:
            self.cap.append((eng, emit, list(reads), list(writes), dma))
            return None
        ev = Ev(eng, emit, dma)
        deps = set()
        xr = [t for t in reads if t.excl]
        if xr:
            reads = [t for t in reads if not t.excl]
            writes = list(writes) + xr
        for t in reads:
            if t.w is not None:
                deps.add(t.w)
        for t in writes:
            if t.w is not None:
                deps.add(t.w)
            deps.update(t.rs.values())
        if dma:
            i = self.ndma[eng]
            self.ndma[eng] = i + 1
            ev.sem = (eng, i % self.NRING)
            ev.val = 16 * (i // self.NRING + 1)
            rd = self.recent_dma[eng]
            if len(rd) >= self.NRING:
                ev.pre = rd[-self.NRING]
            rd.append(ev)
            if len(rd) > 2 * self.NRING:
                del rd[0]
        else:
            self.last_compute[eng] = ev
        ev.deps = [d for d in deps
                   if not (d.eng == "pe" and eng == "pe" and not d.dma and not dma)]
        for t in reads:
            t.rs[("dma", id(ev)) if dma else eng] = ev
        for t in writes:
            t.w = ev
            t.rs = {}
        self.q[eng].append(ev)
        return ev

    def barrier(self):
        deps = [e for e in self.last_compute.values() if e is not None]
        for eng in self.ENGS:
            deps.extend(self.recent_dma[eng][-self.NRING:])
        for eng in self.ENGS:
            ev = Ev(eng, None, False)
            ev.deps = list(deps)
            self.q[eng].append(ev)

    def finalize(self, sems):
        for eng in self.ENGS:
            for ev in self.q[eng]:
                for d in ev.deps:
                    d.needed = True
        for eng in self.ENGS:
            cnt = 0
            for ev in self.q[eng]:
                if ev.dma or ev.emit is None:
                    continue
                if ev.needed:
                    cnt += 1
                    ev.val = cnt
                    ev.sem = eng
        self.sems = sems

    def emit_engine(self, eng, handle):
        seen = {}
        sems = self.sems
        for ev in self.q[eng]:
            need = {}
            for d in ev.deps:
                if need.get(d.sem, 0) < d.val:
                    need[d.sem] = d.val
            if ev.pre is not None:
                d = ev.pre
                if need.get(d.sem, 0) < d.val:
                    need[d.sem] = d.val
            for s, v in need.items():
                if seen.get(s, 0) < v:
                    handle.wait_ge(sems[s], v)
                    seen[s] = v
            if ev.emit is None:
                continue
            inst = ev.emit(handle)
            if ev.dma:
                inst.then_inc(sems[ev.sem], 16)
            elif ev.needed:
                inst.then_inc(sems[ev.sem], 1)


class _Stop(Exception):
    pass


def build_nc(stop=None, dbg=None):
    nc = bass.Bass("TRN2", target_bir_lowering=False)
    DBG = {}

    def din(name, shape, dt=F32):
        return nc.dram_tensor(name, list(shape), dt, kind="ExternalInput").ap()

    x = din("x", [2, S, D])
    w_in = din("w_in", [D, 1728])
    wq = din("wq", [384, 768])
    wkv = din("wkv", [256, 1024])
    w_out = din("w_out", [D, D])
    w_up = din("w_up", [D, 2 * DFF])
    w_down = din("w_down", [DFF, D])
    cst_d = din("cst", [128, 384], BF16)
    cols_d = din("cols", [128, NCOLS])
    grows_d = din("grows", [3, 128, D])
    rope1_d = din("rope1", [2, 128, S])
    ropeg_d = din("ropeg", [128, 128])
    y = nc.dram_tensor("y", [2, S, D], F32, kind="ExternalOutput").ap()

    A = Arena(nc)
    P = Prog()
    PS2 = [nc.alloc_psum_tensor(f"ps{i}", [128, 1024], F32) for i in range(4)]
    T_ps = [Tile(excl=True) for _ in range(8)]

    def bank(b):
        return PS2[b // 2][:, (b % 2) * 512:(b % 2 + 1) * 512]

    rr = [0]

    def nb():
        b = rr[0] % 8
        rr[0] += 1
        return b

    cst = A.alloc("cst", [128, 384], BF16)
    cols = A.alloc("cols", [128, NCOLS], F32)
    ropeg = A.alloc("ropeg", [128, 128], F32)
    st = A.alloc("st", [128, 16], F32)
    T_cst, T_cols, T_ropeg = Tile(), Tile(), Tile()
    T_st = [Tile() for _ in range(4)]
    st2 = A.alloc("st2", [128, 48], F32)
    ident, ones, perm = cst[:, 0:128], cst[:, 128:256], cst[:, 256:384]
    P.op("sp", lambda e: e.dma_start(out=cst[:], in_=cst_d), writes=[T_cst], dma=True)
    P.op("sp", lambda e: e.dma_start(out=cols[:], in_=cols_d), writes=[T_cols], dma=True)
    P.op("sp", lambda e: e.dma_start(out=ropeg[:], in_=ropeg_d), writes=[T_ropeg], dma=True)

    def col(c):
        return cols[:, c:c + 1]

    stcnt = [0]

    def mm_group(out_ap, pairs, first=True, last=True):
        def f(e):
            n = len(pairs)
            inst = None
            for i, (l, r) in enumerate(pairs):
                inst = e.matmul(out_ap, lhsT=l, rhs=r, start=(first and i == 0), stop=(last and i == n - 1))
            return inst
        return f

    def rstd_cols(eng_src_ap, width, n_feat, Tsrc, Tdst_list, junk, T_junk):
        pass

    def make_uT(src_aps, src_tiles, grow, T_grow, uT, T_uT, tis, junk, T_junk, ubf, T_ubf, evac_eng, rstd=None):
        for idx, ti in enumerate(tis):
            src, Ts = src_aps[idx], src_tiles[idx]
            k = stcnt[0] % 4
            stcnt[0] += 1
            sl = (stcnt[0] - 1) % len(ubf)
            c0 = 4 * k
            if rstd is None:
                P.op("act", lambda e, src=src, c0=c0: e.activation(out=junk[:], in_=src, func=AF.Square,
                                                                     accum_out=st[:, c0:c0 + 1]),
                     reads=[Ts], writes=[T_junk, T_st[k]])
                P.op("act", lambda e, c0=c0: e.activation(out=st[:, c0 + 1:c0 + 2], in_=st[:, c0:c0 + 1], func=AF.Ln,
                                                          scale=1.0 / D, bias=EPS),
                     reads=[T_st[k]], writes=[T_st[k]])
                P.op("act", lambda e, c0=c0: e.activation(out=st[:, c0 + 2:c0 + 3], in_=st[:, c0 + 1:c0 + 2], func=AF.Exp,
                                                          scale=-0.5),
                     reads=[T_st[k]], writes=[T_st[k]])
                rs_ap, T_rs = st[:, c0 + 2:c0 + 3], T_st[k]
            else:
                rs_ap, T_rs = rstd[idx]
            P.op("dve", lambda e, src=src, rs_ap=rs_ap, sl=sl: e.scalar_tensor_tensor(
                out=ubf[sl][:], in0=src, scalar=rs_ap, in1=grow[:], op0=ALU.mult, op1=ALU.mult),
                reads=[Ts, T_rs, T_grow], writes=[T_ubf[sl]])
            b = nb()
            psT = bank(b).bitcast(BF16)

            def tr(e, sl=sl, psT=psT):
                inst = None
                for c in range(8):
                    inst = e.transpose(out=psT[:, c * 128:(c + 1) * 128], in_=ubf[sl][:, c * 128:(c + 1) * 128],
                                       identity=ident)
                return inst
            P.op("pe", tr, reads=[T_ubf[sl], T_cst], writes=[T_ps[b]])
            t0 = ti * 128
            if evac_eng == "act":
                P.op("act", lambda e, psT=psT, t0=t0: e.activation(
                    out=uT[:, :, t0:t0 + 128], in_=psT.rearrange("p (c t) -> p c t", c=8), func=AF.Copy),
                    reads=[T_ps[b]], writes=[T_uT[ti]])
            else:
                P.op("dve", lambda e, psT=psT, t0=t0: e.tensor_copy(
                    out=uT[:, :, t0:t0 + 128], in_=psT.rearrange("p (c t) -> p c t", c=8)),
                    reads=[T_ps[b]], writes=[T_uT[ti]])

    def bstats(sq_aps, T_sq, nfeat, lnbias, lnt, T_lnt, rdst, T_rdst):
        b = nb()
        P.op("pe", mm_group(bank(b), [(ones, a) for a in sq_aps]), reads=T_sq + [T_cst], writes=[T_ps[b]])
        P.op("act", lambda e, b=b: e.activation(out=lnt[:], in_=bank(b), func=AF.Ln, scale=1.0 / nfeat, bias=EPS),
             reads=[T_ps[b]], writes=[T_lnt])
        P.op("act", lambda e: e.activation(out=rdst[:], in_=lnt[:], func=AF.Exp, scale=-0.5, bias=float(lnbias)),
             reads=[T_lnt], writes=[T_rdst])

    def chk(s_, k_):
        if stop is not None and (s_, k_) == tuple(stop):
            raise _Stop()

    def seq(s):
        w_in_sb = A.alloc("w_in", [128, 8, 1792], BF16)
        uT = A.alloc("uT", [128, 8, S], BF16)
        T_win, T_uT = Tile(), [Tile() for _ in range(16)]
        w_in_v = w_in.rearrange("(c p) n -> p c n", p=128)
        P.op("pool", lambda e: e.dma_start(out=w_in_sb[:, :, 0:704], in_=w_in_v[:, :, 0:704]), writes=[T_win], dma=True)
        P.op("pool", lambda e: e.dma_start(out=w_in_sb[:, :, 704:768], in_=w_in_v[:, :, 640:704]), writes=[T_win], dma=True)
        P.op("pool", lambda e: e.dma_start(out=w_in_sb[:, :, 768:1792], in_=w_in_v[:, :, 704:1728]), writes=[T_win], dma=True)

        wq_sb = A.alloc("wq", [128, 3, 768], BF16)
        wkv_sb = A.alloc("wkv", [128, 2, 1024], BF16)
        rope1 = A.alloc("rope1", [128, 2, S], F32)
        grow = A.alloc("grow", [128, D], F32)
        T_wq, T_wkv, T_rope1, T_grow = Tile(), Tile(), Tile(), Tile()
        P.op("pool", lambda e: e.dma_start(out=wq_sb[:], in_=wq.rearrange("(c p) n -> p c n", p=128)), writes=[T_wq], dma=True)
        P.op("pool", lambda e: e.dma_start(out=wkv_sb[:], in_=wkv.rearrange("(c p) n -> p c n", p=128)), writes=[T_wkv], dma=True)
        P.op("sp", lambda e: e.dma_start(out=rope1[:], in_=rope1_d.rearrange("a p t -> p a t")), writes=[T_rope1], dma=True)
        P.op("sp", lambda e: e.dma_start(out=grow[:], in_=grows_d[0]), writes=[T_grow], dma=True)
        Qn = A.alloc("Qn", [128, 4, S], BF16)
        Qpe = A.alloc("Qpe", [128, 2, S], BF16)
        Kn = A.alloc("Kn", [128, 4, S], BF16)
        Kpl = A.alloc("Kpl", [128, S], BF16)
        Kph = A.alloc("Kph", [128, S], BF16)
        Vm = A.alloc("Vm", [128, 16, 512], BF16)
        T_Q = [Tile() for _ in range(NB)]
        DBG.update(uT=uT, Qn=Qn, Qpe=Qpe, Kn=Kn, Kpe=Kpl, Vm=Vm, w_in_sb=w_in_sb)
        T_K, T_V = Tile(), Tile()
        T_Kpe = Tile()
        P.op("pool", lambda e: e.memset(Kpl[64:128, :], 0.0), writes=[T_Kpe])
        P.op("pool", lambda e: e.memset(Kph[0:64, :], 0.0), writes=[T_Kpe])
        T_Qpe = [Tile() for _ in range(NB)]
        xt = [A.alloc(f"xt{i}", [128, D], F32) for i in range(2)]
        T_xt = [Tile(), Tile()]
        ubf = [A.alloc(f"ubf{i}", [128, D], BF16) for i in range(2)]
        T_ubf = [Tile(), Tile()]
        junk = A.alloc("junk", [128, D], BF16)
        T_junk = Tile()
        cqs = [A.alloc(f"cq{i}", [128, 3, 512], BF16) for i in range(2)]
        sqs = [A.alloc(f"sq{i}", [128, 3, 512], BF16) for i in range(2)]
        ckvs = [A.alloc(f"ckv{i}", [128, 2, 512], BF16) for i in range(2)]
        sqkvs = [A.alloc(f"sqkv{i}", [128, 2, 512], BF16) for i in range(2)]
        T_cqs, T_sqs, T_ckvs, T_sqkvs = ([Tile(), Tile()] for _ in range(4))
        lnt = A.alloc("lnt", [128, 512], F32)
        rq = A.alloc("rq", [128, 512], F32)
        rkv = A.alloc("rkv", [128, 512], F32)
        T_lnt, T_rq, T_rkv = Tile(), Tile(), Tile()
        rkc = A.alloc("rkc", [128, 8], F32)
        T_rkc = Tile()
        qpfs = [A.alloc(f"qpf{i}", [128, 512], F32) for i in range(2)]
        qpbs = [A.alloc(f"qpb{i}", [128, 512], BF16) for i in range(2)]
        t1s = [A.alloc("t10", [128, 512], F32)] * 2
        t2s = [A.alloc("t20", [128, 512], F32)] * 2
        T_qpfs, T_qpbs = ([Tile(), Tile()] for _ in range(2))
        T_t1s, T_t2s = [Tile()] * 2, [Tile()] * 2
        xcnt = [0]
        chn = [0]

        def front(n):
            blk = slice(n * 512, (n + 1) * 512)
            cq, sq, ckv, sqkv = cqs[n % 2], sqs[n % 2], ckvs[n % 2], sqkvs[n % 2]
            T_cq, T_sq, T_ckv, T_sqkv = T_cqs[n % 2], T_sqs[n % 2], T_ckvs[n % 2], T_sqkvs[n % 2]
            for i in range(4):
                ti = n * 4 + i
                sl = xcnt[0] % 2
                xcnt[0] += 1
                P.op("sp", lambda e, sl=sl, ti=ti: e.dma_start(out=xt[sl][:], in_=x[s, ti * 128:(ti + 1) * 128, :]),
                     writes=[T_xt[sl]], dma=True)
                make_uT([xt[sl][:]], [T_xt[sl]], grow, T_grow, uT, T_uT, [ti], junk, T_junk, ubf, T_ubf, "act")
            uT_blk = [uT[:, k, blk] for k in range(8)]
            T_uTb = [T_uT[n * 4 + i] for i in range(4)]
            for c in range(3):
                b = nb()
                P.op("pe", mm_group(bank(b), [(w_in_sb[:, k, c * 128:(c + 1) * 128], uT_blk[k]) for k in range(8)]),
                     reads=T_uTb + [T_win], writes=[T_ps[b]])
                P.op("dve", lambda e, b=b, c=c: e.tensor_scalar(out=cq[:, c, :], in0=bank(b), scalar1=col(C_GQ + c),
                                                                 scalar2=None, op0=ALU.mult),
                     reads=[T_ps[b], T_cols], writes=[T_cq])
                P.op("act", lambda e, b=b, c=c: e.activation(out=sq[:, c, :], in_=bank(b), func=AF.Square),
                     reads=[T_ps[b]], writes=[T_sq])
            for c in range(2):
                b = nb()
                P.op("pe", mm_group(bank(b), [(w_in_sb[:, k, 384 + c * 128:384 + (c + 1) * 128], uT_blk[k]) for k in range(8)]),
                     reads=T_uTb + [T_win], writes=[T_ps[b]])
                P.op("dve", lambda e, b=b, c=c: e.tensor_scalar(out=ckv[:, c, :], in0=bank(b), scalar1=col(C_GKV + c),
                                                                 scalar2=None, op0=ALU.mult),
                     reads=[T_ps[b], T_cols], writes=[T_ckv])
                P.op("act", lambda e, b=b, c=c: e.activation(out=sqkv[:, c, :], in_=bank(b), func=AF.Square),
                     reads=[T_ps[b]], writes=[T_sqkv])

        def back(n):
            blk = slice(n * 512, (n + 1) * 512)
            cq, sq, ckv, sqkv = cqs[n % 2], sqs[n % 2], ckvs[n % 2], sqkvs[n % 2]
            T_cq, T_sq, T_ckv, T_sqkv = T_cqs[n % 2], T_sqs[n % 2], T_ckvs[n % 2], T_sqkvs[n % 2]
            uT_blk = [uT[:, k, blk] for k in range(8)]
            T_uTb = [T_uT[n * 4 + i] for i in range(4)]
            bstats([sq[:, c, :] for c in range(3)], [T_sq], 384, np.log(192.0 ** -0.5), lnt, T_lnt, rq, T_rq)
            bstats([sqkv[:, c, :] for c in range(2)], [T_sqkv], 256, 0.0, lnt, T_lnt, rkv, T_rkv)
            b = nb()

            def colstats(e, b=b):
                inst = None
                for i in range(4):
                    for c in range(2):
                        inst = e.matmul(bank(b)[:, i:i + 1], lhsT=sqkv[:, c, i * 128:(i + 1) * 128], rhs=ones[:, 0:1],
                                        start=(c == 0), stop=(c == 1))
                return inst
            P.op("pe", colstats, reads=[T_sqkv, T_cst], writes=[T_ps[b]])
            P.op("act", lambda e, b=b: e.activation(out=rkc[:, 4:8], in_=bank(b)[:, 0:4], func=AF.Ln, scale=1.0 / 256, bias=EPS),
                 reads=[T_ps[b]], writes=[T_rkc])
            P.op("act", lambda e: e.activation(out=rkc[:, 0:4], in_=rkc[:, 4:8], func=AF.Exp, scale=-0.5),
                 reads=[T_rkc], writes=[T_rkc])
            w = chn[0] % 2
            chn[0] += 1
            qpb, t1, t2 = qpbs[w], t1s[w], t2s[w]
            T_qpb, T_t1, T_t2 = T_qpbs[w], T_t1s[w], T_t2s[w]
            bk = nb()
            P.op("pe", mm_group(bank(bk), [(w_in_sb[:, k, 640:768], uT_blk[k]) for k in range(8)]),
                 reads=T_uTb + [T_win], writes=[T_ps[bk]])
            P.op("act", lambda e, bk=bk: e.activation(out=qpb[:], in_=bank(bk), func=AF.Copy), reads=[T_ps[bk]], writes=[T_qpb])
            b2 = nb()
            P.op("pe", mm_group(bank(b2), [(perm, qpb[:])]), reads=[T_qpb, T_cst], writes=[T_ps[b2]])
            P.op("dve", lambda e, bk=bk: e.tensor_tensor(out=t1[:], in0=bank(bk), in1=rope1[:, 0, blk], op=ALU.mult),
                 reads=[T_ps[bk], T_rope1], writes=[T_t1])
            P.op("dve", lambda e, b2=b2: e.tensor_tensor(out=t2[:], in0=bank(b2), in1=rope1[:, 1, blk], op=ALU.mult),
                 reads=[T_ps[b2], T_rope1], writes=[T_t2])
            P.op("pool", lambda e: e.tensor_tensor(out=Kpl[0:64, blk], in0=t1[0:64, :], in1=t2[0:64, :], op=ALU.add),
                 reads=[T_t1, T_t2], writes=[T_Kpe])
            P.op("pool", lambda e: e.tensor_tensor(out=Kph[64:128, blk], in0=t1[64:128, :], in1=t2[64:128, :], op=ALU.add),
                 reads=[T_t1, T_t2], writes=[T_Kpe])
            for h in range(4):
                b = nb()
                P.op("pe", mm_group(bank(b), [(wq_sb[:, c, h * 192:h * 192 + 128], cq[:, c, :]) for c in range(3)]),
                     reads=[T_cq, T_wq], writes=[T_ps[b]])
                P.op("dve", lambda e, b=b, h=h: e.tensor_tensor(out=Qn[:, h, blk], in0=bank(b), in1=rq[:], op=ALU.mult),
                     reads=[T_ps[b], T_rq], writes=[T_Q[n]])
            for pr in range(2):
                w = chn[0] % 2
                chn[0] += 1
                b = nb()

                def qpe_mm(e, b=b, pr=pr):
                    inst = None
                    for half in range(2):
                        h = 2 * pr + half
                        for c in range(3):
                            inst = e.matmul(bank(b)[half * 64:(half + 1) * 64, :],
                                            lhsT=wq_sb[:, c, h * 192 + 128:h * 192 + 192], rhs=cq[:, c, :],
                                            start=(c == 0), stop=(c == 2))
                    return inst
                P.op("pe", qpe_mm, reads=[T_cq, T_wq], writes=[T_ps[b]])
                P.op("dve", lambda e, b=b, w=w: e.tensor_tensor(out=qpfs[w][:], in0=bank(b), in1=rq[:], op=ALU.mult),
                     reads=[T_ps[b], T_rq], writes=[T_qpfs[w]])
                P.op("act", lambda e, w=w: e.activation(out=qpbs[w][:], in_=qpfs[w][:], func=AF.Copy),
                     reads=[T_qpfs[w]], writes=[T_qpbs[w]])
                b2 = nb()
                P.op("pe", mm_group(bank(b2), [(perm, qpbs[w][:])]), reads=[T_qpbs[w], T_cst], writes=[T_ps[b2]])
                P.op("pool", lambda e, w=w: e.tensor_tensor(out=t1s[w][:], in0=qpfs[w][:], in1=rope1[:, 0, blk], op=ALU.mult),
                     reads=[T_qpfs[w], T_rope1], writes=[T_t1s[w]])
                P.op("dve", lambda e, b2=b2, w=w: e.tensor_tensor(out=t2s[w][:], in0=bank(b2), in1=rope1[:, 1, blk], op=ALU.mult),
                     reads=[T_ps[b2], T_rope1], writes=[T_t2s[w]])
                P.op("pool", lambda e, pr=pr, w=w: e.tensor_tensor(out=Qpe[:, pr, blk], in0=t1s[w][:], in1=t2s[w][:], op=ALU.add),
                     reads=[T_t1s[w], T_t2s[w]], writes=[T_Qpe[n]])
            for h in range(4):
                b = nb()
                P.op("pe", mm_group(bank(b), [(wkv_sb[:, c, h * 256:h * 256 + 128], ckv[:, c, :]) for c in range(2)]),
                     reads=[T_ckv, T_wkv], writes=[T_ps[b]])
                P.op("dve", lambda e, b=b, h=h: e.tensor_tensor(out=Kn[:, h, blk], in0=bank(b), in1=rkv[:], op=ALU.mult),
                     reads=[T_ps[b], T_rkv], writes=[T_K])
            for i in range(4):
                b = nb()
                vr = [wkv_sb[:, c, :].rearrange("p (h two d) -> p h two d", h=4, two=2)[:, :, 1, :] for c in range(2)]
                P.op("pe", mm_group(bank(b), [(ckv[:, c, i * 128:(i + 1) * 128], vr[c]) for c in range(2)]),
                     reads=[T_ckv, T_wkv], writes=[T_ps[b]])
                P.op("act", lambda e, b=b, i=i: e.activation(out=Vm[:, n * 4 + i, :], in_=bank(b), func=AF.Copy,
                                                             scale=rkc[:, i:i + 1]),
                     reads=[T_ps[b], T_rkc], writes=[T_V])

        P.capture()
        front(0)
        P.replay(P.end_capture())
        for n in range(NB):
            P.capture()
            back(n)
            bl = P.end_capture()
            fl = []
            if n + 1 < NB:
                P.capture()
                front(n + 1)
                fl = P.end_capture()
            P.replay(bl, fl)
        A.free("wq", "wkv", "rope1", "grow", "xt0", "xt1", "ubf0", "ubf1", "junk", "cq0", "cq1", "sq0", "sq1",
               "ckv0", "ckv1", "sqkv0", "sqkv1", "lnt", "rq", "rkv", "rkc", "qpf0", "qpf1", "qpb0", "qpb1",
               "t10", "t20")
        P.barrier()
        A.release()
        chk(s, 1)

        def attention(score_pairs, v_of, gg_col0, o_dst, T_o_dst, wfn, extra_reads, recip_dve):
            PT = [A.alloc(f"PT{i}", [128, 1024], BF16) for i in range(4)]
            T_PT = [Tile() for _ in range(4)]
            on = A.alloc("on", [128, 4, 512], F32)
            osq = A.alloc("osq", [128, 4, 512], BF16)
            rinv = A.alloc("rinv", [128, 512], F32)
            glt = A.alloc("glt", [128, 512], F32)
            grs = A.alloc("grs", [128, 512], F32)
            T_on = [Tile() for _ in range(4)]
            T_osq = [Tile() for _ in range(4)]
            T_rinv, T_glt, T_grs = Tile(), Tile(), Tile()
            TS = [A.alloc(f"TS{i}", [128, 512], BF16) for i in range(4)]
            T_TS = [Tile() for _ in range(4)]
            steps = []
            for n in range(NB):
                for h in range(4):
                    for k2 in range(8):
                        steps.append(("a", n, h, k2))
                        if wfn is not None and n >= 1 and h in (1, 2) and k2 in (2, 6):
                            steps.append(("w", n - 1, (h - 1) * 2 + k2 // 4))
            later = []

            def S_op(i):
                sl = i % 2
                if steps[i][0] == "w":
                    wfn["pe"](steps[i][1], steps[i][2], sl)
                    return
                _, n, h, k2 = steps[i]

                def f(e):
                    inst = None
                    for j in range(2):
                        prs = score_pairs(n, h, 2 * k2 + j)
                        for m, (l, r) in enumerate(prs):
                            inst = e.matmul(PS2[sl][:, j * 512:(j + 1) * 512], lhsT=l, rhs=r,
                                            start=(m == 0), stop=(m == len(prs) - 1))
                    return inst
                P.op("pe", f, reads=[T_Q[n], T_K] + extra_reads(n), writes=[T_ps[2 * sl], T_ps[2 * sl + 1]])

            def EXP_op(i):
                sl = i % 2
                if steps[i][0] == "w":
                    wfn["post"](steps[i][1], steps[i][2], sl)
                    later.append((i + 4, lambda: wfn["stats"](steps[i][1], steps[i][2])))
                    return
                P.op("act", lambda e: e.activation(out=PT[i % 4][:], in_=PS2[sl][:], func=AF.Exp),
                     reads=[T_ps[2 * sl], T_ps[2 * sl + 1]], writes=[T_PT[i % 4]])

            def PV_op(i):
                if steps[i][0] == "w":
                    return
                _, n, h, k2 = steps[i]
                par = h % 2
                ob = 4 + par
                P.op("dve", lambda e: e.tensor_tensor(out=TS[i % 4][:], in0=PT[i % 4][:, 0:512], in1=PT[i % 4][:, 512:1024],
                                                      op=ALU.add), reads=[T_PT[i % 4]], writes=[T_TS[i % 4]])

                def f(e):
                    inst = None
                    for j in range(2):
                        kc = 2 * k2 + j
                        inst = e.matmul(bank(ob), lhsT=v_of(h, kc), rhs=PT[i % 4][:, j * 512:(j + 1) * 512],
                                        start=(kc == 0), stop=(kc == 15))
                    return inst
                P.op("pe", f, reads=[T_PT[i % 4], T_V], writes=[T_ps[ob]])

            def SUM_op(i):
                if steps[i][0] == "w":
                    return
                _, n, h, k2 = steps[i]
                sb = 6 + h % 2
                P.op("pe", lambda e: e.matmul(bank(sb), lhsT=ones, rhs=TS[i % 4][:], start=(k2 == 0), stop=(k2 == 7)),
                     reads=[T_TS[i % 4], T_cst], writes=[T_ps[sb]])

            def finalize_head(i):
                _, n, h, k2 = steps[i]
                par = h % 2
                ob, sb = 4 + par, 6 + par
                def mult():
                    P.op("dve", lambda e: e.tensor_tensor(out=on[:, h, :], in0=bank(ob), in1=rinv[:], op=ALU.mult),
                         reads=[T_ps[ob], T_rinv], writes=[T_on[h]])
                if recip_dve:
                    def chunk(c):
                        P.op("dve", lambda e: e.reciprocal(out=rinv[:, c * 128:(c + 1) * 128], in_=bank(sb)[:, c * 128:(c + 1) * 128]),
                             reads=[T_ps[sb]], writes=[T_rinv])
                    chunk(0)
                    chunk(1)
                    later.append((i + 2, lambda: chunk(2)))
                    later.append((i + 2, lambda: chunk(3)))
                    later.append((i + 3, mult))
                    d0 = 3
                else:
                    P.op("act", lambda e: e.activation(out=glt[:], in_=bank(sb), func=AF.Ln), reads=[T_ps[sb]], writes=[T_glt])
                    P.op("act", lambda e: e.activation(out=rinv[:], in_=glt[:], func=AF.Exp, scale=-1.0), reads=[T_glt], writes=[T_rinv])
                    mult()
                    d0 = 1
                later.append((i + 3 + d0, lambda: P.op(
                    "act", lambda e: e.activation(out=osq[:, h, :], in_=on[:, h, :], func=AF.Square),
                    reads=[T_on[h]], writes=[T_osq[h]])))
                if h == 3:
                    def gss():
                        P.op("pe", mm_group(bank(7), [(ones, osq[:, hh, :]) for hh in range(4)]),
                             reads=T_osq + [T_cst], writes=[T_ps[7]])
                        P.op("act", lambda e: e.activation(out=glt[:], in_=bank(7), func=AF.Ln, scale=1.0 / 512, bias=EPS),
                             reads=[T_ps[7]], writes=[T_glt])
                        P.op("act", lambda e: e.activation(out=grs[:], in_=glt[:], func=AF.Exp, scale=-0.5),
                             reads=[T_glt], writes=[T_grs])
                        for hh in range(4):
                            P.op("dve", lambda e, hh=hh: e.scalar_tensor_tensor(
                                out=o_dst(n, hh), in0=on[:, hh, :], scalar=col(gg_col0 + hh), in1=grs[:],
                                op0=ALU.mult, op1=ALU.mult),
                                reads=[T_on[hh], T_cols, T_grs], writes=[T_o_dst(n)])
                    later.append((i + 4 + d0, gss))

            def run_due(i):
                due = [x for x in later if x[0] <= i]
                for x in due:
                    later.remove(x)
                    x[1]()

            S_op(0)
            for i in range(len(steps)):
                if i + 1 < len(steps):
                    S_op(i + 1)
                EXP_op(i)
                if i >= 2:
                    SUM_op(i - 2)
                PV_op(i)
                run_due(i)
                if steps[i][0] == "a" and steps[i][3] == 7:
                    later.append((i + 2, lambda i=i: finalize_head(i)))
            SUM_op(len(steps) - 2)
            SUM_op(len(steps) - 1)
            while later:
                later.sort(key=lambda x: x[0])
                x = later.pop(0)
                x[1]()
            if wfn is not None:
                for ii in range(4):
                    wfn["pe"](NB - 1, ii, ii % 2)
                    wfn["post"](NB - 1, ii, ii % 2)
                    wfn["stats"](NB - 1, ii)
            A.free("PT0", "PT1", "PT2", "on", "osq", "rinv", "glt", "grs", "TS0", "TS1", "TS2", "TS3", "PT3")

        o_mla = A.alloc("o_mla", [128, 4, S], BF16)
        T_omla = [Tile() for _ in range(NB)]
        DBG.update(o_mla=o_mla)

        def mla_pairs(n, h, kc):
            blk = slice(n * 512, (n + 1) * 512)
            ks = slice(kc * 128, (kc + 1) * 128)
            kp = Kpl if h % 2 == 0 else Kph
            return [(Kn[:, h, ks], Qn[:, h, blk]), (kp[:, ks], Qpe[:, h // 2, blk])]

        attention(mla_pairs, lambda h, kc: Vm[:, kc, h * 128:(h + 1) * 128], C_GGM,
                  lambda n, hh: o_mla[:, hh, n * 512:(n + 1) * 512], lambda n: T_omla[n], None,
                  lambda n: [T_Qpe[n], T_Kpe], False)
        A.free("Qn", "Qpe", "Kn", "Kpl", "Kph", "Vm")
        P.barrier()
        A.release()
        chk(s, 2)

        Qg = A.alloc("Qg", [128, 4, S], BF16)
        Kg = A.alloc("Kg", [128, 2, S], BF16)
        Vg = A.alloc("Vg", [128, 16, 256], BF16)
        DBG.update(Qg=Qg, Kg=Kg, Vg=Vg)
        T_Q = [Tile() for _ in range(NB)]
        T_K, T_V = Tile(), Tile()
        w_out_sb = A.alloc("w_out", [128, 8, D], BF16)
        T_wout = Tile()
        P.op("pool", lambda e: e.dma_start(out=w_out_sb[:], in_=w_out.rearrange("(c p) n -> p c n", p=128)),
             writes=[T_wout], dma=True)
        qg = [A.alloc(f"qg{i}", [128, 512], BF16) for i in range(2)]
        sqg = [A.alloc(f"sqg{i}", [128, 512], BF16) for i in range(2)]
        lng = [A.alloc(f"lng{i}", [128, 512], F32) for i in range(2)]
        rg = [A.alloc(f"rg{i}", [128, 512], F32) for i in range(2)]
        g1 = [A.alloc(f"g1{i}", [128, 512], F32) for i in range(2)]
        g2 = [A.alloc(f"g2{i}", [128, 512], F32) for i in range(2)]
        T_qg, T_sqg, T_lng, T_rg, T_g1, T_g2 = ([Tile(), Tile()] for _ in range(6))
        rstep = ropeg[:].ap[0][0]
        hcnt = 0
        items = []
        for n in range(NB):
            blk = slice(n * 512, (n + 1) * 512)
            uT_blk = [uT[:, k, blk] for k in range(8)]
            T_uTb = [T_uT[n * 4 + i] for i in range(4)]
            cc_top = bass.AP(ropeg, 0 * rstep + n * 8, [[rstep, 64], [1, 8], [0, 64]])
            ss_top = bass.AP(ropeg, 0 * rstep + 64 + n * 8, [[rstep, 64], [1, 8], [0, 64]])
            cc_bot = bass.AP(ropeg, 64 * rstep, [[rstep, 64], [0, 8], [1, 64]])
            ss_bot = bass.AP(ropeg, 64 * rstep + 64, [[rstep, 64], [0, 8], [1, 64]])

            def v3(ap):
                return ap.rearrange("p (a b) -> p a b", a=8)
            for hd in range(6):
                isq = hd < 4
                c0 = 768 + hd * 128 if isq else 1280 + (hd - 4) * 128
                gcol = C_GQQ if isq else C_GQK
                dst = Qg[:, hd, blk] if isq else Kg[:, hd - 4, blk]
                Tdst = T_Q[n] if isq else T_K
                lnb = np.log(128.0 ** -0.5) if isq else 0.0
                u = hcnt % 2
                hcnt += 1
                P.capture()
                b = nb()
                P.op("pe", mm_group(bank(b), [(w_in_sb[:, k, c0:c0 + 128], uT_blk[k]) for k in range(8)]),
                     reads=T_uTb + [T_win], writes=[T_ps[b]])
                P.op("act", lambda e, b=b, u=u, gcol=gcol: e.activation(out=qg[u][:], in_=bank(b), func=AF.Copy, scale=col(gcol)),
                     reads=[T_ps[b], T_cols], writes=[T_qg[u]])
                P.op("act", lambda e, b=b, u=u: e.activation(out=sqg[u][:], in_=bank(b), func=AF.Square),
                     reads=[T_ps[b]], writes=[T_sqg[u]])
                P.op("dve", lambda e, b=b, u=u, gcol=gcol, cc_top=cc_top: e.scalar_tensor_tensor(
                    out=v3(g1[u][0:64, :]), in0=v3(bank(b)[0:64, :]), scalar=cols[0:64, gcol:gcol + 1], in1=cc_top,
                    op0=ALU.mult, op1=ALU.mult), reads=[T_ps[b], T_cols, T_ropeg], writes=[T_g1[u]])
                P.op("dve", lambda e, b=b, u=u, gcol=gcol, cc_bot=cc_bot: e.scalar_tensor_tensor(
                    out=v3(g1[u][64:128, :]), in0=v3(bank(b)[64:128, :]), scalar=cols[64:128, gcol:gcol + 1], in1=cc_bot,
                    op0=ALU.mult, op1=ALU.mult), reads=[T_ps[b], T_cols, T_ropeg], writes=[T_g1[u]])
                bs = nb()
                P.op("pe", mm_group(bank(bs), [(ones, sqg[u][:])]), reads=[T_sqg[u], T_cst], writes=[T_ps[bs]])
                bw = nb()
                P.op("pe", mm_group(bank(bw), [(perm, qg[u][:])]), reads=[T_qg[u], T_cst], writes=[T_ps[bw]])
                P.op("act", lambda e, bs=bs, u=u: e.activation(out=lng[u][:], in_=bank(bs), func=AF.Ln, scale=1.0 / 128, bias=EPS),
                     reads=[T_ps[bs]], writes=[T_lng[u]])
                P.op("act", lambda e, u=u, lnb=lnb: e.activation(out=rg[u][:], in_=lng[u][:], func=AF.Exp, scale=-0.5, bias=float(lnb)),
                     reads=[T_lng[u]], writes=[T_rg[u]])
                P.op("dve", lambda e, bw=bw, u=u, ss_top=ss_top: e.tensor_tensor(
                    out=v3(g2[u][0:64, :]), in0=v3(bank(bw)[0:64, :]), in1=ss_top, op=ALU.mult),
                    reads=[T_ps[bw], T_ropeg], writes=[T_g2[u]])
                P.op("dve", lambda e, bw=bw, u=u, ss_bot=ss_bot: e.tensor_tensor(
                    out=v3(g2[u][64:128, :]), in0=v3(bank(bw)[64:128, :]), in1=ss_bot, op=ALU.mult),
                    reads=[T_ps[bw], T_ropeg], writes=[T_g2[u]])
                P.op("pool", lambda e, u=u: e.tensor_tensor(out=g1[u][:], in0=g1[u][:], in1=g2[u][:], op=ALU.add),
                     reads=[T_g2[u]], writes=[T_g1[u]])
                P.op("pool", lambda e, u=u, dst=dst: e.tensor_tensor(out=dst, in0=g1[u][:], in1=rg[u][:], op=ALU.mult),
                     reads=[T_g1[u], T_rg[u]], writes=[Tdst])
                L = P.end_capture()
                items.append((L[:5], L[5:]))
            for i in range(4):
                P.capture()
                b = nb()
                t0 = (n * 4 + i) * 128
                P.op("pe", mm_group(bank(b)[:, 0:256], [(uT[:, k, t0:t0 + 128], w_in_sb[:, k, 1536:1792]) for k in range(8)]),
                     reads=[T_uT[n * 4 + i], T_win], writes=[T_ps[b]])
                P.op("act", lambda e, b=b, i=i, n=n: e.activation(out=Vg[:, n * 4 + i, :], in_=bank(b)[:, 0:256], func=AF.Copy),
                     reads=[T_ps[b]], writes=[T_V])
                items[-1 - i % 2] = (items[-1 - i % 2][0], items[-1 - i % 2][1] + P.end_capture())
        for i in range(len(items)):
            P.replay(items[i][0])
            P.replay(items[i][1])
        A.free("w_in", "uT", "qg0", "qg1", "sqg0", "sqg1", "lng0", "lng1", "rg0", "rg1", "g10", "g11", "g20", "g21")
        P.barrier()
        A.release()
        chk(s, 3)

        hres = A.alloc("h", [128, 16, D], F32)
        T_h = [Tile() for _ in range(16)]
        DBG.update(h=hres)
        og = A.alloc("og", [128, 4, 512], BF16)
        T_og = Tile()
        xr = [A.alloc(f"xr{i}", [128, D], F32) for i in range(4)]
        T_xr = [Tile() for _ in range(4)]
        junk2 = A.alloc("junk2", [128, D], BF16)
        T_junk2 = Tile()
        T_ss2 = [Tile() for _ in range(16)]
        T_r2 = Tile()

        def w_pe(n, i, sl):
            ti = n * 4 + i
            t0 = ti * 128
            P.op("sp", lambda e: e.dma_start(out=xr[i][:], in_=x[s, t0:t0 + 128, :]), writes=[T_xr[i]], dma=True)

            def f(e):
                inst = None
                for fh in range(2):
                    pairs = [(o_mla[:, c, t0:t0 + 128], w_out_sb[:, c, fh * 512:(fh + 1) * 512]) for c in range(4)]
                    pairs += [(og[:, c, i * 128:(i + 1) * 128], w_out_sb[:, 4 + c, fh * 512:(fh + 1) * 512]) for c in range(4)]
                    for m, (l, r) in enumerate(pairs):
                        inst = e.matmul(PS2[sl][:, fh * 512:(fh + 1) * 512], lhsT=l, rhs=r, start=(m == 0), stop=(m == 7))
                return inst
            P.op("pe", f, reads=[T_omla[n], T_og, T_wout], writes=[T_ps[2 * sl], T_ps[2 * sl + 1]])

        def w_post(n, i, sl):
            ti = n * 4 + i
            P.op("dve", lambda e: e.tensor_tensor(out=hres[:, ti, :], in0=PS2[sl][:], in1=xr[i][:], op=ALU.add),
                 reads=[T_ps[2 * sl], T_ps[2 * sl + 1], T_xr[i]], writes=[T_h[ti]])

        def w_stats(n, i):
            ti = n * 4 + i
            P.op("act", lambda e: e.activation(out=junk2[:], in_=hres[:, ti, :], func=AF.Square, accum_out=st2[:, ti:ti + 1]),
                 reads=[T_h[ti]], writes=[T_junk2, T_ss2[ti]])

        def gqa_pairs(n, h, kc):
            return [(Kg[:, h // 2, kc * 128:(kc + 1) * 128], Qg[:, h, n * 512:(n + 1) * 512])]

        attention(gqa_pairs, lambda h, kc: Vg[:, kc, (h // 2) * 128:(h // 2 + 1) * 128], C_GGG,
                  lambda n, hh: og[:, hh, :], lambda n: T_og, {"pe": w_pe, "post": w_post, "stats": w_stats}, lambda n: [], False)
        A.free("Qg", "Kg", "Vg", "w_out", "o_mla", "og", "xr0", "xr1", "xr2", "xr3", "junk2")
        P.barrier()
        A.release()
        chk(s, 4)

        u2T = A.alloc("u2T", [128, 8, S], BF16)
        T_u2T = [Tile() for _ in range(16)]
        DBG.update(u2T=u2T)
        DBGP3 = True
        grow2 = A.alloc("grow2", [128, D], F32)
        T_grow2 = Tile()
        P.op("sp", lambda e: e.dma_start(out=grow2[:], in_=grows_d[1]), writes=[T_grow2], dma=True)
        ubf = [A.alloc(f"ubf{i}", [128, D], BF16) for i in range(4)]
        T_ubf = [Tile() for _ in range(4)]
        junk = A.alloc("junk", [128, D], BF16)
        T_junk = Tile()
        NWU = 3
        wg = [A.alloc(f"wg{i}", [128, 8, 128], BF16) for i in range(NWU)]
        wv = [A.alloc(f"wv{i}", [128, 8, 128], BF16) for i in range(NWU)]
        T_wu = [Tile() for _ in range(NWU)]
        NWD = 8
        wd = [A.alloc(f"wd{i}", [128, D], BF16) for i in range(NWD)]
        T_wd = [Tile() for _ in range(NWD)]
        actT = [A.alloc(f"actT{i}", [128, S], BF16) for i in range(6)]
        T_actT = [Tile() for _ in range(6)]
        hug = [A.alloc(f"hug{i}", [128, S + 2], BF16) for i in range(2)]
        huv = [A.alloc(f"huv{i}", [128, S + 2], BF16) for i in range(2)]
        T_hu = [[Tile() for _ in range(NB)] for _ in range(2)]
        T_hv = [[Tile() for _ in range(NB)] for _ in range(2)]
        dg = [A.alloc(f"dg{i}", [128, 6, 128], BF16) for i in range(2)]
        T_dg = [Tile(), Tile()]
        sg = [A.alloc(f"sg{i}", [128, 512], BF16) for i in range(2)]
        T_sg = [Tile(), Tile()]
        DBG.update(actT0=actT[0], actT4=actT[4], hug0=hug[0], hug1=hug[1], huv0=huv[0], huv1=huv[1])
        for u in range(2):
            for buf, TT in ((hug[u], T_hu[u]), (huv[u], T_hv[u])):
                P.op("pool", lambda e, buf=buf: e.memset(buf[:, 0:1], 0.0), writes=[TT[0]])
                P.op("pool", lambda e, buf=buf: e.memset(buf[:, S + 1:S + 2], 0.0), writes=[TT[NB - 1]])
        w_up_v = w_up.rearrange("(c p) n -> p c n", p=128)
        jlist = []
        for gi, gsz in enumerate(GROUPS):
            j0 = sum(GROUPS[:gi])
            jlist.append(list(range(j0, j0 + gsz)))

        def load_w(j):
            su, sd = j % NWU, j % NWD
            P.op("pool", lambda e: e.dma_start(out=wg[su][:], in_=w_up_v[:, :, j * 128:(j + 1) * 128]), writes=[T_wu[su]], dma=True)
            P.op("pool", lambda e: e.dma_start(out=wv[su][:], in_=w_up_v[:, :, DFF + j * 128:DFF + (j + 1) * 128]),
                 writes=[T_wu[su]], dma=True)
            P.op("pool", lambda e: e.dma_start(out=wd[sd][:], in_=w_down[j * 128:(j + 1) * 128, :]), writes=[T_wd[sd]], dma=True)

        def dg_build(j):
            u = j % 2
            for k in range(3):
                P.op("pool", lambda e, k=k: e.tensor_scalar(out=dg[u][:, k, :], in0=ident, scalar1=col(C_CW + k * 44 + j),
                                                           scalar2=None, op0=ALU.mult),
                     reads=[T_cst, T_cols], writes=[T_dg[u]])
                P.op("pool", lambda e, k=k: e.tensor_scalar(out=dg[u][:, 3 + k, :], in0=ident, scalar1=col(C_CW + k * 44 + 22 + j),
                                                           scalar2=None, op0=ALU.mult),
                     reads=[T_cst, T_cols], writes=[T_dg[u]])

        def hu_block(j, n):
            su, u = j % NWU, j % 2
            blk = slice(n * 512, (n + 1) * 512)
            rd = [T_u2T[n * 4 + i] for i in range(4)] + [T_wu[su]]
            b = n % 2
            P.op("pe", mm_group(bank(b), [(wg[su][:, k, :], u2T[:, k, blk]) for k in range(8)]), reads=rd, writes=[T_ps[b]])
            P.op("act", lambda e, b=b: e.activation(out=hug[u][:, 1 + n * 512:1 + (n + 1) * 512], in_=bank(b), func=AF.Copy),
                 reads=[T_ps[b]], writes=[T_hu[u][n]])
            b = 2 + n % 2
            P.op("pe", mm_group(bank(b), [(wv[su][:, k, :], u2T[:, k, blk]) for k in range(8)]), reads=rd, writes=[T_ps[b]])
            P.op("dve", lambda e, b=b: e.tensor_copy(out=huv[u][:, 1 + n * 512:1 + (n + 1) * 512], in_=bank(b)),
                 reads=[T_ps[b]], writes=[T_hv[u][n]])

        def hu_ops(j):
            dg_build(j)
            for n in range(NB):
                hu_block(j, n)

        def conv_ops(j, jj):
            u = j % 2
            for n in range(NB):
                nbrs = [T_hu[u][m] for m in (n - 1, n, n + 1) if 0 <= m < NB]
                nbrv = [T_hv[u][m] for m in (n - 1, n, n + 1) if 0 <= m < NB]
                bg, bv = 4 + n % 2, 6 + n % 2
                P.op("pe", mm_group(bank(bg), [(dg[u][:, k, :], hug[u][:, n * 512 + k:n * 512 + k + 512]) for k in range(3)]),
                     reads=nbrs + [T_dg[u]], writes=[T_ps[bg]])
                P.op("pe", mm_group(bank(bv), [(dg[u][:, 3 + k, :], huv[u][:, n * 512 + k:n * 512 + k + 512]) for k in range(3)]),
                     reads=nbrv + [T_dg[u]], writes=[T_ps[bv]])
                v = n % 2
                P.op("act", lambda e, bg=bg, v=v: e.activation(out=sg[v][:], in_=bank(bg), func=AF.Silu, bias=col(C_CB + j)),
                     reads=[T_ps[bg], T_cols], writes=[T_sg[v]])
                P.op("dve", lambda e, bv=bv, v=v, n=n: e.scalar_tensor_tensor(
                    out=actT[jj][:, n * 512:(n + 1) * 512], in0=bank(bv), scalar=col(C_CB + 22 + j), in1=sg[v][:],
                    op0=ALU.add, op1=ALU.mult),
                    reads=[T_ps[bv], T_cols, T_sg[v]], writes=[T_actT[jj]])

        yt = None
        allj = [j for g in jlist for j in g]
        load_w(0)
        load_w(1)
        dg_build(0)
        dg_build(1)
        P.op("act", lambda e: e.activation(out=st2[:, 16:32], in_=st2[:, 0:16], func=AF.Ln, scale=1.0 / D, bias=EPS),
             reads=T_ss2, writes=[T_r2])
        P.op("act", lambda e: e.activation(out=st2[:, 32:48], in_=st2[:, 16:32], func=AF.Exp, scale=-0.5),
             reads=[T_r2], writes=[T_r2])
        for n in range(NB):
            for i in range(4):
                ti = n * 4 + i
                make_uT([hres[:, ti, :]], [T_h[ti]], grow2, T_grow2, u2T, T_u2T, [ti], junk, T_junk, ubf, T_ubf,
                        "act", rstd=[(st2[:, 32 + ti:33 + ti], T_r2)])
            hu_block(0, n)
            hu_block(1, n)
        def down_ops(gi):
            js = jlist[gi]
            lastg = gi == len(jlist) - 1
            if lastg:
                growf = A.alloc("growf", [128, D], F32)
                T_growf = Tile()
                P.op("sp", lambda e: e.dma_start(out=growf[:], in_=grows_d[2]), writes=[T_growf], dma=True)
                yt = [A.alloc(f"yt{i}", [128, D], F32) for i in range(2)]
                T_yt = [Tile(), Tile()]
            fin = []
            for ti in range(16):
                t0 = ti * 128
                for fh in range(2):
                    b = (2 * ti + fh) % 4
                    P.op("pe", mm_group(bank(b), [(actT[jj][:, t0:t0 + 128], wd[j % NWD][:, fh * 512:(fh + 1) * 512])
                                                  for jj, j in enumerate(js)]),
                         reads=[T_actT[jj] for jj in range(len(js))] + [T_wd[j % NWD] for j in js], writes=[T_ps[b]])
                    P.op("dve", lambda e, b=b, ti=ti, fh=fh: e.tensor_tensor(
                        out=hres[:, ti, fh * 512:(fh + 1) * 512], in0=bank(b), in1=hres[:, ti, fh * 512:(fh + 1) * 512], op=ALU.add),
                        reads=[T_ps[b], T_h[ti]], writes=[T_h[ti]])
                if lastg:
                    k = stcnt[0] % 4
                    stcnt[0] += 1
                    c0 = 4 * k
                    sl = ti % 2
                    P.op("act", lambda e, ti=ti, c0=c0: e.activation(out=junk[:], in_=hres[:, ti, :], func=AF.Square,
                                                                    accum_out=st[:, c0:c0 + 1]),
                         reads=[T_h[ti]], writes=[T_junk, T_st[k]])
                    P.op("act", lambda e, c0=c0: e.activation(out=st[:, c0 + 1:c0 + 2], in_=st[:, c0:c0 + 1], func=AF.Ln,
                                                              scale=1.0 / D, bias=EPS), reads=[T_st[k]], writes=[T_st[k]])
                    P.op("act", lambda e, c0=c0: e.activation(out=st[:, c0 + 2:c0 + 3], in_=st[:, c0 + 1:c0 + 2], func=AF.Exp,
                                                              scale=-0.5), reads=[T_st[k]], writes=[T_st[k]])
                    def finish(ti=ti, c0=c0, sl=sl, k=k, t0=t0):
                        P.op("dve", lambda e: e.scalar_tensor_tensor(
                            out=yt[sl][:], in0=hres[:, ti, :], scalar=st[:, c0 + 2:c0 + 3], in1=growf[:], op0=ALU.mult, op1=ALU.mult),
                            reads=[T_h[ti], T_st[k], T_growf], writes=[T_yt[sl]])
                        P.op("sp", lambda e: e.dma_start(out=y[s, t0:t0 + 128, :], in_=yt[sl][:]),
                             reads=[T_yt[sl]], dma=True)
                    fin.append(finish)
                    if len(fin) >= 2:
                        fin.pop(0)()
            for f_ in fin:
                f_()

        flat = [(gi, jj, j) for gi, js in enumerate(jlist) for jj, j in enumerate(js)]
        for idx, (gi, jj, j) in enumerate(flat):
            if j >= 2:
                hu_ops(j)
            if idx > 0:
                pgi, pjj, pj = flat[idx - 1]
                conv_ops(pj, pjj)
                if pjj == len(jlist[pgi]) - 1:
                    down_ops(pgi)
            if j + 2 < NJ:
                load_w(j + 2)
        pgi, pjj, pj = flat[-1]
        conv_ops(pj, pjj)
        down_ops(pgi)
        names = ["u2T", "grow2", "growf", "ubf0", "ubf1", "ubf2", "ubf3", "junk", "h", "yt0", "yt1"]
        names += [f"wg{i}" for i in range(NWU)] + [f"wv{i}" for i in range(NWU)] + [f"wd{i}" for i in range(NWD)]
        names += [f"actT{i}" for i in range(6)] + ["hug0", "hug1", "huv0", "huv1", "dg0", "dg1", "sg0", "sg1"]
        A.free(*names)
        P.barrier()
        A.release()
        chk(s, 5)

    try:
        for s_ in range(2):
            seq(s_)
    except _Stop:
        P.barrier()
        for name in (dbg or []):
            t = DBG[name]
            shp = list(t[:].shape)
            dd = nc.dram_tensor("dbg_" + name, shp, t[:].dtype, kind="ExternalOutput").ap()
            P.op("sp", lambda e, t=t, dd=dd: e.dma_start(out=dd, in_=t[:]), dma=True)
        P.barrier()

    semnames = list(Prog.ENGS) + [(e, i) for e in Prog.ENGS for i in range(Prog.NRING)]
    with contextlib.ExitStack() as es:
        sems = {k: es.enter_context(nc.semaphore("s_" + str(k).replace(" ", "").replace("'", "").replace("(", "")
                                                 .replace(")", "").replace(",", "_"))) for k in semnames}
        P.finalize(sems)
        block = es.enter_context(nc.Block())
        block.tensor(lambda e: P.emit_engine("pe", e))
        block.scalar(lambda e: P.emit_engine("act", e))
        block.vector(lambda e: P.emit_engine("dve", e))
        block.gpsimd(lambda e: P.emit_engine("pool", e))
        block.sync(lambda e: P.emit_engine("sp", e))
    return nc


def _rope_tables():
    inv = (10000.0 ** (-np.arange(0, 64, 2, dtype=np.float32) / 64.0)).astype(np.float32)
    p = np.arange(128)
    f = p % 32
    sign = np.where((p % 64) < 32, -1.0, 1.0).astype(np.float32)
    t = np.arange(S, dtype=np.float32)
    ang1 = (t[None, :] * inv[f][:, None]).astype(np.float32)
    rope1 = np.stack([np.cos(ang1), sign[:, None] * np.sin(ang1)], 0).astype(np.float32)
    pos = np.arange(64, dtype=np.float32)
    angg = (pos[None, :] * inv[f][:, None]).astype(np.float32)
    ropeg = np.concatenate([np.cos(angg), sign[:, None] * np.sin(angg)], 1).astype(np.float32)
    return rope1, ropeg


def _consts():
    ident = np.eye(128, dtype=np.float32)
    ones = np.ones((128, 128), np.float32)
    perm = np.zeros((128, 128), np.float32)
    for m in range(128):
        k = m + 32 if (m % 64) < 32 else m - 32
        perm[k, m] = 1.0
    return np.concatenate([ident, ones, perm], 1).astype(ml_dtypes.bfloat16)


_NC_CACHE = {}


def kernel(x, norm1_g, w_in, mla_q_norm_g, mla_w_q_up, mla_kv_norm_g, mla_w_kv_up,
           gqa_q_norm_g, gqa_k_norm_g, group_norm_mla_g, group_norm_gqa_g, w_out,
           norm2_g, ffn_w_up, ffn_conv_w, ffn_conv_b, ffn_w_down, final_norm_g):
    f32 = np.float32
    x = np.asarray(x, f32)

    def colify(v, n):
        return np.asarray(v, f32).reshape(n, 128).T

    cols = np.concatenate([
        colify(mla_q_norm_g[0], 3), colify(mla_kv_norm_g[0], 2),
        colify(gqa_q_norm_g[0], 1), colify(gqa_k_norm_g[0], 1),
        colify(group_norm_mla_g[0], 4), colify(group_norm_gqa_g[0], 4),
        colify(ffn_conv_w[0, 0], 44), colify(ffn_conv_w[0, 1], 44), colify(ffn_conv_w[0, 2], 44),
        colify(ffn_conv_b[0], 44)], 1)
    cols = np.ascontiguousarray(cols, f32)
    assert cols.shape == (128, NCOLS)
    grows = np.ascontiguousarray(np.stack([
        np.broadcast_to(np.asarray(norm1_g[0], f32), (128, D)),
        np.broadcast_to(np.asarray(norm2_g[0], f32), (128, D)),
        np.broadcast_to(np.asarray(final_norm_g, f32), (128, D))], 0))
    rope1, ropeg = _rope_tables()
    cst = _consts()
    if "nc" not in _NC_CACHE:
        _NC_CACHE["nc"] = build_nc()
    nc = _NC_CACHE["nc"]
    shared = {
        "w_in": np.ascontiguousarray(w_in[0], f32), "wq": np.ascontiguousarray(mla_w_q_up[0], f32),
        "wkv": np.ascontiguousarray(mla_w_kv_up[0], f32), "w_out": np.ascontiguousarray(w_out[0], f32),
        "w_up": np.ascontiguousarray(ffn_w_up[0], f32), "w_down": np.ascontiguousarray(ffn_w_down[0], f32),
        "cst": cst, "cols": cols, "grows": grows, "rope1": rope1, "ropeg": ropeg,
    }
    in_maps = []
    for c in range(8):
        m = dict(shared)
        m["x"] = np.ascontiguousarray(x[2 * c:2 * c + 2])
        in_maps.append(m)
    res = run_bass_kernel_spmd(nc, in_maps, core_ids=list(range(8)))
    out = np.concatenate([np.asarray(r["y"], f32) for r in res.results], 0)
    return out
```

```python
import contextlib
import numpy as np
import ml_dtypes
import concourse.bass as bass
import concourse.mybir as mybir
from concourse.bass_utils import run_bass_kernel_spmd

F32 = mybir.dt.float32
BF16 = mybir.dt.bfloat16
AF = mybir.ActivationFunctionType
ALU = mybir.AluOpType

SB_LO = 16512
SB_HI = 229344
EPS = 1e-6
S = 2048
D = 1024
NB = 4
DFF = 2816
NJ = 22
GROUPS = [6, 6, 6, 4]

C_GQ, C_GKV, C_GQQ, C_GQK, C_GGM, C_GGG, C_CW, C_CB, NCOLS = 0, 3, 5, 6, 7, 11, 15, 147, 191


class Tile:
    __slots__ = ("w", "rs", "excl")

    def __init__(self, excl=False):
        self.w = None
        self.rs = {}
        self.excl = excl


class Ev:
    __slots__ = ("eng", "emit", "deps", "val", "sem", "dma", "needed", "pre")

    def __init__(self, eng, emit, dma):
        self.eng = eng
        self.emit = emit
        self.dma = dma
        self.deps = ()
        self.val = None
        self.sem = None
        self.needed = False
        self.pre = None


class Arena:
    def __init__(self, nc):
        self.nc = nc
        self.live = {}
        self.pending = []
        self.n = 0

    def alloc(self, name, shape, dtype):
        esz = 2 if dtype == BF16 else 4
        free = 1
        for d in shape[1:]:
            free *= d
        nbytes = ((free * esz + 63) // 64) * 64
        pos = SB_LO
        for lo, hi in sorted(self.live.values()):
            if pos + nbytes <= lo:
                break
            pos = max(pos, hi)
        if pos + nbytes > SB_HI:
            raise RuntimeError(f"SBUF arena full allocating {name} ({nbytes}B), used={self.used()}")
        assert name not in self.live, name
        self.live[name] = (pos, pos + nbytes)
        self.n += 1
        return self.nc.alloc_sbuf_tensor_at(f"{name}_{self.n}", list(shape), dtype, offset=pos)

    def free(self, *names):
        self.pending.extend(names)

    def release(self):
        for n in self.pending:
            del self.live[n]
        self.pending = []

    def used(self):
        return sum(hi - lo for lo, hi in self.live.values())


class Prog:
    ENGS = ("pe", "act", "dve", "pool", "sp")
    NRING = 8

    def __init__(self):
        self.q = {e: [] for e in self.ENGS}
        self.ndma = {e: 0 for e in self.ENGS}
        self.last_compute = {e: None for e in self.ENGS}
        self.recent_dma = {e: [] for e in self.ENGS}
        self.cap = None

    def capture(self):
        self.cap = []

    def end_capture(self):
        l, self.cap = self.cap, None
        return l

    def replay(self, *lists):
        idx = [0] * len(lists)
        total = sum(len(l) for l in lists)
        for _ in range(total):
            best, bk = None, None
            for k, l in enumerate(lists):
                if idx[k] < len(l):
                    frac = idx[k] / len(l)
                    if best is None or frac < best:
                        best, bk = frac, k
            a = lists[bk][idx[bk]]
            idx[bk] += 1
            self.op(*a)

    def op(self, eng, emit, reads=(), writes=(), dma=False):
        if self.cap is not None:
            self.cap.append((eng, emit, list(reads), list(writes), dma))
            return None
        ev = Ev(eng, emit, dma)
        deps = set()
        xr = [t for t in reads if t.excl]
        if xr:
            reads = [t for t in reads if not t.excl]
            writes = list(writes) + xr
        for t in reads:
            if t.w is not None:
                deps.add(t.w)
        for t in writes:
            if t.w is not None:
                deps.add(t.w)
            deps.update(t.rs.values())
        if dma:
            i = self.ndma[eng]
            self.ndma[eng] = i + 1
            ev.sem = (eng, i % self.NRING)
            ev.val = 16 * (i // self.NRING + 1)
            rd = self.recent_dma[eng]
            if len(rd) >= self.NRING:
                ev.pre = rd[-self.NRING]
            rd.append(ev)
            if len(rd) > 2 * self.NRING:
                del rd[0]
        else:
            self.last_compute[eng] = ev
        ev.deps = [d for d in deps
                   if not (d.eng == "pe" and eng == "pe" and not d.dma and not dma)]
        for t in reads:
            t.rs[("dma", id(ev)) if dma else eng] = ev
        for t in writes:
            t.w = ev
            t.rs = {}
        self.q[eng].append(ev)
        return ev

    def barrier(self):
        deps = [e for e in self.last_compute.values() if e is not None]
        for eng in self.ENGS:
            deps.extend(self.recent_dma[eng][-self.NRING:])
        for eng in self.ENGS:
            ev = Ev(eng, None, False)
            ev.deps = list(deps)
            self.q[eng].append(ev)

    def finalize(self, sems):
        for eng in self.ENGS:
            for ev in self.q[eng]:
                for d in ev.deps:
                    d.needed = True
        for eng in self.ENGS:
            cnt = 0
            for ev in self.q[eng]:
                if ev.dma or ev.emit is None:
                    continue
                if ev.needed:
                    cnt += 1
                    ev.val = cnt
                    ev.sem = eng
        self.sems = sems

    def emit_engine(self, eng, handle):
        seen = {}
        sems = self.sems
        for ev in self.q[eng]:
            need = {}
            for d in ev.deps:
                if need.get(d.sem, 0) < d.val:
                    need[d.sem] = d.val
            if ev.pre is not None:
                d = ev.pre
                if need.get(d.sem, 0) < d.val:
                    need[d.sem] = d.val
            for s, v in need.items():
                if seen.get(s, 0) < v:
                    handle.wait_ge(sems[s], v)
                    seen[s] = v
            if ev.emit is None:
                continue
            inst = ev.emit(handle)
            if ev.dma:
                inst.then_inc(sems[ev.sem], 16)
            elif ev.needed:
                inst.then_inc(sems[ev.sem], 1)


class _Stop(Exception):
    pass


def build_nc(stop=None, dbg=None):
    nc = bass.Bass("TRN2", target_bir_lowering=False)
    DBG = {}

    def din(name, shape, dt=F32):
        return nc.dram_tensor(name, list(shape), dt, kind="ExternalInput").ap()

    x = din("x", [2, S, D])
    w_in = din("w_in", [D, 1728])
    wq = din("wq", [384, 768])
    wkv = din("wkv", [256, 1024])
    w_out = din("w_out", [D, D])
    w_up = din("w_up", [D, 2 * DFF])
    w_down = din("w_down", [DFF, D])
    cst_d = din("cst", [128, 384], BF16)
    cols_d = din("cols", [128, NCOLS])
    grows_d = din("grows", [3, 128, D])
    rope1_d = din("rope1", [2, 128, S])
    ropeg_d = din("ropeg", [128, 128])
    y = nc.dram_tensor("y", [2, S, D], F32, kind="ExternalOutput").ap()

    A = Arena(nc)
    P = Prog()
    PS2 = [nc.alloc_psum_tensor(f"ps{i}", [128, 1024], F32) for i in range(4)]
    T_ps = [Tile(excl=True) for _ in range(8)]

    def bank(b):
        return PS2[b // 2][:, (b % 2) * 512:(b % 2 + 1) * 512]

    rr = [0]

    def nb():
        b = rr[0] % 8
        rr[0] += 1
        return b

    cst = A.alloc("cst", [128, 384], BF16)
    cols = A.alloc("cols", [128, NCOLS], F32)
    ropeg = A.alloc("ropeg", [128, 128], F32)
    st = A.alloc("st", [128, 16], F32)
    T_cst, T_cols, T_ropeg = Tile(), Tile(), Tile()
    T_st = [Tile() for _ in range(4)]
    st2 = A.alloc("st2", [128, 48], F32)
    ident, ones, perm = cst[:, 0:128], cst[:, 128:256], cst[:, 256:384]
    P.op("sp", lambda e: e.dma_start(out=cst[:], in_=cst_d), writes=[T_cst], dma=True)
    P.op("sp", lambda e: e.dma_start(out=cols[:], in_=cols_d), writes=[T_cols], dma=True)
    P.op("sp", lambda e: e.dma_start(out=ropeg[:], in_=ropeg_d), writes=[T_ropeg], dma=True)

    def col(c):
        return cols[:, c:c + 1]

    stcnt = [0]

    def mm_group(out_ap, pairs, first=True, last=True):
        def f(e):
            n = len(pairs)
            inst = None
            for i, (l, r) in enumerate(pairs):
                inst = e.matmul(out_ap, lhsT=l, rhs=r, start=(first and i == 0), stop=(last and i == n - 1))
            return inst
        return f

    def rstd_cols(eng_src_ap, width, n_feat, Tsrc, Tdst_list, junk, T_junk):
        pass

    def make_uT(src_aps, src_tiles, grow, T_grow, uT, T_uT, tis, junk, T_junk, ubf, T_ubf, evac_eng, rstd=None):
        for idx, ti in enumerate(tis):
            src, Ts = src_aps[idx], src_tiles[idx]
            k = stcnt[0] % 4
            stcnt[0] += 1
            sl = (stcnt[0] - 1) % len(ubf)
            c0 = 4 * k
            if rstd is None:
                P.op("act", lambda e, src=src, c0=c0: e.activation(out=junk[:], in_=src, func=AF.Square,
                                                                     accum_out=st[:, c0:c0 + 1]),
                     reads=[Ts], writes=[T_junk, T_st[k]])
                P.op("act", lambda e, c0=c0: e.activation(out=st[:, c0 + 1:c0 + 2], in_=st[:, c0:c0 + 1], func=AF.Ln,
                                                          scale=1.0 / D, bias=EPS),
                     reads=[T_st[k]], writes=[T_st[k]])
                P.op("act", lambda e, c0=c0: e.activation(out=st[:, c0 + 2:c0 + 3], in_=st[:, c0 + 1:c0 + 2], func=AF.Exp,
                                                          scale=-0.5),
                     reads=[T_st[k]], writes=[T_st[k]])
                rs_ap, T_rs = st[:, c0 + 2:c0 + 3], T_st[k]
            else:
                rs_ap, T_rs = rstd[idx]
            P.op("dve", lambda e, src=src, rs_ap=rs_ap, sl=sl: e.scalar_tensor_tensor(
                out=ubf[sl][:], in0=src, scalar=rs_ap, in1=grow[:], op0=ALU.mult, op1=ALU.mult),
                reads=[Ts, T_rs, T_grow], writes=[T_ubf[sl]])
            b = nb()
            psT = bank(b).bitcast(BF16)

            def tr(e, sl=sl, psT=psT):
                inst = None
                for c in range(8):
                    inst = e.transpose(out=psT[:, c * 128:(c + 1) * 128], in_=ubf[sl][:, c * 128:(c + 1) * 128],
                                       identity=ident)
                return inst
            P.op("pe", tr, reads=[T_ubf[sl], T_cst], writes=[T_ps[b]])
            t0 = ti * 128
            if evac_eng == "act":
                P.op("act", lambda e, psT=psT, t0=t0: e.activation(
                    out=uT[:, :, t0:t0 + 128], in_=psT.rearrange("p (c t) -> p c t", c=8), func=AF.Copy),
                    reads=[T_ps[b]], writes=[T_uT[ti]])
            else:
                P.op("dve", lambda e, psT=psT, t0=t0: e.tensor_copy(
                    out=uT[:, :, t0:t0 + 128], in_=psT.rearrange("p (c t) -> p c t", c=8)),
                    reads=[T_ps[b]], writes=[T_uT[ti]])

    def bstats(sq_aps, T_sq, nfeat, lnbias, lnt, T_lnt, rdst, T_rdst):
        b = nb()
        P.op("pe", mm_group(bank(b), [(ones, a) for a in sq_aps]), reads=T_sq + [T_cst], writes=[T_ps[b]])
        P.op("act", lambda e, b=b: e.activation(out=lnt[:], in_=bank(b), func=AF.Ln, scale=1.0 / nfeat, bias=EPS),
             reads=[T_ps[b]], writes=[T_lnt])
        P.op("act", lambda e: e.activation(out=rdst[:], in_=lnt[:], func=AF.Exp, scale=-0.5, bias=float(lnbias)),
             reads=[T_lnt], writes=[T_rdst])

    def chk(s_, k_):
        if stop is not None and (s_, k_) == tuple(stop):
            raise _Stop()

    def seq(s):
        w_in_sb = A.alloc("w_in", [128, 8, 1792], BF16)
        uT = A.alloc("uT", [128, 8, S], BF16)
        T_win, T_uT = Tile(), [Tile() for _ in range(16)]
        w_in_v = w_in.rearrange("(c p) n -> p c n", p=128)
        P.op("pool", lambda e: e.dma_start(out=w_in_sb[:, :, 0:704], in_=w_in_v[:, :, 0:704]), writes=[T_win], dma=True)
        P.op("pool", lambda e: e.dma_start(out=w_in_sb[:, :, 704:768], in_=w_in_v[:, :, 640:704]), writes=[T_win], dma=True)
        P.op("pool", lambda e: e.dma_start(out=w_in_sb[:, :, 768:1792], in_=w_in_v[:, :, 704:1728]), writes=[T_win], dma=True)

        wq_sb = A.alloc("wq", [128, 3, 768], BF16)
        wkv_sb = A.alloc("wkv", [128, 2, 1024], BF16)
        rope1 = A.alloc("rope1", [128, 2, S], F32)
        grow = A.alloc("grow", [128, D], F32)
        T_wq, T_wkv, T_rope1, T_grow = Tile(), Tile(), Tile(), Tile()
        P.op("pool", lambda e: e.dma_start(out=wq_sb[:], in_=wq.rearrange("(c p) n -> p c n", p=128)), writes=[T_wq], dma=True)
        P.op("pool", lambda e: e.dma_start(out=wkv_sb[:], in_=wkv.rearrange("(c p) n -> p c n", p=128)), writes=[T_wkv], dma=True)
        P.op("sp", lambda e: e.dma_start(out=rope1[:], in_=rope1_d.rearrange("a p t -> p a t")), writes=[T_rope1], dma=True)
        P.op("sp", lambda e: e.dma_start(out=grow[:], in_=grows_d[0]), writes=[T_grow], dma=True)
        Qn = A.alloc("Qn", [128, 4, S], BF16)
        Qpe = A.alloc("Qpe", [128, 2, S], BF16)
        Kn = A.alloc("Kn", [128, 4, S], BF16)
        Kpl = A.alloc("Kpl", [128, S], BF16)
        Kph = A.alloc("Kph", [128, S], BF16)
        Vm = A.alloc("Vm", [128, 16, 512], BF16)
        T_Q = [Tile() for _ in range(NB)]
        DBG.update(uT=uT, Qn=Qn, Qpe=Qpe, Kn=Kn, Kpe=Kpl, Vm=Vm, w_in_sb=w_in_sb)
        T_K, T_V = Tile(), Tile()
        T_Kpe = Tile()
        P.op("pool", lambda e: e.memset(Kpl[64:128, :], 0.0), writes=[T_Kpe])
        P.op("pool", lambda e: e.memset(Kph[0:64, :], 0.0), writes=[T_Kpe])
        T_Qpe = [Tile() for _ in range(NB)]
        xt = [A.alloc(f"xt{i}", [128, D], F32) for i in range(2)]
        T_xt = [Tile(), Tile()]
        ubf = [A.alloc(f"ubf{i}", [128, D], BF16) for i in range(2)]
        T_ubf = [Tile(), Tile()]
        junk = A.alloc("junk", [128, D], BF16)
        T_junk = Tile()
        cqs = [A.alloc(f"cq{i}", [128, 3, 512], BF16) for i in range(2)]
        sqs = [A.alloc(f"sq{i}", [128, 3, 512], BF16) for i in range(2)]
        ckvs = [A.alloc(f"ckv{i}", [128, 2, 512], BF16) for i in range(2)]
        sqkvs = [A.alloc(f"sqkv{i}", [128, 2, 512], BF16) for i in range(2)]
        T_cqs, T_sqs, T_ckvs, T_sqkvs = ([Tile(), Tile()] for _ in range(4))
        lnt = A.alloc("lnt", [128, 512], F32)
        rq = A.alloc("rq", [128, 512], F32)
        rkv = A.alloc("rkv", [128, 512], F32)
        T_lnt, T_rq, T_rkv = Tile(), Tile(), Tile()
        rkc = A.alloc("rkc", [128, 8], F32)
        T_rkc = Tile()
        qpfs = [A.alloc(f"qpf{i}", [128, 512], F32) for i in range(2)]
        qpbs = [A.alloc(f"qpb{i}", [128, 512], BF16) for i in range(2)]
        t1s = [A.alloc("t10", [128, 512], F32)] * 2
        t2s = [A.alloc("t20", [128, 512], F32)] * 2
        T_qpfs, T_qpbs = ([Tile(), Tile()] for _ in range(2))
        T_t1s, T_t2s = [Tile()] * 2, [Tile()] * 2
        xcnt = [0]
        chn = [0]

        def front(n):
            blk = slice(n * 512, (n + 1) * 512)
            cq, sq, ckv, sqkv = cqs[n % 2], sqs[n % 2], ckvs[n % 2], sqkvs[n % 2]
            T_cq, T_sq, T_ckv, T_sqkv = T_cqs[n % 2], T_sqs[n % 2], T_ckvs[n % 2], T_sqkvs[n % 2]
            for i in range(4):
                ti = n * 4 + i
                sl = xcnt[0] % 2
                xcnt[0] += 1
                P.op("sp", lambda e, sl=sl, ti=ti: e.dma_start(out=xt[sl][:], in_=x[s, ti * 128:(ti + 1) * 128, :]),
                     writes=[T_xt[sl]], dma=True)
                make_uT([xt[sl][:]], [T_xt[sl]], grow, T_grow, uT, T_uT, [ti], junk, T_junk, ubf, T_ubf, "act")
            uT_blk = [uT[:, k, blk] for k in range(8)]
            T_uTb = [T_uT[n * 4 + i] for i in range(4)]
            for c in range(3):
                b = nb()
                P.op("pe", mm_group(bank(b), [(w_in_sb[:, k, c * 128:(c + 1) * 128], uT_blk[k]) for k in range(8)]),
                     reads=T_uTb + [T_win], writes=[T_ps[b]])
                P.op("dve", lambda e, b=b, c=c: e.tensor_scalar(out=cq[:, c, :], in0=bank(b), scalar1=col(C_GQ + c),
                                                                 scalar2=None, op0=ALU.mult),
                     reads=[T_ps[b], T_cols], writes=[T_cq])
                P.op("act", lambda e, b=b, c=c: e.activation(out=sq[:, c, :], in_=bank(b), func=AF.Square),
                     reads=[T_ps[b]], writes=[T_sq])
            for c in range(2):
                b = nb()
                P.op("pe", mm_group(bank(b), [(w_in_sb[:, k, 384 + c * 128:384 + (c + 1) * 128], uT_blk[k]) for k in range(8)]),
                     reads=T_uTb + [T_win], writes=[T_ps[b]])
                P.op("dve", lambda e, b=b, c=c: e.tensor_scalar(out=ckv[:, c, :], in0=bank(b), scalar1=col(C_GKV + c),
                                                                 scalar2=None, op0=ALU.mult),
                     reads=[T_ps[b], T_cols], writes=[T_ckv])
                P.op("act", lambda e, b=b, c=c: e.activation(out=sqkv[:, c, :], in_=bank(b), func=AF.Square),
                     reads=[T_ps[b]], writes=[T_sqkv])

        def back(n):
            blk = slice(n * 512, (n + 1) * 512)
            cq, sq, ckv, sqkv = cqs[n % 2], sqs[n % 2], ckvs[n % 2], sqkvs[n % 2]
            T_cq, T_sq, T_ckv, T_sqkv = T_cqs[n % 2], T_sqs[n % 2], T_ckvs[n % 2], T_sqkvs[n % 2]
            uT_blk = [uT[:, k, blk] for k in range(8)]
            T_uTb = [T_uT[n * 4 + i] for i in range(4)]
            bstats([sq[:, c, :] for c in range(3)], [T_sq], 384, np.log(192.0 ** -0.5), lnt, T_lnt, rq, T_rq)
            bstats([sqkv[:, c, :] for c in range(2)], [T_sqkv], 256, 0.0, lnt, T_lnt, rkv, T_rkv)
            b = nb()

            def colstats(e, b=b):
                inst = None
                for i in range(4):
                    for c in range(2):
                        inst = e.matmul(bank(b)[:, i:i + 1], lhsT=sqkv[:, c, i * 128:(i + 1) * 128], rhs=ones[:, 0:1],
                                        start=(c == 0), stop=(c == 1))
                return inst
            P.op("pe", colstats, reads=[T_sqkv, T_cst], writes=[T_ps[b]])
            P.op("act", lambda e, b=b: e.activation(out=rkc[:, 4:8], in_=bank(b)[:, 0:4], func=AF.Ln, scale=1.0 / 256, bias=EPS),
                 reads=[T_ps[b]], writes=[T_rkc])
            P.op("act", lambda e: e.activation(out=rkc[:, 0:4], in_=rkc[:, 4:8], func=AF.Exp, scale=-0.5),
                 reads=[T_rkc], writes=[T_rkc])
            w = chn[0] % 2
            chn[0] += 1
            qpb, t1, t2 = qpbs[w], t1s[w], t2s[w]
            T_qpb, T_t1, T_t2 = T_qpbs[w], T_t1s[w], T_t2s[w]
            bk = nb()
            P.op("pe", mm_group(bank(bk), [(w_in_sb[:, k, 640:768], uT_blk[k]) for k in range(8)]),
                 reads=T_uTb + [T_win], writes=[T_ps[bk]])
            P.op("act", lambda e, bk=bk: e.activation(out=qpb[:], in_=bank(bk), func=AF.Copy), reads=[T_ps[bk]], writes=[T_qpb])
            b2 = nb()
            P.op("pe", mm_group(bank(b2), [(perm, qpb[:])]), reads=[T_qpb, T_cst], writes=[T_ps[b2]])
            P.op("dve", lambda e, bk=bk: e.tensor_tensor(out=t1[:], in0=bank(bk), in1=rope1[:, 0, blk], op=ALU.mult),
                 reads=[T_ps[bk], T_rope1], writes=[T_t1])
            P.op("dve", lambda e, b2=b2: e.tensor_tensor(out=t2[:], in0=bank(b2), in1=rope1[:, 1, blk], op=ALU.mult),
                 reads=[T_ps[b2], T_rope1], writes=[T_t2])
            P.op("pool", lambda e: e.tensor_tensor(out=Kpl[0:64, blk], in0=t1[0:64, :], in1=t2[0:64, :], op=ALU.add),
                 reads=[T_t1, T_t2], writes=[T_Kpe])
            P.op("pool", lambda e: e.tensor_tensor(out=Kph[64:128, blk], in0=t1[64:128, :], in1=t2[64:128, :], op=ALU.add),
                 reads=[T_t1, T_t2], writes=[T_Kpe])
            for h in range(4):
                b = nb()
                P.op("pe", mm_group(bank(b), [(wq_sb[:, c, h * 192:h * 192 + 128], cq[:, c, :]) for c in range(3)]),
                     reads=[T_cq, T_wq], writes=[T_ps[b]])
                P.op("dve", lambda e, b=b, h=h: e.tensor_tensor(out=Qn[:, h, blk], in0=bank(b), in1=rq[:], op=ALU.mult),
                     reads=[T_ps[b], T_rq], writes=[T_Q[n]])
            for pr in range(2):
                w = chn[0] % 2
                chn[0] += 1
                b = nb()

                def qpe_mm(e, b=b, pr=pr):
                    inst = None
                    for half in range(2):
                        h = 2 * pr + half
                        for c in range(3):
                            inst = e.matmul(bank(b)[half * 64:(half + 1) * 64, :],
                                            lhsT=wq_sb[:, c, h * 192 + 128:h * 192 + 192], rhs=cq[:, c, :],
                                            start=(c == 0), stop=(c == 2))
                    return inst
                P.op("pe", qpe_mm, reads=[T_cq, T_wq], writes=[T_ps[b]])
                P.op("dve", lambda e, b=b, w=w: e.tensor_tensor(out=qpfs[w][:], in0=bank(b), in1=rq[:], op=ALU.mult),
                     reads=[T_ps[b], T_rq], writes=[T_qpfs[w]])
                P.op("act", lambda e, w=w: e.activation(out=qpbs[w][:], in_=qpfs[w][:], func=AF.Copy),
                     reads=[T_qpfs[w]], writes=[T_qpbs[w]])
                b2 = nb()
                P.op("pe", mm_group(bank(b2), [(perm, qpbs[w][:])]), reads=[T_qpbs[w], T_cst], writes=[T_ps[b2]])
                P.op("pool", lambda e, w=w: e.tensor_tensor(out=t1s[w][:], in0=qpfs[w][:], in1=rope1[:, 0, blk], op=ALU.mult),
                     reads=[T_qpfs[w], T_rope1], writes=[T_t1s[w]])
                P.op("dve", lambda e, b2=b2, w=w: e.tensor_tensor(out=t2s[w][:], in0=bank(b2), in1=rope1[:, 1, blk], op=ALU.mult),
                     reads=[T_ps[b2], T_rope1], writes=[T_t2s[w]])
                P.op("pool", lambda e, pr=pr, w=w: e.tensor_tensor(out=Qpe[:, pr, blk], in0=t1s[w][:], in1=t2s[w][:], op=ALU.add),
                     reads=[T_t1s[w], T_t2s[w]], writes=[T_Qpe[n]])
            for h in range(4):
                b = nb()
                P.op("pe", mm_group(bank(b), [(wkv_sb[:, c, h * 256:h * 256 + 128], ckv[:, c, :]) for c in range(2)]),
                     reads=[T_ckv, T_wkv], writes=[T_ps[b]])
                P.op("dve", lambda e, b=b, h=h: e.tensor_tensor(out=Kn[:, h, blk], in0=bank(b), in1=rkv[:], op=ALU.mult),
                     reads=[T_ps[b], T_rkv], writes=[T_K])
            for i in range(4):
                b = nb()
                vr = [wkv_sb[:, c, :].rearrange("p (h two d) -> p h two d", h=4, two=2)[:, :, 1, :] for c in range(2)]
                P.op("pe", mm_group(bank(b), [(ckv[:, c, i * 128:(i + 1) * 128], vr[c]) for c in range(2)]),
                     reads=[T_ckv, T_wkv], writes=[T_ps[b]])
                P.op("act", lambda e, b=b, i=i: e.activation(out=Vm[:, n * 4 + i, :], in_=bank(b), func=AF.Copy,
                                                             scale=rkc[:, i:i + 1]),
                     reads=[T_ps[b], T_rkc], writes=[T_V])

        P.capture()
        front(0)
        P.replay(P.end_capture())
        for n in range(NB):
            P.capture()
            back(n)
            bl = P.end_capture()
            fl = []
            if n + 1 < NB:
                P.capture()
                front(n + 1)
                fl = P.end_capture()
            P.replay(bl, fl)
        A.free("wq", "wkv", "rope1", "grow", "xt0", "xt1", "ubf0", "ubf1", "junk", "cq0", "cq1", "sq0", "sq1",
               "ckv0", "ckv1", "sqkv0", "sqkv1", "lnt", "rq", "rkv", "rkc", "qpf0", "qpf1", "qpb0", "qpb1",
               "t10", "t20")
        P.barrier()
        A.release()
        chk(s, 1)

        def attention(score_pairs, v_of, gg_col0, o_dst, T_o_dst, wfn, extra_reads, recip_dve):
            PT = [A.alloc(f"PT{i}", [128, 1024], BF16) for i in range(4)]
            T_PT = [Tile() for _ in range(4)]
            on = A.alloc("on", [128, 4, 512], F32)
            osq = A.alloc("osq", [128, 4, 512], BF16)
            rinv = A.alloc("rinv", [128, 512], F32)
            glt = A.alloc("glt", [128, 512], F32)
            grs = A.alloc("grs", [128, 512], F32)
            T_on = [Tile() for _ in range(4)]
            T_osq = [Tile() for _ in range(4)]
            T_rinv, T_glt, T_grs = Tile(), Tile(), Tile()
            TS = [A.alloc(f"TS{i}", [128, 512], BF16) for i in range(4)]
            T_TS = [Tile() for _ in range(4)]
            steps = []
            for n in range(NB):
                for h in range(4):
                    for k2 in range(8):
                        steps.append(("a", n, h, k2))
                        if wfn is not None and n >= 1 and h in (1, 2) and k2 in (2, 6):
                            steps.append(("w", n - 1, (h - 1) * 2 + k2 // 4))
            later = []

            def S_op(i):
                sl = i % 2
                if steps[i][0] == "w":
                    wfn["pe"](steps[i][1], steps[i][2], sl)
                    return
                _, n, h, k2 = steps[i]

                def f(e):
                    inst = None
                    for j in range(2):
                        prs = score_pairs(n, h, 2 * k2 + j)
                        for m, (l, r) in enumerate(prs):
                            inst = e.matmul(PS2[sl][:, j * 512:(j + 1) * 512], lhsT=l, rhs=r,
                                            start=(m == 0), stop=(m == len(prs) - 1))
                    return inst
                P.op("pe", f, reads=[T_Q[n], T_K] + extra_reads(n), writes=[T_ps[2 * sl], T_ps[2 * sl + 1]])

            def EXP_op(i):
                sl = i % 2
                if steps[i][0] == "w":
                    wfn["post"](steps[i][1], steps[i][2], sl)
                    later.append((i + 4, lambda: wfn["stats"](steps[i][1], steps[i][2])))
                    return
                P.op("act", lambda e: e.activation(out=PT[i % 4][:], in_=PS2[sl][:], func=AF.Exp),
                     reads=[T_ps[2 * sl], T_ps[2 * sl + 1]], writes=[T_PT[i % 4]])

            def PV_op(i):
                if steps[i][0] == "w":
                    return
                _, n, h, k2 = steps[i]
                par = h % 2
                ob = 4 + par
                P.op("dve", lambda e: e.tensor_tensor(out=TS[i % 4][:], in0=PT[i % 4][:, 0:512], in1=PT[i % 4][:, 512:1024],
                                                      op=ALU.add), reads=[T_PT[i % 4]], writes=[T_TS[i % 4]])

                def f(e):
                    inst = None
                    for j in range(2):
                        kc = 2 * k2 + j
                        inst = e.matmul(bank(ob), lhsT=v_of(h, kc), rhs=PT[i % 4][:, j * 512:(j + 1) * 512],
                                        start=(kc == 0), stop=(kc == 15))
                    return inst
                P.op("pe", f, reads=[T_PT[i % 4], T_V], writes=[T_ps[ob]])

            def SUM_op(i):
                if steps[i][0] == "w":
                    return
                _, n, h, k2 = steps[i]
                sb = 6 + h % 2
                P.op("pe", lambda e: e.matmul(bank(sb), lhsT=ones, rhs=TS[i % 4][:], start=(k2 == 0), stop=(k2 == 7)),
                     reads=[T_TS[i % 4], T_cst], writes=[T_ps[sb]])

            def finalize_head(i):
                _, n, h, k2 = steps[i]
                par = h % 2
                ob, sb = 4 + par, 6 + par
                def mult():
                    P.op("dve", lambda e: e.tensor_tensor(out=on[:, h, :], in0=bank(ob), in1=rinv[:], op=ALU.mult),
                         reads=[T_ps[ob], T_rinv], writes=[T_on[h]])
                if recip_dve:
                    def chunk(c):
                        P.op("dve", lambda e: e.reciprocal(out=rinv[:, c * 128:(c + 1) * 128], in_=bank(sb)[:, c * 128:(c + 1) * 128]),
                             reads=[T_ps[sb]], writes=[T_rinv])
                    chunk(0)
                    chunk(1)
                    later.append((i + 2, lambda: chunk(2)))
                    later.append((i + 2, lambda: chunk(3)))
                    later.append((i + 3, mult))
                    d0 = 3
                else:
                    P.op("act", lambda e: e.activation(out=glt[:], in_=bank(sb), func=AF.Ln), reads=[T_ps[sb]], writes=[T_glt])
                    P.op("act", lambda e: e.activation(out=rinv[:], in_=glt[:], func=AF.Exp, scale=-1.0), reads=[T_glt], writes=[T_rinv])
                    mult()
                    d0 = 1
                later.append((i + 3 + d0, lambda: P.op(
                    "act", lambda e: e.activation(out=osq[:, h, :], in_=on[:, h, :], func=AF.Square),
                    reads=[T_on[h]], writes=[T_osq[h]])))
                if h == 3:
                    def gss():
                        P.op("pe", mm_group(bank(7), [(ones, osq[:, hh, :]) for hh in range(4)]),
                             reads=T_osq + [T_cst], writes=[T_ps[7]])
                        P.op("act", lambda e: e.activation(out=glt[:], in_=bank(7), func=AF.Ln, scale=1.0 / 512, bias=EPS),
                             reads=[T_ps[7]], writes=[T_glt])
                        P.op("act", lambda e: e.activation(out=grs[:], in_=glt[:], func=AF.Exp, scale=-0.5),
                             reads=[T_glt], writes=[T_grs])
                        for hh in range(4):
                            P.op("dve", lambda e, hh=hh: e.scalar_tensor_tensor(
                                out=o_dst(n, hh), in0=on[:, hh, :], scalar=col(gg_col0 + hh), in1=grs[:],
                                op0=ALU.mult, op1=ALU.mult),
                                reads=[T_on[hh], T_cols, T_grs], writes=[T_o_dst(n)])
                    later.append((i + 4 + d0, gss))

            def run_due(i):
                due = [x for x in later if x[0] <= i]
                for x in due:
                    later.remove(x)
                    x[1]()

            S_op(0)
            for i in range(len(steps)):
                if i + 1 < len(steps):
                    S_op(i + 1)
                EXP_op(i)
                if i >= 2:
                    SUM_op(i - 2)
                PV_op(i)
                run_due(i)
                if steps[i][0] == "a" and steps[i][3] == 7:
                    later.append((i + 2, lambda i=i: finalize_head(i)))
            SUM_op(len(steps) - 2)
            SUM_op(len(steps) - 1)
            while later:
                later.sort(key=lambda x: x[0])
                x = later.pop(0)
                x[1]()
            if wfn is not None:
                for ii in range(4):
                    wfn["pe"](NB - 1, ii, ii % 2)
                    wfn["post"](NB - 1, ii, ii % 2)
                    wfn["stats"](NB - 1, ii)
            A.free("PT0", "PT1", "PT2", "on", "osq", "rinv", "glt", "grs", "TS0", "TS1", "TS2", "TS3", "PT3")

        o_mla = A.alloc("o_mla", [128, 4, S], BF16)
        T_omla = [Tile() for _ in range(NB)]
        DBG.update(o_mla=o_mla)

        def mla_pairs(n, h, kc):
            blk = slice(n * 512, (n + 1) * 512)
            ks = slice(kc * 128, (kc + 1) * 128)
            kp = Kpl if h % 2 == 0 else Kph
            return [(Kn[:, h, ks], Qn[:, h, blk]), (kp[:, ks], Qpe[:, h // 2, blk])]

        attention(mla_pairs, lambda h, kc: Vm[:, kc, h * 128:(h + 1) * 128], C_GGM,
                  lambda n, hh: o_mla[:, hh, n * 512:(n + 1) * 512], lambda n: T_omla[n], None,
                  lambda n: [T_Qpe[n], T_Kpe], False)
        A.free("Qn", "Qpe", "Kn", "Kpl", "Kph", "Vm")
        P.barrier()
        A.release()
        chk(s, 2)

        Qg = A.alloc("Qg", [128, 4, S], BF16)
        Kg = A.alloc("Kg", [128, 2, S], BF16)
        Vg = A.alloc("Vg", [128, 16, 256], BF16)
        DBG.update(Qg=Qg, Kg=Kg, Vg=Vg)
        T_Q = [Tile() for _ in range(NB)]
        T_K, T_V = Tile(), Tile()
        w_out_sb = A.alloc("w_out", [128, 8, D], BF16)
        T_wout = Tile()
        P.op("pool", lambda e: e.dma_start(out=w_out_sb[:], in_=w_out.rearrange("(c p) n -> p c n", p=128)),
             writes=[T_wout], dma=True)
        qg = [A.alloc(f"qg{i}", [128, 512], BF16) for i in range(2)]
        sqg = [A.alloc(f"sqg{i}", [128, 512], BF16) for i in range(2)]
        lng = [A.alloc(f"lng{i}", [128, 512], F32) for i in range(2)]
        rg = [A.alloc(f"rg{i}", [128, 512], F32) for i in range(2)]
        g1 = [A.alloc(f"g1{i}", [128, 512], F32) for i in range(2)]
        g2 = [A.alloc(f"g2{i}", [128, 512], F32) for i in range(2)]
        T_qg, T_sqg, T_lng, T_rg, T_g1, T_g2 = ([Tile(), Tile()] for _ in range(6))
        rstep = ropeg[:].ap[0][0]
        hcnt = 0
        items = []
        for n in range(NB):
            blk = slice(n * 512, (n + 1) * 512)
            uT_blk = [uT[:, k, blk] for k in range(8)]
            T_uTb = [T_uT[n * 4 + i] for i in range(4)]
            cc_top = bass.AP(ropeg, 0 * rstep + n * 8, [[rstep, 64], [1, 8], [0, 64]])
            ss_top = bass.AP(ropeg, 0 * rstep + 64 + n * 8, [[rstep, 64], [1, 8], [0, 64]])
            cc_bot = bass.AP(ropeg, 64 * rstep, [[rstep, 64], [0, 8], [1, 64]])
            ss_bot = bass.AP(ropeg, 64 * rstep + 64, [[rstep, 64], [0, 8], [1, 64]])

            def v3(ap):
                return ap.rearrange("p (a b) -> p a b", a=8)
            for hd in range(6):
                isq = hd < 4
                c0 = 768 + hd * 128 if isq else 1280 + (hd - 4) * 128
                gcol = C_GQQ if isq else C_GQK
                dst = Qg[:, hd, blk] if isq else Kg[:, hd - 4, blk]
                Tdst = T_Q[n] if isq else T_K
                lnb = np.log(128.0 ** -0.5) if isq else 0.0
                u = hcnt % 2
                hcnt += 1
                P.capture()
                b = nb()
                P.op("pe", mm_group(bank(b), [(w_in_sb[:, k, c0:c0 + 128], uT_blk[k]) for k in range(8)]),
                     reads=T_uTb + [T_win], writes=[T_ps[b]])
                P.op("act", lambda e, b=b, u=u, gcol=gcol: e.activation(out=qg[u][:], in_=bank(b), func=AF.Copy, scale=col(gcol)),
                     reads=[T_ps[b], T_cols], writes=[T_qg[u]])
                P.op("act", lambda e, b=b, u=u: e.activation(out=sqg[u][:], in_=bank(b), func=AF.Square),
                     reads=[T_ps[b]], writes=[T_sqg[u]])
                P.op("dve", lambda e, b=b, u=u, gcol=gcol, cc_top=cc_top: e.scalar_tensor_tensor(
                    out=v3(g1[u][0:64, :]), in0=v3(bank(b)[0:64, :]), scalar=cols[0:64, gcol:gcol + 1], in1=cc_top,
                    op0=ALU.mult, op1=ALU.mult), reads=[T_ps[b], T_cols, T_ropeg], writes=[T_g1[u]])
                P.op("dve", lambda e, b=b, u=u, gcol=gcol, cc_bot=cc_bot: e.scalar_tensor_tensor(
                    out=v3(g1[u][64:128, :]), in0=v3(bank(b)[64:128, :]), scalar=cols[64:128, gcol:gcol + 1], in1=cc_bot,
                    op0=ALU.mult, op1=ALU.mult), reads=[T_ps[b], T_cols, T_ropeg], writes=[T_g1[u]])
                bs = nb()
                P.op("pe", mm_group(bank(bs), [(ones, sqg[u][:])]), reads=[T_sqg[u], T_cst], writes=[T_ps[bs]])
                bw = nb()
                P.op("pe", mm_group(bank(bw), [(perm, qg[u][:])]), reads=[T_qg[u], T_cst], writes=[T_ps[bw]])
                P.op("act", lambda e, bs=bs, u=u: e.activation(out=lng[u][:], in_=bank(bs), func=AF.Ln, scale=1.0 / 128, bias=EPS),
                     reads=[T_ps[bs]], writes=[T_lng[u]])
                P.op("act", lambda e, u=u, lnb=lnb: e.activation(out=rg[u][:], in_=lng[u][:], func=AF.Exp, scale=-0.5, bias=float(lnb)),
                     reads=[T_lng[u]], writes=[T_rg[u]])
                P.op("dve", lambda e, bw=bw, u=u, ss_top=ss_top: e.tensor_tensor(
                    out=v3(g2[u][0:64, :]), in0=v3(bank(bw)[0:64, :]), in1=ss_top, op=ALU.mult),
                    reads=[T_ps[bw], T_ropeg], writes=[T_g2[u]])
                P.op("dve", lambda e, bw=bw, u=u, ss_bot=ss_bot: e.tensor_tensor(
                    out=v3(g2[u][64:128, :]), in0=v3(bank(bw)[64:128, :]), in1=ss_bot, op=ALU.mult),
                    reads=[T_ps[bw], T_ropeg], writes=[T_g2[u]])
                P.op("pool", lambda e, u=u: e.tensor_tensor(out=g1[u][:], in0=g1[u][:], in1=g2[u][:], op=ALU.add),
                     reads=[T_g2[u]], writes=[T_g1[u]])
                P.op("pool", lambda e, u=u, dst=dst: e.tensor_tensor(out=dst, in0=g1[u][:], in1=rg[u][:], op=ALU.mult),
                     reads=[T_g1[u], T_rg[u]], writes=[Tdst])
                L = P.end_capture()
                items.append((L[:5], L[5:]))
            for i in range(4):
                P.capture()
                b = nb()
                t0 = (n * 4 + i) * 128
                P.op("pe", mm_group(bank(b)[:, 0:256], [(uT[:, k, t0:t0 + 128], w_in_sb[:, k, 1536:1792]) for k in range(8)]),
                     reads=[T_uT[n * 4 + i], T_win], writes=[T_ps[b]])
                P.op("act", lambda e, b=b, i=i, n=n: e.activation(out=Vg[:, n * 4 + i, :], in_=bank(b)[:, 0:256], func=AF.Copy),
                     reads=[T_ps[b]], writes=[T_V])
                items[-1 - i % 2] = (items[-1 - i % 2][0], items[-1 - i % 2][1] + P.end_capture())
        for i in range(len(items)):
            P.replay(items[i][0])
            P.replay(items[i][1])
        A.free("w_in", "uT", "qg0", "qg1", "sqg0", "sqg1", "lng0", "lng1", "rg0", "rg1", "g10", "g11", "g20", "g21")
        P.barrier()
        A.release()
        chk(s, 3)

        hres = A.alloc("h", [128, 16, D], F32)
        T_h = [Tile() for _ in range(16)]
        DBG.update(h=hres)
        og = A.alloc("og", [128, 4, 512], BF16)
        T_og = Tile()
        xr = [A.alloc(f"xr{i}", [128, D], F32) for i in range(4)]
        T_xr = [Tile() for _ in range(4)]
        junk2 = A.alloc("junk2", [128, D], BF16)
        T_junk2 = Tile()
        T_ss2 = [Tile() for _ in range(16)]
        T_r2 = Tile()

        def w_pe(n, i, sl):
            ti = n * 4 + i
            t0 = ti * 128
            P.op("sp", lambda e: e.dma_start(out=xr[i][:], in_=x[s, t0:t0 + 128, :]), writes=[T_xr[i]], dma=True)

            def f(e):
                inst = None
                for fh in range(2):
                    pairs = [(o_mla[:, c, t0:t0 + 128], w_out_sb[:, c, fh * 512:(fh + 1) * 512]) for c in range(4)]
                    pairs += [(og[:, c, i * 128:(i + 1) * 128], w_out_sb[:, 4 + c, fh * 512:(fh + 1) * 512]) for c in range(4)]
                    for m, (l, r) in enumerate(pairs):
                        inst = e.matmul(PS2[sl][:, fh * 512:(fh + 1) * 512], lhsT=l, rhs=r, start=(m == 0), stop=(m == 7))
                return inst
            P.op("pe", f, reads=[T_omla[n], T_og, T_wout], writes=[T_ps[2 * sl], T_ps[2 * sl + 1]])

        def w_post(n, i, sl):
            ti = n * 4 + i
            P.op("dve", lambda e: e.tensor_tensor(out=hres[:, ti, :], in0=PS2[sl][:], in1=xr[i][:], op=ALU.add),
                 reads=[T_ps[2 * sl], T_ps[2 * sl + 1], T_xr[i]], writes=[T_h[ti]])

        def w_stats(n, i):
            ti = n * 4 + i
            P.op("act", lambda e: e.activation(out=junk2[:], in_=hres[:, ti, :], func=AF.Square, accum_out=st2[:, ti:ti + 1]),
                 reads=[T_h[ti]], writes=[T_junk2, T_ss2[ti]])

        def gqa_pairs(n, h, kc):
            return [(Kg[:, h // 2, kc * 128:(kc + 1) * 128], Qg[:, h, n * 512:(n + 1) * 512])]

        attention(gqa_pairs, lambda h, kc: Vg[:, kc, (h // 2) * 128:(h // 2 + 1) * 128], C_GGG,
                  lambda n, hh: og[:, hh, :], lambda n: T_og, {"pe": w_pe, "post": w_post, "stats": w_stats}, lambda n: [], False)
        A.free("Qg", "Kg", "Vg", "w_out", "o_mla", "og", "xr0", "xr1", "xr2", "xr3", "junk2")
        P.barrier()
        A.release()
        chk(s, 4)

        u2T = A.alloc("u2T", [128, 8, S], BF16)
        T_u2T = [Tile() for _ in range(16)]
        DBG.update(u2T=u2T)
        DBGP3 = True
        grow2 = A.alloc("grow2", [128, D], F32)
        T_grow2 = Tile()
        P.op("sp", lambda e: e.dma_start(out=grow2[:], in_=grows_d[1]), writes=[T_grow2], dma=True)
        ubf = [A.alloc(f"ubf{i}", [128, D], BF16) for i in range(2)]
        T_ubf = [Tile() for _ in range(2)]
        junk = A.alloc("junk", [128, D], BF16)
        T_junk = Tile()
        NWU = 3
        wg = [A.alloc(f"wg{i}", [128, 8, 128], BF16) for i in range(NWU)]
        wv = [A.alloc(f"wv{i}", [128, 8, 128], BF16) for i in range(NWU)]
        T_wu = [Tile() for _ in range(NWU)]
        NWD = 8
        wd = [A.alloc(f"wd{i}", [128, D], BF16) for i in range(NWD)]
        T_wd = [Tile() for _ in range(NWD)]
        actT = [A.alloc(f"actT{i}", [128, S], BF16) for i in range(6)]
        T_actT = [Tile() for _ in range(6)]
        hug = [A.alloc(f"hug{i}", [128, S + 2], BF16) for i in range(2)]
        huv = [A.alloc(f"huv{i}", [128, S + 2], F32) for i in range(2)]
        cvs = [A.alloc(f"cvs{i}", [128, 512], F32) for i in range(2)]
        T_cv = [Tile(), Tile()]
        T_hu = [[Tile() for _ in range(NB)] for _ in range(2)]
        T_hv = [[Tile() for _ in range(NB)] for _ in range(2)]
        dg = [A.alloc(f"dg{i}", [128, 3, 128], BF16) for i in range(2)]
        T_dg = [Tile(), Tile()]
        sg = [A.alloc(f"sg{i}", [128, 512], BF16) for i in range(2)]
        T_sg = [Tile(), Tile()]
        DBG.update(actT0=actT[0], actT4=actT[4], hug0=hug[0], hug1=hug[1], huv0=huv[0], huv1=huv[1])
        for u in range(2):
            for buf, TT in ((hug[u], T_hu[u]), (huv[u], T_hv[u])):
                P.op("pool", lambda e, buf=buf: e.memset(buf[:, 0:1], 0.0), writes=[TT[0]])
                P.op("pool", lambda e, buf=buf: e.memset(buf[:, S + 1:S + 2], 0.0), writes=[TT[NB - 1]])
        w_up_v = w_up.rearrange("(c p) n -> p c n", p=128)
        jlist = []
        for gi, gsz in enumerate(GROUPS):
            j0 = sum(GROUPS[:gi])
            jlist.append(list(range(j0, j0 + gsz)))

        def load_w(j):
            su, sd = j % NWU, j % NWD
            P.op("pool", lambda e: e.dma_start(out=wg[su][:], in_=w_up_v[:, :, j * 128:(j + 1) * 128]), writes=[T_wu[su]], dma=True)
            P.op("pool", lambda e: e.dma_start(out=wv[su][:], in_=w_up_v[:, :, DFF + j * 128:DFF + (j + 1) * 128]),
                 writes=[T_wu[su]], dma=True)
            P.op("pool", lambda e: e.dma_start(out=wd[sd][:], in_=w_down[j * 128:(j + 1) * 128, :]), writes=[T_wd[sd]], dma=True)

        def dg_build(j):
            u = j % 2
            for k in range(3):
                P.op("act", lambda e, k=k: e.activation(out=dg[u][:, k, :], in_=ident, func=AF.Copy, scale=col(C_CW + k * 44 + j)),
                     reads=[T_cst, T_cols], writes=[T_dg[u]])

        def hu_block(j, n):
            su, u = j % NWU, j % 2
            blk = slice(n * 512, (n + 1) * 512)
            rd = [T_u2T[n * 4 + i] for i in range(4)] + [T_wu[su]]
            b = n % 2
            P.op("pe", mm_group(bank(b), [(wg[su][:, k, :], u2T[:, k, blk]) for k in range(8)]), reads=rd, writes=[T_ps[b]])
            P.op("act", lambda e, b=b: e.activation(out=hug[u][:, 1 + n * 512:1 + (n + 1) * 512], in_=bank(b), func=AF.Copy),
                 reads=[T_ps[b]], writes=[T_hu[u][n]])
            b = (2, 3, 6, 7)[n % 4]
            P.op("pe", mm_group(bank(b), [(wv[su][:, k, :], u2T[:, k, blk]) for k in range(8)]), reads=rd, writes=[T_ps[b]])
            P.op("act", lambda e, b=b: e.activation(out=huv[u][:, 1 + n * 512:1 + (n + 1) * 512], in_=bank(b), func=AF.Copy),
                 reads=[T_ps[b]], writes=[T_hv[u][n]])

        def hu_ops(j):
            dg_build(j)
            for n in range(NB):
                hu_block(j, n)

        def conv_ops(j, jj):
            u = j % 2
            cw0, cw1, cw2, cbv = (col(C_CW + k * 44 + 22 + j) for k in range(3)), None, None, col(C_CB + 22 + j)
            cw0, cw1, cw2 = col(C_CW + 22 + j), col(C_CW + 44 + 22 + j), col(C_CW + 88 + 22 + j)
            for n in range(NB):
                nbrs = [T_hu[u][m] for m in (n - 1, n, n + 1) if 0 <= m < NB]
                nbrv = [T_hv[u][m] for m in (n - 1, n, n + 1) if 0 <= m < NB]
                bg = 4 + n % 2
                v = n % 2
                o0 = n * 512
                P.op("pe", mm_group(bank(bg), [(dg[u][:, k, :], hug[u][:, o0 + k:o0 + k + 512]) for k in range(3)]),
                     reads=nbrs + [T_dg[u]], writes=[T_ps[bg]])
                P.op("dve", lambda e, v=v, o0=o0: e.tensor_scalar(out=cvs[v][:], in0=huv[u][:, o0 + 1:o0 + 513], scalar1=cw1,
                                                                 scalar2=cbv, op0=ALU.mult, op1=ALU.add),
                     reads=nbrv + [T_cols], writes=[T_cv[v]])
                P.op("dve", lambda e, v=v, o0=o0: e.scalar_tensor_tensor(out=cvs[v][:], in0=huv[u][:, o0:o0 + 512], scalar=cw0,
                                                                        in1=cvs[v][:], op0=ALU.mult, op1=ALU.add),
                     reads=nbrv + [T_cols], writes=[T_cv[v]])
                P.op("dve", lambda e, v=v, o0=o0: e.scalar_tensor_tensor(out=cvs[v][:], in0=huv[u][:, o0 + 2:o0 + 514], scalar=cw2,
                                                                        in1=cvs[v][:], op0=ALU.mult, op1=ALU.add),
                     reads=nbrv + [T_cols], writes=[T_cv[v]])
                P.op("act", lambda e, bg=bg, v=v: e.activation(out=sg[v][:], in_=bank(bg), func=AF.Silu, bias=col(C_CB + j)),
                     reads=[T_ps[bg], T_cols], writes=[T_sg[v]])
                P.op("dve", lambda e, v=v, o0=o0: e.tensor_tensor(out=actT[jj][:, o0:o0 + 512], in0=cvs[v][:], in1=sg[v][:], op=ALU.mult),
                     reads=[T_cv[v], T_sg[v]], writes=[T_actT[jj]])

        yt = None
        allj = [j for g in jlist for j in g]
        load_w(0)
        load_w(1)
        dg_build(0)
        dg_build(1)
        P.op("act", lambda e: e.activation(out=st2[:, 16:32], in_=st2[:, 0:16], func=AF.Ln, scale=1.0 / D, bias=EPS),
             reads=T_ss2, writes=[T_r2])
        P.op("act", lambda e: e.activation(out=st2[:, 32:48], in_=st2[:, 16:32], func=AF.Exp, scale=-0.5),
             reads=[T_r2], writes=[T_r2])
        for n in range(NB):
            for i in range(4):
                ti = n * 4 + i
                make_uT([hres[:, ti, :]], [T_h[ti]], grow2, T_grow2, u2T, T_u2T, [ti], junk, T_junk, ubf, T_ubf,
                        "act", rstd=[(st2[:, 32 + ti:33 + ti], T_r2)])
            hu_block(0, n)
            hu_block(1, n)
        def down_ops(gi):
            js = jlist[gi]
            lastg = gi == len(jlist) - 1
            if lastg:
                growf = A.alloc("growf", [128, D], F32)
                T_growf = Tile()
                P.op("sp", lambda e: e.dma_start(out=growf[:], in_=grows_d[2]), writes=[T_growf], dma=True)
                yt = [A.alloc(f"yt{i}", [128, D], F32) for i in range(2)]
                T_yt = [Tile(), Tile()]
            fin = []
            for ti in range(16):
                t0 = ti * 128
                for fh in range(2):
                    b = (2 * ti + fh) % 4
                    P.op("pe", mm_group(bank(b), [(actT[jj][:, t0:t0 + 128], wd[j % NWD][:, fh * 512:(fh + 1) * 512])
                                                  for jj, j in enumerate(js)]),
                         reads=[T_actT[jj] for jj in range(len(js))] + [T_wd[j % NWD] for j in js], writes=[T_ps[b]])
                    P.op("dve", lambda e, b=b, ti=ti, fh=fh: e.tensor_tensor(
                        out=hres[:, ti, fh * 512:(fh + 1) * 512], in0=bank(b), in1=hres[:, ti, fh * 512:(fh + 1) * 512], op=ALU.add),
                        reads=[T_ps[b], T_h[ti]], writes=[T_h[ti]])
                if lastg:
                    k = stcnt[0] % 4
                    stcnt[0] += 1
                    c0 = 4 * k
                    sl = ti % 2
                    P.op("act", lambda e, ti=ti, c0=c0: e.activation(out=junk[:], in_=hres[:, ti, :], func=AF.Square,
                                                                    accum_out=st[:, c0:c0 + 1]),
                         reads=[T_h[ti]], writes=[T_junk, T_st[k]])
                    P.op("act", lambda e, c0=c0: e.activation(out=st[:, c0 + 1:c0 + 2], in_=st[:, c0:c0 + 1], func=AF.Ln,
                                                              scale=1.0 / D, bias=EPS), reads=[T_st[k]], writes=[T_st[k]])
                    P.op("act", lambda e, c0=c0: e.activation(out=st[:, c0 + 2:c0 + 3], in_=st[:, c0 + 1:c0 + 2], func=AF.Exp,
                                                              scale=-0.5), reads=[T_st[k]], writes=[T_st[k]])
                    def finish(ti=ti, c0=c0, sl=sl, k=k, t0=t0):
                        P.op("dve", lambda e: e.scalar_tensor_tensor(
                            out=yt[sl][:], in0=hres[:, ti, :], scalar=st[:, c0 + 2:c0 + 3], in1=growf[:], op0=ALU.mult, op1=ALU.mult),
                            reads=[T_h[ti], T_st[k], T_growf], writes=[T_yt[sl]])
                        P.op("sp", lambda e: e.dma_start(out=y[s, t0:t0 + 128, :], in_=yt[sl][:]),
                             reads=[T_yt[sl]], dma=True)
                    fin.append(finish)
                    if len(fin) >= 2:
                        fin.pop(0)()
            for f_ in fin:
                f_()

        flat = [(gi, jj, j) for gi, js in enumerate(jlist) for jj, j in enumerate(js)]
        for idx, (gi, jj, j) in enumerate(flat):
            if j >= 2:
                hu_ops(j)
            if idx > 0:
                pgi, pjj, pj = flat[idx - 1]
                conv_ops(pj, pjj)
                if pjj == len(jlist[pgi]) - 1:
                    down_ops(pgi)
            if j + 2 < NJ:
                load_w(j + 2)
        pgi, pjj, pj = flat[-1]
        conv_ops(pj, pjj)
        down_ops(pgi)
        names = ["u2T", "grow2", "growf", "ubf0", "ubf1", "cvs0", "cvs1", "junk", "h", "yt0", "yt1"]
        names += [f"wg{i}" for i in range(NWU)] + [f"wv{i}" for i in range(NWU)] + [f"wd{i}" for i in range(NWD)]
        names += [f"actT{i}" for i in range(6)] + ["hug0", "hug1", "huv0", "huv1", "dg0", "dg1", "sg0", "sg1"]
        A.free(*names)
        P.barrier()
        A.release()
        chk(s, 5)

    try:
        for s_ in range(2):
            seq(s_)
    except _Stop:
        P.barrier()
        for name in (dbg or []):
            t = DBG[name]
            shp = list(t[:].shape)
            dd = nc.dram_tensor("dbg_" + name, shp, t[:].dtype, kind="ExternalOutput").ap()
            P.op("sp", lambda e, t=t, dd=dd: e.dma_start(out=dd, in_=t[:]), dma=True)
        P.barrier()

    semnames = list(Prog.ENGS) + [(e, i) for e in Prog.ENGS for i in range(Prog.NRING)]
    with contextlib.ExitStack() as es:
        sems = {k: es.enter_context(nc.semaphore("s_" + str(k).replace(" ", "").replace("'", "").replace("(", "")
                                                 .replace(")", "").replace(",", "_"))) for k in semnames}
        P.finalize(sems)
        block = es.enter_context(nc.Block())
        block.tensor(lambda e: P.emit_engine("pe", e))
        block.scalar(lambda e: P.emit_engine("act", e))
        block.vector(lambda e: P.emit_engine("dve", e))
        block.gpsimd(lambda e: P.emit_engine("pool", e))
        block.sync(lambda e: P.emit_engine("sp", e))
    return nc


def _rope_tables():
    inv = (10000.0 ** (-np.arange(0, 64, 2, dtype=np.float32) / 64.0)).astype(np.float32)
    p = np.arange(128)
    f = p % 32
    sign = np.where((p % 64) < 32, -1.0, 1.0).astype(np.float32)
    t = np.arange(S, dtype=np.float32)
    ang1 = (t[None, :] * inv[f][:, None]).astype(np.float32)
    rope1 = np.stack([np.cos(ang1), sign[:, None] * np.sin(ang1)], 0).astype(np.float32)
    pos = np.arange(64, dtype=np.float32)
    angg = (pos[None, :] * inv[f][:, None]).astype(np.float32)
    ropeg = np.concatenate([np.cos(angg), sign[:, None] * np.sin(angg)], 1).astype(np.float32)
    return rope1, ropeg


def _consts():
    ident = np.eye(128, dtype=np.float32)
    ones = np.ones((128, 128), np.float32)
    perm = np.zeros((128, 128), np.float32)
    for m in range(128):
        k = m + 32 if (m % 64) < 32 else m - 32
        perm[k, m] = 1.0
    return np.concatenate([ident, ones, perm], 1).astype(ml_dtypes.bfloat16)


_NC_CACHE = {}


def kernel(x, norm1_g, w_in, mla_q_norm_g, mla_w_q_up, mla_kv_norm_g, mla_w_kv_up,
           gqa_q_norm_g, gqa_k_norm_g, group_norm_mla_g, group_norm_gqa_g, w_out,
           norm2_g, ffn_w_up, ffn_conv_w, ffn_conv_b, ffn_w_down, final_norm_g):
    f32 = np.float32
    x = np.asarray(x, f32)

    def colify(v, n):
        return np.asarray(v, f32).reshape(n, 128).T

    cols = np.concatenate([
        colify(mla_q_norm_g[0], 3), colify(mla_kv_norm_g[0], 2),
        colify(gqa_q_norm_g[0], 1), colify(gqa_k_norm_g[0], 1),
        colify(group_norm_mla_g[0], 4), colify(group_norm_gqa_g[0], 4),
        colify(ffn_conv_w[0, 0], 44), colify(ffn_conv_w[0, 1], 44), colify(ffn_conv_w[0, 2], 44),
        colify(ffn_conv_b[0], 44)], 1)
    cols = np.ascontiguousarray(cols, f32)
    assert cols.shape == (128, NCOLS)
    grows = np.ascontiguousarray(np.stack([
        np.broadcast_to(np.asarray(norm1_g[0], f32), (128, D)),
        np.broadcast_to(np.asarray(norm2_g[0], f32), (128, D)),
        np.broadcast_to(np.asarray(final_norm_g, f32), (128, D))], 0))
    rope1, ropeg = _rope_tables()
    cst = _consts()
    if "nc" not in _NC_CACHE:
        _NC_CACHE["nc"] = build_nc()
    nc = _NC_CACHE["nc"]
    shared = {
        "w_in": np.ascontiguousarray(w_in[0], f32), "wq": np.ascontiguousarray(mla_w_q_up[0], f32),
        "wkv": np.ascontiguousarray(mla_w_kv_up[0], f32), "w_out": np.ascontiguousarray(w_out[0], f32),
        "w_up": np.ascontiguousarray(ffn_w_up[0], f32), "w_down": np.ascontiguousarray(ffn_w_down[0], f32),
        "cst": cst, "cols": cols, "grows": grows, "rope1": rope1, "ropeg": ropeg,
    }
    in_maps = []
    for c in range(8):
        m = dict(shared)
        m["x"] = np.ascontiguousarray(x[2 * c:2 * c + 2])
        in_maps.append(m)
    res = run_bass_kernel_spmd(nc, in_maps, core_ids=list(range(8)))
    out = np.concatenate([np.asarray(r["y"], f32) for r in res.results], 0)
    return out
```

```python
import contextlib
import numpy as np
import ml_dtypes
import concourse.bass as bass
import concourse.mybir as mybir
from concourse.bass_utils import run_bass_kernel_spmd

F32 = mybir.dt.float32
BF16 = mybir.dt.bfloat16
AF = mybir.ActivationFunctionType
ALU = mybir.AluOpType

SB_LO = 16512
SB_HI = 229344
EPS = 1e-6
S = 2048
D = 1024
NB = 4
DFF = 2816
NJ = 22
GROUPS = [6, 6, 6, 4]

C_GQ, C_GKV, C_GQQ, C_GQK, C_GGM, C_GGG, C_CW, C_CB, NCOLS = 0, 3, 5, 6, 7, 11, 15, 147, 191


class Tile:
    __slots__ = ("w", "rs", "excl")

    def __init__(self, excl=False):
        self.w = None
        self.rs = {}
        self.excl = excl


class Ev:
    __slots__ = ("eng", "emit", "deps", "val", "sem", "dma", "needed", "pre")

    def __init__(self, eng, emit, dma):
        self.eng = eng
        self.emit = emit
        self.dma = dma
        self.deps = ()
        self.val = None
        self.sem = None
        self.needed = False
        self.pre = None


class Arena:
    def __init__(self, nc):
        self.nc = nc
        self.live = {}
        self.pending = []
        self.n = 0

    def alloc(self, name, shape, dtype):
        esz = 2 if dtype == BF16 else 4
        free = 1
        for d in shape[1:]:
            free *= d
        nbytes = ((free * esz + 63) // 64) * 64
        pos = SB_LO
        for lo, hi in sorted(self.live.values()):
            if pos + nbytes <= lo:
                break
            pos = max(pos, hi)
        if pos + nbytes > SB_HI:
            raise RuntimeError(f"SBUF arena full allocating {name} ({nbytes}B), used={self.used()}")
        assert name not in self.live, name
        self.live[name] = (pos, pos + nbytes)
        self.n += 1
        return self.nc.alloc_sbuf_tensor_at(f"{name}_{self.n}", list(shape), dtype, offset=pos)

    def free(self, *names):
        self.pending.extend(names)

    def release(self):
        for n in self.pending:
            del self.live[n]
        self.pending = []

    def used(self):
        return sum(hi - lo for lo, hi in self.live.values())


class Prog:
    ENGS = ("pe", "act", "dve", "pool", "sp")
    NRING = 8

    def __init__(self):
        self.q = {e: [] for e in self.ENGS}
        self.ndma = {e: 0 for e in self.ENGS}
        self.last_compute = {e: None for e in self.ENGS}
        self.recent_dma = {e: [] for e in self.ENGS}
        self.cap = None

    def capture(self):
        self.cap = []

    def end_capture(self):
        l, self.cap = self.cap, None
        return l

    def replay(self, *lists):
        idx = [0] * len(lists)
        total = sum(len(l) for l in lists)
        for _ in range(total):
            best, bk = None, None
            for k, l in enumerate(lists):
                if idx[k] < len(l):
                    frac = idx[k] / len(l)
                    if best is None or frac < best:
                        best, bk = frac, k
            a = lists[bk][idx[bk]]
            idx[bk] += 1
            self.op(*a)

    def op(self, eng, emit, reads=(), writes=(), dma=False):
        if self.cap is not None:
            self.cap.append((eng, emit, list(reads), list(writes), dma))
            return None
        ev = Ev(eng, emit, dma)
        deps = set()
        xr = [t for t in reads if t.excl]
        if xr:
            reads = [t for t in reads if not t.excl]
            writes = list(writes) + xr
        for t in reads:
            if t.w is not None:
                deps.add(t.w)
        for t in writes:
            if t.w is not None:
                deps.add(t.w)
            deps.update(t.rs.values())
        if dma:
            i = self.ndma[eng]
            self.ndma[eng] = i + 1
            ev.sem = (eng, i % self.NRING)
            ev.val = 16 * (i // self.NRING + 1)
            rd = self.recent_dma[eng]
            if len(rd) >= self.NRING:
                ev.pre = rd[-self.NRING]
            rd.append(ev)
            if len(rd) > 2 * self.NRING:
                del rd[0]
        else:
            self.last_compute[eng] = ev
        ev.deps = [d for d in deps
                   if not (d.eng == "pe" and eng == "pe" and not d.dma and not dma)]
        for t in reads:
            t.rs[("dma", id(ev)) if dma else eng] = ev
        for t in writes:
            t.w = ev
            t.rs = {}
        self.q[eng].append(ev)
        return ev

    def barrier(self):
        deps = [e for e in self.last_compute.values() if e is not None]
        for eng in self.ENGS:
            deps.extend(self.recent_dma[eng][-self.NRING:])
        for eng in self.ENGS:
            ev = Ev(eng, None, False)
            ev.deps = list(deps)
            self.q[eng].append(ev)

    def finalize(self, sems):
        for eng in self.ENGS:
            for ev in self.q[eng]:
                for d in ev.deps:
                    d.needed = True
        for eng in self.ENGS:
            cnt = 0
            for ev in self.q[eng]:
                if ev.dma or ev.emit is None:
                    continue
                if ev.needed:
                    cnt += 1
                    ev.val = cnt
                    ev.sem = eng
        self.sems = sems

    def emit_engine(self, eng, handle):
        seen = {}
        sems = self.sems
        for ev in self.q[eng]:
            need = {}
            for d in ev.deps:
                if need.get(d.sem, 0) < d.val:
                    need[d.sem] = d.val
            if ev.pre is not None:
                d = ev.pre
                if need.get(d.sem, 0) < d.val:
                    need[d.sem] = d.val
            for s, v in need.items():
                if seen.get(s, 0) < v:
                    handle.wait_ge(sems[s], v)
                    seen[s] = v
            if ev.emit is None:
                continue
            inst = ev.emit(handle)
            if ev.dma:
                inst.then_inc(sems[ev.sem], 16)
            elif ev.needed:
                inst.then_inc(sems[ev.sem], 1)


class _Stop(Exception):
    pass


def build_nc(stop=None, dbg=None):
    nc = bass.Bass("TRN2", target_bir_lowering=False)
    DBG = {}

    def din(name, shape, dt=F32):
        return nc.dram_tensor(name, list(shape), dt, kind="ExternalInput").ap()

    x = din("x", [2, S, D])
    w_in = din("w_in", [D, 1728])
    wq = din("wq", [384, 768])
    wkv = din("wkv", [256, 1024])
    w_out = din("w_out", [D, D])
    w_up = din("w_up", [D, 2 * DFF])
    w_down = din("w_down", [DFF, D])
    cst_d = din("cst", [128, 384], BF16)
    cols_d = din("cols", [128, NCOLS])
    grows_d = din("grows", [3, 128, D])
    rope1_d = din("rope1", [2, 128, S])
    ropeg_d = din("ropeg", [128, 128])
    y = nc.dram_tensor("y", [2, S, D], F32, kind="ExternalOutput").ap()

    A = Arena(nc)
    P = Prog()
    PS2 = [nc.alloc_psum_tensor(f"ps{i}", [128, 1024], F32) for i in range(4)]
    T_ps = [Tile(excl=True) for _ in range(8)]

    def bank(b):
        return PS2[b // 2][:, (b % 2) * 512:(b % 2 + 1) * 512]

    rr = [0]

    def nb():
        b = rr[0] % 8
        rr[0] += 1
        return b

    cst = A.alloc("cst", [128, 384], BF16)
    cols = A.alloc("cols", [128, NCOLS], F32)
    ropeg = A.alloc("ropeg", [128, 128], F32)
    st = A.alloc("st", [128, 16], F32)
    T_cst, T_cols, T_ropeg = Tile(), Tile(), Tile()
    T_st = [Tile() for _ in range(4)]
    st2 = A.alloc("st2", [128, 48], F32)
    ident, ones, perm = cst[:, 0:128], cst[:, 128:256], cst[:, 256:384]
    P.op("sp", lambda e: e.dma_start(out=cst[:], in_=cst_d), writes=[T_cst], dma=True)
    P.op("sp", lambda e: e.dma_start(out=cols[:], in_=cols_d), writes=[T_cols], dma=True)
    P.op("sp", lambda e: e.dma_start(out=ropeg[:], in_=ropeg_d), writes=[T_ropeg], dma=True)

    def col(c):
        return cols[:, c:c + 1]

    stcnt = [0]

    def mm_group(out_ap, pairs, first=True, last=True):
        def f(e):
            n = len(pairs)
            inst = None
            for i, (l, r) in enumerate(pairs):
                inst = e.matmul(out_ap, lhsT=l, rhs=r, start=(first and i == 0), stop=(last and i == n - 1))
            return inst
        return f

    def rstd_cols(eng_src_ap, width, n_feat, Tsrc, Tdst_list, junk, T_junk):
        pass

    def make_uT(src_aps, src_tiles, grow, T_grow, uT, T_uT, tis, junk, T_junk, ubf, T_ubf, evac_eng, rstd=None):
        for idx, ti in enumerate(tis):
            src, Ts = src_aps[idx], src_tiles[idx]
            k = stcnt[0] % 4
            stcnt[0] += 1
            sl = (stcnt[0] - 1) % len(ubf)
            c0 = 4 * k
            if rstd is None:
                P.op("act", lambda e, src=src, c0=c0: e.activation(out=junk[:], in_=src, func=AF.Square,
                                                                     accum_out=st[:, c0:c0 + 1]),
                     reads=[Ts], writes=[T_junk, T_st[k]])
                P.op("act", lambda e, c0=c0: e.activation(out=st[:, c0 + 1:c0 + 2], in_=st[:, c0:c0 + 1], func=AF.Ln,
                                                          scale=1.0 / D, bias=EPS),
                     reads=[T_st[k]], writes=[T_st[k]])
                P.op("act", lambda e, c0=c0: e.activation(out=st[:, c0 + 2:c0 + 3], in_=st[:, c0 + 1:c0 + 2], func=AF.Exp,
                                                          scale=-0.5),
                     reads=[T_st[k]], writes=[T_st[k]])
                rs_ap, T_rs = st[:, c0 + 2:c0 + 3], T_st[k]
            else:
                rs_ap, T_rs = rstd[idx]
            P.op("dve", lambda e, src=src, rs_ap=rs_ap, sl=sl: e.scalar_tensor_tensor(
                out=ubf[sl][:], in0=src, scalar=rs_ap, in1=grow[:], op0=ALU.mult, op1=ALU.mult),
                reads=[Ts, T_rs, T_grow], writes=[T_ubf[sl]])
            b = nb()
            psT = bank(b).bitcast(BF16)

            def tr(e, sl=sl, psT=psT):
                inst = None
                for c in range(8):
                    inst = e.transpose(out=psT[:, c * 128:(c + 1) * 128], in_=ubf[sl][:, c * 128:(c + 1) * 128],
                                       identity=ident)
                return inst
            P.op("pe", tr, reads=[T_ubf[sl], T_cst], writes=[T_ps[b]])
            t0 = ti * 128
            if evac_eng == "act":
                P.op("act", lambda e, psT=psT, t0=t0: e.activation(
                    out=uT[:, :, t0:t0 + 128], in_=psT.rearrange("p (c t) -> p c t", c=8), func=AF.Copy),
                    reads=[T_ps[b]], writes=[T_uT[ti]])
            else:
                P.op("dve", lambda e, psT=psT, t0=t0: e.tensor_copy(
                    out=uT[:, :, t0:t0 + 128], in_=psT.rearrange("p (c t) -> p c t", c=8)),
                    reads=[T_ps[b]], writes=[T_uT[ti]])

    def bstats(sq_aps, T_sq, nfeat, lnbias, lnt, T_lnt, rdst, T_rdst):
        b = nb()
        P.op("pe", mm_group(bank(b), [(ones, a) for a in sq_aps]), reads=T_sq + [T_cst], writes=[T_ps[b]])
        P.op("act", lambda e, b=b: e.activation(out=lnt[:], in_=bank(b), func=AF.Ln, scale=1.0 / nfeat, bias=EPS),
             reads=[T_ps[b]], writes=[T_lnt])
        P.op("act", lambda e: e.activation(out=rdst[:], in_=lnt[:], func=AF.Exp, scale=-0.5, bias=float(lnbias)),
             reads=[T_lnt], writes=[T_rdst])

    def chk(s_, k_):
        if stop is not None and (s_, k_) == tuple(stop):
            raise _Stop()

    def seq(s):
        w_in_sb = A.alloc("w_in", [128, 8, 1792], BF16)
        uT = A.alloc("uT", [128, 8, S], BF16)
        T_win, T_uT = Tile(), [Tile() for _ in range(16)]
        w_in_v = w_in.rearrange("(c p) n -> p c n", p=128)
        P.op("pool", lambda e: e.dma_start(out=w_in_sb[:, :, 0:704], in_=w_in_v[:, :, 0:704]), writes=[T_win], dma=True)
        P.op("pool", lambda e: e.dma_start(out=w_in_sb[:, :, 704:768], in_=w_in_v[:, :, 640:704]), writes=[T_win], dma=True)
        P.op("pool", lambda e: e.dma_start(out=w_in_sb[:, :, 768:1792], in_=w_in_v[:, :, 704:1728]), writes=[T_win], dma=True)

        wq_sb = A.alloc("wq", [128, 3, 768], BF16)
        wkv_sb = A.alloc("wkv", [128, 2, 1024], BF16)
        rope1 = A.alloc("rope1", [128, 2, S], F32)
        grow = A.alloc("grow", [128, D], F32)
        T_wq, T_wkv, T_rope1, T_grow = Tile(), Tile(), Tile(), Tile()
        P.op("pool", lambda e: e.dma_start(out=wq_sb[:], in_=wq.rearrange("(c p) n -> p c n", p=128)), writes=[T_wq], dma=True)
        P.op("pool", lambda e: e.dma_start(out=wkv_sb[:], in_=wkv.rearrange("(c p) n -> p c n", p=128)), writes=[T_wkv], dma=True)
        P.op("sp", lambda e: e.dma_start(out=rope1[:], in_=rope1_d.rearrange("a p t -> p a t")), writes=[T_rope1], dma=True)
        P.op("sp", lambda e: e.dma_start(out=grow[:], in_=grows_d[0]), writes=[T_grow], dma=True)
        Qn = A.alloc("Qn", [128, 4, S], BF16)
        Qpe = A.alloc("Qpe", [128, 2, S], BF16)
        Kn = A.alloc("Kn", [128, 4, S], BF16)
        Kpl = A.alloc("Kpl", [128, S], BF16)
        Kph = A.alloc("Kph", [128, S], BF16)
        Vm = A.alloc("Vm", [128, 16, 512], BF16)
        T_Q = [Tile() for _ in range(NB)]
        DBG.update(uT=uT, Qn=Qn, Qpe=Qpe, Kn=Kn, Kpe=Kpl, Vm=Vm, w_in_sb=w_in_sb)
        T_K, T_V = Tile(), Tile()
        T_Kpe = Tile()
        P.op("pool", lambda e: e.memset(Kpl[64:128, :], 0.0), writes=[T_Kpe])
        P.op("pool", lambda e: e.memset(Kph[0:64, :], 0.0), writes=[T_Kpe])
        T_Qpe = [Tile() for _ in range(NB)]
        xt = [A.alloc(f"xt{i}", [128, D], F32) for i in range(2)]
        T_xt = [Tile(), Tile()]
        ubf = [A.alloc(f"ubf{i}", [128, D], BF16) for i in range(2)]
        T_ubf = [Tile(), Tile()]
        junk = A.alloc("junk", [128, D], BF16)
        T_junk = Tile()
        cqs = [A.alloc(f"cq{i}", [128, 3, 512], BF16) for i in range(2)]
        sqs = [A.alloc(f"sq{i}", [128, 3, 512], BF16) for i in range(2)]
        ckvs = [A.alloc(f"ckv{i}", [128, 2, 512], BF16) for i in range(2)]
        sqkvs = [A.alloc(f"sqkv{i}", [128, 2, 512], BF16) for i in range(2)]
        T_cqs, T_sqs, T_ckvs, T_sqkvs = ([Tile(), Tile()] for _ in range(4))
        lnt = A.alloc("lnt", [128, 512], F32)
        rq = A.alloc("rq", [128, 512], F32)
        rkv = A.alloc("rkv", [128, 512], F32)
        T_lnt, T_rq, T_rkv = Tile(), Tile(), Tile()
        rkc = A.alloc("rkc", [128, 8], F32)
        T_rkc = Tile()
        qpfs = [A.alloc(f"qpf{i}", [128, 512], F32) for i in range(2)]
        qpbs = [A.alloc(f"qpb{i}", [128, 512], BF16) for i in range(2)]
        t1s = [A.alloc("t10", [128, 512], F32)] * 2
        t2s = [A.alloc("t20", [128, 512], F32)] * 2
        T_qpfs, T_qpbs = ([Tile(), Tile()] for _ in range(2))
        T_t1s, T_t2s = [Tile()] * 2, [Tile()] * 2
        xcnt = [0]
        chn = [0]

        def front(n):
            blk = slice(n * 512, (n + 1) * 512)
            cq, sq, ckv, sqkv = cqs[n % 2], sqs[n % 2], ckvs[n % 2], sqkvs[n % 2]
            T_cq, T_sq, T_ckv, T_sqkv = T_cqs[n % 2], T_sqs[n % 2], T_ckvs[n % 2], T_sqkvs[n % 2]
            for i in range(4):
                ti = n * 4 + i
                sl = xcnt[0] % 2
                xcnt[0] += 1
                P.op("sp", lambda e, sl=sl, ti=ti: e.dma_start(out=xt[sl][:], in_=x[s, ti * 128:(ti + 1) * 128, :]),
                     writes=[T_xt[sl]], dma=True)
                make_uT([xt[sl][:]], [T_xt[sl]], grow, T_grow, uT, T_uT, [ti], junk, T_junk, ubf, T_ubf, "act")
            uT_blk = [uT[:, k, blk] for k in range(8)]
            T_uTb = [T_uT[n * 4 + i] for i in range(4)]
            for c in range(3):
                b = nb()
                P.op("pe", mm_group(bank(b), [(w_in_sb[:, k, c * 128:(c + 1) * 128], uT_blk[k]) for k in range(8)]),
                     reads=T_uTb + [T_win], writes=[T_ps[b]])
                P.op("dve", lambda e, b=b, c=c: e.tensor_scalar(out=cq[:, c, :], in0=bank(b), scalar1=col(C_GQ + c),
                                                                 scalar2=None, op0=ALU.mult),
                     reads=[T_ps[b], T_cols], writes=[T_cq])
                P.op("act", lambda e, b=b, c=c: e.activation(out=sq[:, c, :], in_=bank(b), func=AF.Square),
                     reads=[T_ps[b]], writes=[T_sq])
            for c in range(2):
                b = nb()
                P.op("pe", mm_group(bank(b), [(w_in_sb[:, k, 384 + c * 128:384 + (c + 1) * 128], uT_blk[k]) for k in range(8)]),
                     reads=T_uTb + [T_win], writes=[T_ps[b]])
                P.op("dve", lambda e, b=b, c=c: e.tensor_scalar(out=ckv[:, c, :], in0=bank(b), scalar1=col(C_GKV + c),
                                                                 scalar2=None, op0=ALU.mult),
                     reads=[T_ps[b], T_cols], writes=[T_ckv])
                P.op("act", lambda e, b=b, c=c: e.activation(out=sqkv[:, c, :], in_=bank(b), func=AF.Square),
                     reads=[T_ps[b]], writes=[T_sqkv])

        def back(n):
            blk = slice(n * 512, (n + 1) * 512)
            cq, sq, ckv, sqkv = cqs[n % 2], sqs[n % 2], ckvs[n % 2], sqkvs[n % 2]
            T_cq, T_sq, T_ckv, T_sqkv = T_cqs[n % 2], T_sqs[n % 2], T_ckvs[n % 2], T_sqkvs[n % 2]
            uT_blk = [uT[:, k, blk] for k in range(8)]
            T_uTb = [T_uT[n * 4 + i] for i in range(4)]
            bstats([sq[:, c, :] for c in range(3)], [T_sq], 384, np.log(192.0 ** -0.5), lnt, T_lnt, rq, T_rq)
            bstats([sqkv[:, c, :] for c in range(2)], [T_sqkv], 256, 0.0, lnt, T_lnt, rkv, T_rkv)
            b = nb()

            def colstats(e, b=b):
                inst = None
                for i in range(4):
                    for c in range(2):
                        inst = e.matmul(bank(b)[:, i:i + 1], lhsT=sqkv[:, c, i * 128:(i + 1) * 128], rhs=ones[:, 0:1],
                                        start=(c == 0), stop=(c == 1))
                return inst
            P.op("pe", colstats, reads=[T_sqkv, T_cst], writes=[T_ps[b]])
            P.op("act", lambda e, b=b: e.activation(out=rkc[:, 4:8], in_=bank(b)[:, 0:4], func=AF.Ln, scale=1.0 / 256, bias=EPS),
                 reads=[T_ps[b]], writes=[T_rkc])
            P.op("act", lambda e: e.activation(out=rkc[:, 0:4], in_=rkc[:, 4:8], func=AF.Exp, scale=-0.5),
                 reads=[T_rkc], writes=[T_rkc])
            w = chn[0] % 2
            chn[0] += 1
            qpb, t1, t2 = qpbs[w], t1s[w], t2s[w]
            T_qpb, T_t1, T_t2 = T_qpbs[w], T_t1s[w], T_t2s[w]
            bk = nb()
            P.op("pe", mm_group(bank(bk), [(w_in_sb[:, k, 640:768], uT_blk[k]) for k in range(8)]),
                 reads=T_uTb + [T_win], writes=[T_ps[bk]])
            P.op("act", lambda e, bk=bk: e.activation(out=qpb[:], in_=bank(bk), func=AF.Copy), reads=[T_ps[bk]], writes=[T_qpb])
            b2 = nb()
            P.op("pe", mm_group(bank(b2), [(perm, qpb[:])]), reads=[T_qpb, T_cst], writes=[T_ps[b2]])
            P.op("dve", lambda e, bk=bk: e.tensor_tensor(out=t1[:], in0=bank(bk), in1=rope1[:, 0, blk], op=ALU.mult),
                 reads=[T_ps[bk], T_rope1], writes=[T_t1])
            P.op("dve", lambda e, b2=b2: e.tensor_tensor(out=t2[:], in0=bank(b2), in1=rope1[:, 1, blk], op=ALU.mult),
                 reads=[T_ps[b2], T_rope1], writes=[T_t2])
            P.op("pool", lambda e: e.tensor_tensor(out=Kpl[0:64, blk], in0=t1[0:64, :], in1=t2[0:64, :], op=ALU.add),
                 reads=[T_t1, T_t2], writes=[T_Kpe])
            P.op("pool", lambda e: e.tensor_tensor(out=Kph[64:128, blk], in0=t1[64:128, :], in1=t2[64:128, :], op=ALU.add),
                 reads=[T_t1, T_t2], writes=[T_Kpe])
            for h in range(4):
                b = nb()
                P.op("pe", mm_group(bank(b), [(wq_sb[:, c, h * 192:h * 192 + 128], cq[:, c, :]) for c in range(3)]),
                     reads=[T_cq, T_wq], writes=[T_ps[b]])
                P.op("dve", lambda e, b=b, h=h: e.tensor_tensor(out=Qn[:, h, blk], in0=bank(b), in1=rq[:], op=ALU.mult),
                     reads=[T_ps[b], T_rq], writes=[T_Q[n]])
            for pr in range(2):
                w = chn[0] % 2
                chn[0] += 1
                b = nb()

                def qpe_mm(e, b=b, pr=pr):
                    inst = None
                    for half in range(2):
                        h = 2 * pr + half
                        for c in range(3):
                            inst = e.matmul(bank(b)[half * 64:(half + 1) * 64, :],
                                            lhsT=wq_sb[:, c, h * 192 + 128:h * 192 + 192], rhs=cq[:, c, :],
                                            start=(c == 0), stop=(c == 2))
                    return inst
                P.op("pe", qpe_mm, reads=[T_cq, T_wq], writes=[T_ps[b]])
                P.op("dve", lambda e, b=b, w=w: e.tensor_tensor(out=qpfs[w][:], in0=bank(b), in1=rq[:], op=ALU.mult),
                     reads=[T_ps[b], T_rq], writes=[T_qpfs[w]])
                P.op("act", lambda e, w=w: e.activation(out=qpbs[w][:], in_=qpfs[w][:], func=AF.Copy),
                     reads=[T_qpfs[w]], writes=[T_qpbs[w]])
                b2 = nb()
                P.op("pe", mm_group(bank(b2), [(perm, qpbs[w][:])]), reads=[T_qpbs[w], T_cst], writes=[T_ps[b2]])
                P.op("pool", lambda e, w=w: e.tensor_tensor(out=t1s[w][:], in0=qpfs[w][:], in1=rope1[:, 0, blk], op=ALU.mult),
                     reads=[T_qpfs[w], T_rope1], writes=[T_t1s[w]])
                P.op("dve", lambda e, b2=b2, w=w: e.tensor_tensor(out=t2s[w][:], in0=bank(b2), in1=rope1[:, 1, blk], op=ALU.mult),
                     reads=[T_ps[b2], T_rope1], writes=[T_t2s[w]])
                P.op("pool", lambda e, pr=pr, w=w: e.tensor_tensor(out=Qpe[:, pr, blk], in0=t1s[w][:], in1=t2s[w][:], op=ALU.add),
                     reads=[T_t1s[w], T_t2s[w]], writes=[T_Qpe[n]])
            for h in range(4):
                b = nb()
                P.op("pe", mm_group(bank(b), [(wkv_sb[:, c, h * 256:h * 256 + 128], ckv[:, c, :]) for c in range(2)]),
                     reads=[T_ckv, T_wkv], writes=[T_ps[b]])
                P.op("dve", lambda e, b=b, h=h: e.tensor_tensor(out=Kn[:, h, blk], in0=bank(b), in1=rkv[:], op=ALU.mult),
                     reads=[T_ps[b], T_rkv], writes=[T_K])
            for i in range(4):
                b = nb()
                vr = [wkv_sb[:, c, :].rearrange("p (h two d) -> p h two d", h=4, two=2)[:, :, 1, :] for c in range(2)]
                P.op("pe", mm_group(bank(b), [(ckv[:, c, i * 128:(i + 1) * 128], vr[c]) for c in range(2)]),
                     reads=[T_ckv, T_wkv], writes=[T_ps[b]])
                P.op("act", lambda e, b=b, i=i: e.activation(out=Vm[:, n * 4 + i, :], in_=bank(b), func=AF.Copy,
                                                             scale=rkc[:, i:i + 1]),
                     reads=[T_ps[b], T_rkc], writes=[T_V])

        P.capture()
        front(0)
        P.replay(P.end_capture())
        for n in range(NB):
            P.capture()
            back(n)
            bl = P.end_capture()
            fl = []
            if n + 1 < NB:
                P.capture()
                front(n + 1)
                fl = P.end_capture()
            P.replay(bl, fl)
        A.free("wq", "wkv", "rope1", "grow", "xt0", "xt1", "ubf0", "ubf1", "junk", "cq0", "cq1", "sq0", "sq1",
               "ckv0", "ckv1", "sqkv0", "sqkv1", "lnt", "rq", "rkv", "rkc", "qpf0", "qpf1", "qpb0", "qpb1",
               "t10", "t20")
        P.barrier()
        A.release()
        chk(s, 1)

        def attention(score_pairs, v_of, gg_col0, o_dst, T_o_dst, wfn, extra_reads, recip_dve):
            PT = [A.alloc(f"PT{i}", [128, 1024], BF16) for i in range(4)]
            T_PT = [Tile() for _ in range(4)]
            on = A.alloc("on", [128, 4, 512], F32)
            osq = A.alloc("osq", [128, 4, 512], BF16)
            rinv = A.alloc("rinv", [128, 512], F32)
            glt = A.alloc("glt", [128, 512], F32)
            grs = A.alloc("grs", [128, 512], F32)
            T_on = [Tile() for _ in range(4)]
            T_osq = [Tile() for _ in range(4)]
            T_rinv, T_glt, T_grs = Tile(), Tile(), Tile()
            TS = [A.alloc(f"TS{i}", [128, 512], BF16) for i in range(4)]
            T_TS = [Tile() for _ in range(4)]
            steps = []
            WPOS = [(2, 0), (2, 4), (3, 0), (3, 4)]
            for n in range(NB):
                for h in range(4):
                    for k2 in range(8):
                        steps.append(("a", n, h, k2))
            later = []

            def S_op(i):
                sl = i % 2
                if steps[i][0] == "w":
                    wfn["pe"](steps[i][1], steps[i][2], sl)
                    return
                _, n, h, k2 = steps[i]

                def f(e):
                    inst = None
                    for j in range(2):
                        prs = score_pairs(n, h, 2 * k2 + j)
                        for m, (l, r) in enumerate(prs):
                            inst = e.matmul(PS2[sl][:, j * 512:(j + 1) * 512], lhsT=l, rhs=r,
                                            start=(m == 0), stop=(m == len(prs) - 1))
                    return inst
                P.op("pe", f, reads=[T_Q[n], T_K] + extra_reads(n), writes=[T_ps[2 * sl], T_ps[2 * sl + 1]])

            def EXP_op(i):
                sl = i % 2
                if steps[i][0] == "w":
                    wfn["post"](steps[i][1], steps[i][2], sl)
                    pass
                    return
                P.op("act", lambda e: e.activation(out=PT[i % 4][:], in_=PS2[sl][:], func=AF.Exp),
                     reads=[T_ps[2 * sl], T_ps[2 * sl + 1]], writes=[T_PT[i % 4]])

            def PV_op(i):
                if steps[i][0] == "w":
                    return
                _, n, h, k2 = steps[i]
                par = h % 2
                ob = 4 + par
                P.op("dve", lambda e: e.tensor_tensor(out=TS[i % 4][:], in0=PT[i % 4][:, 0:512], in1=PT[i % 4][:, 512:1024],
                                                      op=ALU.add), reads=[T_PT[i % 4]], writes=[T_TS[i % 4]])

                def f(e):
                    inst = None
                    for j in range(2):
                        kc = 2 * k2 + j
                        inst = e.matmul(bank(ob), lhsT=v_of(h, kc), rhs=PT[i % 4][:, j * 512:(j + 1) * 512],
                                        start=(kc == 0), stop=(kc == 15))
                    return inst
                P.op("pe", f, reads=[T_PT[i % 4], T_V], writes=[T_ps[ob]])

            def SUM_op(i):
                if steps[i][0] == "w":
                    return
                _, n, h, k2 = steps[i]
                sb = 6 + h % 2
                P.op("pe", lambda e: e.matmul(bank(sb), lhsT=ones, rhs=TS[i % 4][:], start=(k2 == 0), stop=(k2 == 7)),
                     reads=[T_TS[i % 4], T_cst], writes=[T_ps[sb]])

            def finalize_head(i):
                _, n, h, k2 = steps[i]
                par = h % 2
                ob, sb = 4 + par, 6 + par
                def mult():
                    P.op("dve", lambda e: e.tensor_tensor(out=on[:, h, :], in0=bank(ob), in1=rinv[:], op=ALU.mult),
                         reads=[T_ps[ob], T_rinv], writes=[T_on[h]])
                if recip_dve:
                    def chunk(c):
                        P.op("dve", lambda e: e.reciprocal(out=rinv[:, c * 128:(c + 1) * 128], in_=bank(sb)[:, c * 128:(c + 1) * 128]),
                             reads=[T_ps[sb]], writes=[T_rinv])
                    chunk(0)
                    chunk(1)
                    later.append((i + 2, lambda: chunk(2)))
                    later.append((i + 2, lambda: chunk(3)))
                    later.append((i + 3, mult))
                    d0 = 3
                else:
                    def recip_act():
                        P.op("act", lambda e: e.activation(out=glt[:], in_=bank(sb), func=AF.Ln), reads=[T_ps[sb]], writes=[T_glt])
                        P.op("act", lambda e: e.activation(out=rinv[:], in_=glt[:], func=AF.Exp, scale=-1.0),
                             reads=[T_glt], writes=[T_rinv])
                    later.append((i + 3, recip_act))
                    later.append((i + 4, mult))
                    d0 = 3
                later.append((i + 3 + d0, lambda: P.op(
                    "act", lambda e: e.activation(out=osq[:, h, :], in_=on[:, h, :], func=AF.Square),
                    reads=[T_on[h]], writes=[T_osq[h]])))
                if h == 3:
                    def gss_pe():
                        P.op("pe", mm_group(bank(7), [(ones, osq[:, hh, :]) for hh in range(4)]),
                             reads=T_osq + [T_cst], writes=[T_ps[7]])

                    def gss_act():
                        P.op("act", lambda e: e.activation(out=glt[:], in_=bank(7), func=AF.Ln, scale=1.0 / 512, bias=EPS),
                             reads=[T_ps[7]], writes=[T_glt])
                        P.op("act", lambda e: e.activation(out=grs[:], in_=glt[:], func=AF.Exp, scale=-0.5),
                             reads=[T_glt], writes=[T_grs])

                    def gss_dve():
                        for hh in range(4):
                            P.op("dve", lambda e, hh=hh: e.scalar_tensor_tensor(
                                out=o_dst(n, hh), in0=on[:, hh, :], scalar=col(gg_col0 + hh), in1=grs[:],
                                op0=ALU.mult, op1=ALU.mult),
                                reads=[T_on[hh], T_cols, T_grs], writes=[T_o_dst(n)])
                    later.append((i + 4 + d0, gss_pe))
                    later.append((i + 6 + d0, gss_act))
                    later.append((i + 7 + d0, gss_dve))

            def run_due(i):
                due = [x for x in later if x[0] <= i]
                for x in due:
                    later.remove(x)
                    x[1]()

            S_op(0)
            for i in range(len(steps)):
                if i + 1 < len(steps):
                    S_op(i + 1)
                EXP_op(i)
                if i >= 2:
                    SUM_op(i - 2)
                PV_op(i)
                if wfn is not None:
                    _, n_, h_, k2_ = steps[i]
                    if n_ >= 1 and h_ >= 1:
                        if k2_ == 0:
                            for t_ in sorted(set(((h_ - 1) * 3 + d) // 2 for d in range(3) if (h_ - 1) * 3 + d < 8)):
                                wfn["load"](n_ - 1, t_)
                        if k2_ in (4, 5, 6):
                            opn = (h_ - 1) * 3 + (k2_ - 4)
                            if opn < 8:
                                q_ = 1 - h_ % 2
                                bk_ = (4 + q_) if (k2_ - 4) % 2 == 0 else (6 + q_)
                                wfn["pe"](n_ - 1, opn // 2, opn % 2, bk_)
                                later.append((i + 1, lambda m=n_ - 1, t=opn // 2, f=opn % 2, bk=bk_: wfn["post"](m, t, f, bk)))
                                if opn % 2 == 1:
                                    later.append((i + 4, lambda m=n_ - 1, t=opn // 2: wfn["stats"](m, t)))
                run_due(i)
                if steps[i][0] == "a" and steps[i][3] == 7:
                    later.append((i + 2, lambda i=i: finalize_head(i)))
            SUM_op(len(steps) - 2)
            SUM_op(len(steps) - 1)
            while later:
                later.sort(key=lambda x: x[0])
                x = later.pop(0)
                x[1]()
            if wfn is not None:
                for ii in range(4):
                    wfn["load"](NB - 1, ii)
                for ii in range(4):
                    for fh in range(2):
                        bk_ = 4 + (2 * ii + fh) % 4
                        wfn["pe"](NB - 1, ii, fh, bk_)
                        wfn["post"](NB - 1, ii, fh, bk_)
                    wfn["stats"](NB - 1, ii)
            A.free("PT0", "PT1", "PT2", "on", "osq", "rinv", "glt", "grs", "TS0", "TS1", "TS2", "TS3", "PT3")

        o_mla = A.alloc("o_mla", [128, 4, S], BF16)
        T_omla = [Tile() for _ in range(NB)]
        DBG.update(o_mla=o_mla)

        def mla_pairs(n, h, kc):
            blk = slice(n * 512, (n + 1) * 512)
            ks = slice(kc * 128, (kc + 1) * 128)
            kp = Kpl if h % 2 == 0 else Kph
            return [(Kn[:, h, ks], Qn[:, h, blk]), (kp[:, ks], Qpe[:, h // 2, blk])]

        attention(mla_pairs, lambda h, kc: Vm[:, kc, h * 128:(h + 1) * 128], C_GGM,
                  lambda n, hh: o_mla[:, hh, n * 512:(n + 1) * 512], lambda n: T_omla[n], None,
                  lambda n: [T_Qpe[n], T_Kpe], False)
        A.free("Qn", "Qpe", "Kn", "Kpl", "Kph", "Vm")
        P.barrier()
        A.release()
        chk(s, 2)

        Qg = A.alloc("Qg", [128, 4, S], BF16)
        Kg = A.alloc("Kg", [128, 2, S], BF16)
        Vg = A.alloc("Vg", [128, 16, 256], BF16)
        DBG.update(Qg=Qg, Kg=Kg, Vg=Vg)
        T_Q = [Tile() for _ in range(NB)]
        T_K, T_V = Tile(), Tile()
        w_out_sb = A.alloc("w_out", [128, 8, D], BF16)
        T_wout = Tile()
        P.op("pool", lambda e: e.dma_start(out=w_out_sb[:], in_=w_out.rearrange("(c p) n -> p c n", p=128)),
             writes=[T_wout], dma=True)
        qg = [A.alloc(f"qg{i}", [128, 512], BF16) for i in range(2)]
        sqg = [A.alloc(f"sqg{i}", [128, 512], BF16) for i in range(2)]
        lng = [A.alloc(f"lng{i}", [128, 512], F32) for i in range(2)]
        rg = [A.alloc(f"rg{i}", [128, 512], F32) for i in range(2)]
        g1 = [A.alloc(f"g1{i}", [128, 512], F32) for i in range(2)]
        g2 = [A.alloc(f"g2{i}", [128, 512], F32) for i in range(2)]
        T_qg, T_sqg, T_lng, T_rg, T_g1, T_g2 = ([Tile(), Tile()] for _ in range(6))
        rstep = ropeg[:].ap[0][0]
        hcnt = 0
        items = []
        for n in range(NB):
            blk = slice(n * 512, (n + 1) * 512)
            uT_blk = [uT[:, k, blk] for k in range(8)]
            T_uTb = [T_uT[n * 4 + i] for i in range(4)]
            cc_top = bass.AP(ropeg, 0 * rstep + n * 8, [[rstep, 64], [1, 8], [0, 64]])
            ss_top = bass.AP(ropeg, 0 * rstep + 64 + n * 8, [[rstep, 64], [1, 8], [0, 64]])
            cc_bot = bass.AP(ropeg, 64 * rstep, [[rstep, 64], [0, 8], [1, 64]])
            ss_bot = bass.AP(ropeg, 64 * rstep + 64, [[rstep, 64], [0, 8], [1, 64]])

            def v3(ap):
                return ap.rearrange("p (a b) -> p a b", a=8)
            for hd in range(6):
                isq = hd < 4
                c0 = 768 + hd * 128 if isq else 1280 + (hd - 4) * 128
                gcol = C_GQQ if isq else C_GQK
                dst = Qg[:, hd, blk] if isq else Kg[:, hd - 4, blk]
                Tdst = T_Q[n] if isq else T_K
                lnb = np.log(128.0 ** -0.5) if isq else 0.0
                u = hcnt % 2
                hcnt += 1
                P.capture()
                b = nb()
                P.op("pe", mm_group(bank(b), [(w_in_sb[:, k, c0:c0 + 128], uT_blk[k]) for k in range(8)]),
                     reads=T_uTb + [T_win], writes=[T_ps[b]])
                P.op("act", lambda e, b=b, u=u, gcol=gcol: e.activation(out=qg[u][:], in_=bank(b), func=AF.Copy, scale=col(gcol)),
                     reads=[T_ps[b], T_cols], writes=[T_qg[u]])
                P.op("act", lambda e, b=b, u=u: e.activation(out=sqg[u][:], in_=bank(b), func=AF.Square),
                     reads=[T_ps[b]], writes=[T_sqg[u]])
                P.op("dve", lambda e, b=b, u=u, gcol=gcol, cc_top=cc_top: e.scalar_tensor_tensor(
                    out=v3(g1[u][0:64, :]), in0=v3(bank(b)[0:64, :]), scalar=cols[0:64, gcol:gcol + 1], in1=cc_top,
                    op0=ALU.mult, op1=ALU.mult), reads=[T_ps[b], T_cols, T_ropeg], writes=[T_g1[u]])
                P.op("dve", lambda e, b=b, u=u, gcol=gcol, cc_bot=cc_bot: e.scalar_tensor_tensor(
                    out=v3(g1[u][64:128, :]), in0=v3(bank(b)[64:128, :]), scalar=cols[64:128, gcol:gcol + 1], in1=cc_bot,
                    op0=ALU.mult, op1=ALU.mult), reads=[T_ps[b], T_cols, T_ropeg], writes=[T_g1[u]])
                bs = nb()
                P.op("pe", mm_group(bank(bs), [(ones, sqg[u][:])]), reads=[T_sqg[u], T_cst], writes=[T_ps[bs]])
                bw = nb()
                P.op("pe", mm_group(bank(bw), [(perm, qg[u][:])]), reads=[T_qg[u], T_cst], writes=[T_ps[bw]])
                P.op("act", lambda e, bs=bs, u=u: e.activation(out=lng[u][:], in_=bank(bs), func=AF.Ln, scale=1.0 / 128, bias=EPS),
                     reads=[T_ps[bs]], writes=[T_lng[u]])
                P.op("act", lambda e, u=u, lnb=lnb: e.activation(out=rg[u][:], in_=lng[u][:], func=AF.Exp, scale=-0.5, bias=float(lnb)),
                     reads=[T_lng[u]], writes=[T_rg[u]])
                P.op("dve", lambda e, bw=bw, u=u, ss_top=ss_top: e.tensor_tensor(
                    out=v3(g2[u][0:64, :]), in0=v3(bank(bw)[0:64, :]), in1=ss_top, op=ALU.mult),
                    reads=[T_ps[bw], T_ropeg], writes=[T_g2[u]])
                P.op("dve", lambda e, bw=bw, u=u, ss_bot=ss_bot: e.tensor_tensor(
                    out=v3(g2[u][64:128, :]), in0=v3(bank(bw)[64:128, :]), in1=ss_bot, op=ALU.mult),
                    reads=[T_ps[bw], T_ropeg], writes=[T_g2[u]])
                P.op("pool", lambda e, u=u: e.tensor_tensor(out=g1[u][:], in0=g1[u][:], in1=g2[u][:], op=ALU.add),
                     reads=[T_g2[u]], writes=[T_g1[u]])
                P.op("pool", lambda e, u=u, dst=dst: e.tensor_tensor(out=dst, in0=g1[u][:], in1=rg[u][:], op=ALU.mult),
                     reads=[T_g1[u], T_rg[u]], writes=[Tdst])
                L = P.end_capture()
                items.append((L[:5], L[5:]))
            for i in range(4):
                P.capture()
                b = nb()
                t0 = (n * 4 + i) * 128
                P.op("pe", mm_group(bank(b)[:, 0:256], [(uT[:, k, t0:t0 + 128], w_in_sb[:, k, 1536:1792]) for k in range(8)]),
                     reads=[T_uT[n * 4 + i], T_win], writes=[T_ps[b]])
                P.op("act", lambda e, b=b, i=i, n=n: e.activation(out=Vg[:, n * 4 + i, :], in_=bank(b)[:, 0:256], func=AF.Copy),
                     reads=[T_ps[b]], writes=[T_V])
                items[-1 - i % 2] = (items[-1 - i % 2][0], items[-1 - i % 2][1] + P.end_capture())
        for i in range(len(items)):
            P.replay(items[i][0])
            P.replay(items[i][1])
        A.free("w_in", "uT", "qg0", "qg1", "sqg0", "sqg1", "lng0", "lng1", "rg0", "rg1", "g10", "g11", "g20", "g21")
        P.barrier()
        A.release()
        chk(s, 3)

        hres = A.alloc("h", [128, 16, D], F32)
        T_h = [Tile() for _ in range(16)]
        DBG.update(h=hres)
        og = A.alloc("og", [128, 4, 512], BF16)
        T_og = Tile()
        xr = [A.alloc(f"xr{i}", [128, D], F32) for i in range(4)]
        T_xr = [Tile() for _ in range(4)]
        junk2 = A.alloc("junk2", [128, D], BF16)
        T_junk2 = Tile()
        T_ss2 = [Tile() for _ in range(16)]
        T_r2 = Tile()

        def w_load(n, i):
            ti = n * 4 + i
            t0 = ti * 128
            P.op("sp", lambda e: e.dma_start(out=xr[i][:], in_=x[s, t0:t0 + 128, :]), writes=[T_xr[i]], dma=True)

        def w_pe(n, i, fh, bk):
            ti = n * 4 + i
            t0 = ti * 128
            pairs = [(o_mla[:, c, t0:t0 + 128], w_out_sb[:, c, fh * 512:(fh + 1) * 512]) for c in range(4)]
            pairs += [(og[:, c, i * 128:(i + 1) * 128], w_out_sb[:, 4 + c, fh * 512:(fh + 1) * 512]) for c in range(4)]
            P.op("pe", mm_group(bank(bk), pairs), reads=[T_omla[n], T_og, T_wout], writes=[T_ps[bk]])

        def w_post(n, i, fh, bk):
            ti = n * 4 + i
            P.op("dve", lambda e: e.tensor_tensor(out=hres[:, ti, fh * 512:(fh + 1) * 512], in0=bank(bk),
                                                  in1=xr[i][:, fh * 512:(fh + 1) * 512], op=ALU.add),
                 reads=[T_ps[bk], T_xr[i]], writes=[T_h[ti]])

        def w_stats(n, i):
            ti = n * 4 + i
            P.op("act", lambda e: e.activation(out=junk2[:], in_=hres[:, ti, :], func=AF.Square, accum_out=st2[:, ti:ti + 1]),
                 reads=[T_h[ti]], writes=[T_junk2, T_ss2[ti]])

        def gqa_pairs(n, h, kc):
            return [(Kg[:, h // 2, kc * 128:(kc + 1) * 128], Qg[:, h, n * 512:(n + 1) * 512])]

        attention(gqa_pairs, lambda h, kc: Vg[:, kc, (h // 2) * 128:(h // 2 + 1) * 128], C_GGG,
                  lambda n, hh: og[:, hh, :], lambda n: T_og, {"pe": w_pe, "post": w_post, "stats": w_stats, "load": w_load}, lambda n: [], False)
        A.free("Qg", "Kg", "Vg", "w_out", "o_mla", "og", "xr0", "xr1", "xr2", "xr3", "junk2")
        P.barrier()
        A.release()
        chk(s, 4)

        u2T = A.alloc("u2T", [128, 8, S], BF16)
        T_u2T = [Tile() for _ in range(16)]
        DBG.update(u2T=u2T)
        DBGP3 = True
        grow2 = A.alloc("grow2", [128, D], F32)
        T_grow2 = Tile()
        P.op("sp", lambda e: e.dma_start(out=grow2[:], in_=grows_d[1]), writes=[T_grow2], dma=True)
        ubf = [A.alloc(f"ubf{i}", [128, D], BF16) for i in range(2)]
        T_ubf = [Tile() for _ in range(2)]
        junk = A.alloc("junk", [128, D], BF16)
        T_junk = Tile()
        NWU = 3
        wg = [A.alloc(f"wg{i}", [128, 8, 128], BF16) for i in range(NWU)]
        wv = [A.alloc(f"wv{i}", [128, 8, 128], BF16) for i in range(NWU)]
        T_wu = [Tile() for _ in range(NWU)]
        NWD = 8
        wd = [A.alloc(f"wd{i}", [128, D], BF16) for i in range(NWD)]
        T_wd = [Tile() for _ in range(NWD)]
        actT = [A.alloc(f"actT{i}", [128, S], BF16) for i in range(6)]
        T_actT = [Tile() for _ in range(6)]
        hug = [A.alloc(f"hug{i}", [128, S + 2], BF16) for i in range(2)]
        huv = [A.alloc(f"huv{i}", [128, S + 2], F32) for i in range(2)]
        cvs = [A.alloc(f"cvs{i}", [128, 512], F32) for i in range(2)]
        T_cv = [Tile(), Tile()]
        T_hu = [[Tile() for _ in range(NB)] for _ in range(2)]
        T_hv = [[Tile() for _ in range(NB)] for _ in range(2)]
        dg = [A.alloc(f"dg{i}", [128, 3, 128], BF16) for i in range(2)]
        T_dg = [Tile(), Tile()]
        sg = [A.alloc(f"sg{i}", [128, 512], BF16) for i in range(2)]
        T_sg = [Tile(), Tile()]
        DBG.update(actT0=actT[0], actT4=actT[4], hug0=hug[0], hug1=hug[1], huv0=huv[0], huv1=huv[1])
        for u in range(2):
            for buf, TT in ((hug[u], T_hu[u]), (huv[u], T_hv[u])):
                P.op("pool", lambda e, buf=buf: e.memset(buf[:, 0:1], 0.0), writes=[TT[0]])
                P.op("pool", lambda e, buf=buf: e.memset(buf[:, S + 1:S + 2], 0.0), writes=[TT[NB - 1]])
        w_up_v = w_up.rearrange("(c p) n -> p c n", p=128)
        jlist = []
        for gi, gsz in enumerate(GROUPS):
            j0 = sum(GROUPS[:gi])
            jlist.append(list(range(j0, j0 + gsz)))

        def load_w(j):
            su, sd = j % NWU, j % NWD
            P.op("pool", lambda e: e.dma_start(out=wg[su][:], in_=w_up_v[:, :, j * 128:(j + 1) * 128]), writes=[T_wu[su]], dma=True)
            P.op("pool", lambda e: e.dma_start(out=wv[su][:], in_=w_up_v[:, :, DFF + j * 128:DFF + (j + 1) * 128]),
                 writes=[T_wu[su]], dma=True)
            P.op("pool", lambda e: e.dma_start(out=wd[sd][:], in_=w_down[j * 128:(j + 1) * 128, :]), writes=[T_wd[sd]], dma=True)

        def dg_build(j):
            u = j % 2
            for k in range(3):
                P.op("act", lambda e, k=k: e.activation(out=dg[u][:, k, :], in_=ident, func=AF.Copy, scale=col(C_CW + k * 44 + j)),
                     reads=[T_cst, T_cols], writes=[T_dg[u]])

        def hu_block(j, n):
            su, u = j % NWU, j % 2
            blk = slice(n * 512, (n + 1) * 512)
            rd = [T_u2T[n * 4 + i] for i in range(4)] + [T_wu[su]]
            b = n % 2
            P.op("pe", mm_group(bank(b), [(wg[su][:, k, :], u2T[:, k, blk]) for k in range(8)]), reads=rd, writes=[T_ps[b]])
            P.op("act", lambda e, b=b: e.activation(out=hug[u][:, 1 + n * 512:1 + (n + 1) * 512], in_=bank(b), func=AF.Copy),
                 reads=[T_ps[b]], writes=[T_hu[u][n]])
            b = (2, 3, 6, 7)[n % 4]
            P.op("pe", mm_group(bank(b), [(wv[su][:, k, :], u2T[:, k, blk]) for k in range(8)]), reads=rd, writes=[T_ps[b]])
            P.op("act", lambda e, b=b: e.activation(out=huv[u][:, 1 + n * 512:1 + (n + 1) * 512], in_=bank(b), func=AF.Copy),
                 reads=[T_ps[b]], writes=[T_hv[u][n]])

        def hu_ops(j):
            dg_build(j)
            for n in range(NB):
                hu_block(j, n)

        def conv_ops(j, jj):
            u = j % 2
            cw0, cw1, cw2, cbv = (col(C_CW + k * 44 + 22 + j) for k in range(3)), None, None, col(C_CB + 22 + j)
            cw0, cw1, cw2 = col(C_CW + 22 + j), col(C_CW + 44 + 22 + j), col(C_CW + 88 + 22 + j)
            for n in range(NB):
                nbrs = [T_hu[u][m] for m in (n - 1, n, n + 1) if 0 <= m < NB]
                nbrv = [T_hv[u][m] for m in (n - 1, n, n + 1) if 0 <= m < NB]
                bg = 4 + n % 2
                v = n % 2
                o0 = n * 512
                P.op("pe", mm_group(bank(bg), [(dg[u][:, k, :], hug[u][:, o0 + k:o0 + k + 512]) for k in range(3)]),
                     reads=nbrs + [T_dg[u]], writes=[T_ps[bg]])
                P.op("dve", lambda e, v=v, o0=o0: e.tensor_scalar(out=cvs[v][:], in0=huv[u][:, o0 + 1:o0 + 513], scalar1=cw1,
                                                                 scalar2=cbv, op0=ALU.mult, op1=ALU.add),
                     reads=nbrv + [T_cols], writes=[T_cv[v]])
                P.op("dve", lambda e, v=v, o0=o0: e.scalar_tensor_tensor(out=cvs[v][:], in0=huv[u][:, o0:o0 + 512], scalar=cw0,
                                                                        in1=cvs[v][:], op0=ALU.mult, op1=ALU.add),
                     reads=nbrv + [T_cols], writes=[T_cv[v]])
                P.op("dve", lambda e, v=v, o0=o0: e.scalar_tensor_tensor(out=cvs[v][:], in0=huv[u][:, o0 + 2:o0 + 514], scalar=cw2,
                                                                        in1=cvs[v][:], op0=ALU.mult, op1=ALU.add),
                     reads=nbrv + [T_cols], writes=[T_cv[v]])
                P.op("act", lambda e, bg=bg, v=v: e.activation(out=sg[v][:], in_=bank(bg), func=AF.Silu, bias=col(C_CB + j)),
                     reads=[T_ps[bg], T_cols], writes=[T_sg[v]])
                P.op("dve", lambda e, v=v, o0=o0: e.tensor_tensor(out=actT[jj][:, o0:o0 + 512], in0=cvs[v][:], in1=sg[v][:], op=ALU.mult),
                     reads=[T_cv[v], T_sg[v]], writes=[T_actT[jj]])

        yt = None
        allj = [j for g in jlist for j in g]
        load_w(0)
        load_w(1)
        dg_build(0)
        dg_build(1)
        P.op("act", lambda e: e.activation(out=st2[:, 16:32], in_=st2[:, 0:16], func=AF.Ln, scale=1.0 / D, bias=EPS),
             reads=T_ss2, writes=[T_r2])
        P.op("act", lambda e: e.activation(out=st2[:, 32:48], in_=st2[:, 16:32], func=AF.Exp, scale=-0.5),
             reads=[T_r2], writes=[T_r2])
        for n in range(NB):
            for i in range(4):
                ti = n * 4 + i
                make_uT([hres[:, ti, :]], [T_h[ti]], grow2, T_grow2, u2T, T_u2T, [ti], junk, T_junk, ubf, T_ubf,
                        "act", rstd=[(st2[:, 32 + ti:33 + ti], T_r2)])
            hu_block(0, n)
            hu_block(1, n)
        def down_ops(gi):
            js = jlist[gi]
            lastg = gi == len(jlist) - 1
            if lastg:
                growf = A.alloc("growf", [128, D], F32)
                T_growf = Tile()
                P.op("sp", lambda e: e.dma_start(out=growf[:], in_=grows_d[2]), writes=[T_growf], dma=True)
                yt = [A.alloc(f"yt{i}", [128, D], F32) for i in range(2)]
                T_yt = [Tile(), Tile()]
            fin = []
            for ti in range(16):
                t0 = ti * 128
                for fh in range(2):
                    b = (2 * ti + fh) % 4
                    P.op("pe", mm_group(bank(b), [(actT[jj][:, t0:t0 + 128], wd[j % NWD][:, fh * 512:(fh + 1) * 512])
                                                  for jj, j in enumerate(js)]),
                         reads=[T_actT[jj] for jj in range(len(js))] + [T_wd[j % NWD] for j in js], writes=[T_ps[b]])
                    P.op("dve", lambda e, b=b, ti=ti, fh=fh: e.tensor_tensor(
                        out=hres[:, ti, fh * 512:(fh + 1) * 512], in0=bank(b), in1=hres[:, ti, fh * 512:(fh + 1) * 512], op=ALU.add),
                        reads=[T_ps[b], T_h[ti]], writes=[T_h[ti]])
                if lastg:
                    k = stcnt[0] % 4
                    stcnt[0] += 1
                    c0 = 4 * k
                    sl = ti % 2
                    P.op("act", lambda e, ti=ti, c0=c0: e.activation(out=junk[:], in_=hres[:, ti, :], func=AF.Square,
                                                                    accum_out=st[:, c0:c0 + 1]),
                         reads=[T_h[ti]], writes=[T_junk, T_st[k]])
                    P.op("act", lambda e, c0=c0: e.activation(out=st[:, c0 + 1:c0 + 2], in_=st[:, c0:c0 + 1], func=AF.Ln,
                                                              scale=1.0 / D, bias=EPS), reads=[T_st[k]], writes=[T_st[k]])
                    P.op("act", lambda e, c0=c0: e.activation(out=st[:, c0 + 2:c0 + 3], in_=st[:, c0 + 1:c0 + 2], func=AF.Exp,
                                                              scale=-0.5), reads=[T_st[k]], writes=[T_st[k]])
                    def finish(ti=ti, c0=c0, sl=sl, k=k, t0=t0):
                        P.op("dve", lambda e: e.scalar_tensor_tensor(
                            out=yt[sl][:], in0=hres[:, ti, :], scalar=st[:, c0 + 2:c0 + 3], in1=growf[:], op0=ALU.mult, op1=ALU.mult),
                            reads=[T_h[ti], T_st[k], T_growf], writes=[T_yt[sl]])
                        P.op("sp", lambda e: e.dma_start(out=y[s, t0:t0 + 128, :], in_=yt[sl][:]),
                             reads=[T_yt[sl]], dma=True)
                    fin.append(finish)
                    if len(fin) >= 2:
                        fin.pop(0)()
            for f_ in fin:
                f_()

        flat = [(gi, jj, j) for gi, js in enumerate(jlist) for jj, j in enumerate(js)]
        for idx, (gi, jj, j) in enumerate(flat):
            if j >= 2:
                hu_ops(j)
            if idx > 0:
                pgi, pjj, pj = flat[idx - 1]
                conv_ops(pj, pjj)
                if pjj == len(jlist[pgi]) - 1:
                    down_ops(pgi)
            if j + 2 < NJ:
                load_w(j + 2)
        pgi, pjj, pj = flat[-1]
        conv_ops(pj, pjj)
        down_ops(pgi)
        names = ["u2T", "grow2", "growf", "ubf0", "ubf1", "cvs0", "cvs1", "junk", "h", "yt0", "yt1"]
        names += [f"wg{i}" for i in range(NWU)] + [f"wv{i}" for i in range(NWU)] + [f"wd{i}" for i in range(NWD)]
        names += [f"actT{i}" for i in range(6)] + ["hug0", "hug1", "huv0", "huv1", "dg0", "dg1", "sg0", "sg1"]
        A.free(*names)
        P.barrier()
        A.release()
        chk(s, 5)

    try:
        for s_ in range(2):
            seq(s_)
    except _Stop:
        P.barrier()
        for name in (dbg or []):
            t = DBG[name]
            shp = list(t[:].shape)
            dd = nc.dram_tensor("dbg_" + name, shp, t[:].dtype, kind="ExternalOutput").ap()
            P.op("sp", lambda e, t=t, dd=dd: e.dma_start(out=dd, in_=t[:]), dma=True)
        P.barrier()

    semnames = list(Prog.ENGS) + [(e, i) for e in Prog.ENGS for i in range(Prog.NRING)]
    with contextlib.ExitStack() as es:
        sems = {k: es.enter_context(nc.semaphore("s_" + str(k).replace(" ", "").replace("'", "").replace("(", "")
                                                 .replace(")", "").replace(",", "_"))) for k in semnames}
        P.finalize(sems)
        block = es.enter_context(nc.Block())
        block.tensor(lambda e: P.emit_engine("pe", e))
        block.scalar(lambda e: P.emit_engine("act", e))
        block.vector(lambda e: P.emit_engine("dve", e))
        block.gpsimd(lambda e: P.emit_engine("pool", e))
        block.sync(lambda e: P.emit_engine("sp", e))
    return nc


def _rope_tables():
    inv = (10000.0 ** (-np.arange(0, 64, 2, dtype=np.float32) / 64.0)).astype(np.float32)
    p = np.arange(128)
    f = p % 32
    sign = np.where((p % 64) < 32, -1.0, 1.0).astype(np.float32)
    t = np.arange(S, dtype=np.float32)
    ang1 = (t[None, :] * inv[f][:, None]).astype(np.float32)
    rope1 = np.stack([np.cos(ang1), sign[:, None] * np.sin(ang1)], 0).astype(np.float32)
    pos = np.arange(64, dtype=np.float32)
    angg = (pos[None, :] * inv[f][:, None]).astype(np.float32)
    ropeg = np.concatenate([np.cos(angg), sign[:, None] * np.sin(angg)], 1).astype(np.float32)
    return rope1, ropeg


def _consts():
    ident = np.eye(128, dtype=np.float32)
    ones = np.ones((128, 128), np.float32)
    perm = np.zeros((128, 128), np.float32)
    for m in range(128):
        k = m + 32 if (m % 64) < 32 else m - 32
        perm[k, m] = 1.0
    return np.concatenate([ident, ones, perm], 1).astype(ml_dtypes.bfloat16)


_NC_CACHE = {}


def kernel(x, norm1_g, w_in, mla_q_norm_g, mla_w_q_up, mla_kv_norm_g, mla_w_kv_up,
           gqa_q_norm_g, gqa_k_norm_g, group_norm_mla_g, group_norm_gqa_g, w_out,
           norm2_g, ffn_w_up, ffn_conv_w, ffn_conv_b, ffn_w_down, final_norm_g):
    f32 = np.float32
    x = np.asarray(x, f32)

    def colify(v, n):
        return np.asarray(v, f32).reshape(n, 128).T

    cols = np.concatenate([
        colify(mla_q_norm_g[0], 3), colify(mla_kv_norm_g[0], 2),
        colify(gqa_q_norm_g[0], 1), colify(gqa_k_norm_g[0], 1),
        colify(group_norm_mla_g[0], 4), colify(group_norm_gqa_g[0], 4),
        colify(ffn_conv_w[0, 0], 44), colify(ffn_conv_w[0, 1], 44), colify(ffn_conv_w[0, 2], 44),
        colify(ffn_conv_b[0], 44)], 1)
    cols = np.ascontiguousarray(cols, f32)
    assert cols.shape == (128, NCOLS)
    grows = np.ascontiguousarray(np.stack([
        np.broadcast_to(np.asarray(norm1_g[0], f32), (128, D)),
        np.broadcast_to(np.asarray(norm2_g[0], f32), (128, D)),
        np.broadcast_to(np.asarray(final_norm_g, f32), (128, D))], 0))
    rope1, ropeg = _rope_tables()
    cst = _consts()
    if "nc" not in _NC_CACHE:
        _NC_CACHE["nc"] = build_nc()
    nc = _NC_CACHE["nc"]
    shared = {
        "w_in": np.ascontiguousarray(w_in[0], f32), "wq": np.ascontiguousarray(mla_w_q_up[0], f32),
        "wkv": np.ascontiguousarray(mla_w_kv_up[0], f32), "w_out": np.ascontiguousarray(w_out[0], f32),
        "w_up": np.ascontiguousarray(ffn_w_up[0], f32), "w_down": np.ascontiguousarray(ffn_w_down[0], f32),
        "cst": cst, "cols": cols, "grows": grows, "rope1": rope1, "ropeg": ropeg,
    }
    in_maps = []
    for c in range(8):
        m = dict(shared)
        m["x"] = np.ascontiguousarray(x[2 * c:2 * c + 2])
        in_maps.append(m)
    res = run_bass_kernel_spmd(nc, in_maps, core_ids=list(range(8)))
    out = np.concatenate([np.asarray(r["y"], f32) for r in res.results], 0)
    return out
```

```python
import contextlib
import numpy as np
import ml_dtypes
import concourse.bass as bass
import concourse.mybir as mybir
from concourse.bass_utils import run_bass_kernel_spmd

F32 = mybir.dt.float32
BF16 = mybir.dt.bfloat16
AF = mybir.ActivationFunctionType
ALU = mybir.AluOpType

SB_LO = 16512
SB_HI = 229344
EPS = 1e-6
S = 2048
D = 1024
NB = 4
DFF = 2816
NJ = 22
GROUPS = [6, 6, 6, 4]

C_GQ, C_GKV, C_GQQ, C_GQK, C_GGM, C_GGG, C_CW, C_CB, NCOLS = 0, 3, 5, 6, 7, 11, 15, 147, 191


class Tile:
    __slots__ = ("ws", "rs", "excl")

    def __init__(self, excl=False):
        self.ws = []
        self.rs = {}
        self.excl = excl


class Ev:
    __slots__ = ("eng", "emit", "deps", "val", "sem", "dma", "needed", "pre")

    def __init__(self, eng, emit, dma):
        self.eng = eng
        self.emit = emit
        self.dma = dma
        self.deps = ()
        self.val = None
        self.sem = None
        self.needed = False
        self.pre = None


class Arena:
    def __init__(self, nc):
        self.nc = nc
        self.live = {}
        self.pending = []
        self.n = 0

    def alloc(self, name, shape, dtype):
        esz = 2 if dtype == BF16 else 4
        free = 1
        for d in shape[1:]:
            free *= d
        nbytes = ((free * esz + 63) // 64) * 64
        pos = SB_LO
        for lo, hi in sorted(self.live.values()):
            if pos + nbytes <= lo:
                break
            pos = max(pos, hi)
        if pos + nbytes > SB_HI:
            raise RuntimeError(f"SBUF arena full allocating {name} ({nbytes}B), used={self.used()}")
        assert name not in self.live, name
        self.live[name] = (pos, pos + nbytes)
        self.n += 1
        return self.nc.alloc_sbuf_tensor_at(f"{name}_{self.n}", list(shape), dtype, offset=pos)

    def free(self, *names):
        self.pending.extend(names)

    def release(self):
        for n in self.pending:
            del self.live[n]
        self.pending = []

    def used(self):
        return sum(hi - lo for lo, hi in self.live.values())


class Prog:
    ENGS = ("pe", "act", "dve", "pool", "sp")
    NRING = 8

    def __init__(self):
        self.q = {e: [] for e in self.ENGS}
        self.ndma = {e: 0 for e in self.ENGS}
        self.last_compute = {e: None for e in self.ENGS}
        self.recent_dma = {e: [] for e in self.ENGS}
        self.cap = None

    def capture(self):
        self.cap = []

    def end_capture(self):
        l, self.cap = self.cap, None
        return l

    def replay(self, *lists):
        idx = [0] * len(lists)
        total = sum(len(l) for l in lists)
        for _ in range(total):
            best, bk = None, None
            for k, l in enumerate(lists):
                if idx[k] < len(l):
                    frac = idx[k] / len(l)
                    if best is None or frac < best:
                        best, bk = frac, k
            a = lists[bk][idx[bk]]
            idx[bk] += 1
            self.op(*a)

    def op(self, eng, emit, reads=(), writes=(), dma=False):
        if self.cap is not None:
            self.cap.append((eng, emit, list(reads), list(writes), dma))
            return None
        ev = Ev(eng, emit, dma)
        deps = set()
        xr = [t for t in reads if t.excl]
        if xr:
            reads = [t for t in reads if not t.excl]
            writes = list(writes) + xr
        for t in reads:
            deps.update(t.ws)
        for t in writes:
            if not (dma and t.ws and not t.rs and all(w_.dma for w_ in t.ws)):
                deps.update(t.ws)
            deps.update(t.rs.values())
        if dma:
            i = self.ndma[eng]
            self.ndma[eng] = i + 1
            ev.sem = (eng, i % self.NRING)
            ev.val = 16 * (i // self.NRING + 1)
            rd = self.recent_dma[eng]
            if len(rd) >= self.NRING:
                ev.pre = rd[-self.NRING]
            rd.append(ev)
            if len(rd) > 2 * self.NRING:
                del rd[0]
        else:
            self.last_compute[eng] = ev
        ev.deps = [d for d in deps
                   if not (d.eng == "pe" and eng == "pe" and not d.dma and not dma)]
        for t in reads:
            t.rs[("dma", id(ev)) if dma else eng] = ev
        for t in writes:
            if dma and t.ws and not t.rs and all(w_.dma for w_ in t.ws):
                t.ws = t.ws + [ev]
            else:
                t.ws = [ev]
            t.rs = {}
        self.q[eng].append(ev)
        return ev

    def barrier(self):
        deps = [e for e in self.last_compute.values() if e is not None]
        for eng in self.ENGS:
            deps.extend(self.recent_dma[eng][-self.NRING:])
        for eng in self.ENGS:
            ev = Ev(eng, None, False)
            ev.deps = list(deps)
            self.q[eng].append(ev)

    def finalize(self, sems):
        for eng in self.ENGS:
            for ev in self.q[eng]:
                for d in ev.deps:
                    d.needed = True
        for eng in self.ENGS:
            cnt = 0
            for ev in self.q[eng]:
                if ev.dma or ev.emit is None:
                    continue
                if ev.needed:
                    cnt += 1
                    ev.val = cnt
                    ev.sem = eng
        self.sems = sems

    def emit_engine(self, eng, handle):
        seen = {}
        sems = self.sems
        for ev in self.q[eng]:
            need = {}
            for d in ev.deps:
                if need.get(d.sem, 0) < d.val:
                    need[d.sem] = d.val
            if ev.pre is not None:
                d = ev.pre
                if need.get(d.sem, 0) < d.val:
                    need[d.sem] = d.val
            for s, v in need.items():
                if seen.get(s, 0) < v:
                    handle.wait_ge(sems[s], v)
                    seen[s] = v
            if ev.emit is None:
                continue
            inst = ev.emit(handle)
            if ev.dma:
                inst.then_inc(sems[ev.sem], 16)
            elif ev.needed:
                inst.then_inc(sems[ev.sem], 1)


class _Stop(Exception):
    pass


def build_nc(stop=None, dbg=None):
    nc = bass.Bass("TRN2", target_bir_lowering=False)
    DBG = {}

    def din(name, shape, dt=F32):
        return nc.dram_tensor(name, list(shape), dt, kind="ExternalInput").ap()

    x = din("x", [2, S, D])
    w_in = din("w_in", [D, 1728])
    wq = din("wq", [384, 768])
    wkv = din("wkv", [256, 1024])
    w_out = din("w_out", [D, D])
    w_up = din("w_up", [D, 2 * DFF])
    w_down = din("w_down", [DFF, D])
    cst_d = din("cst", [128, 384], BF16)
    cols_d = din("cols", [128, NCOLS])
    grows_d = din("grows", [3, 128, D])
    rope1_d = din("rope1", [2, 128, S])
    ropeg_d = din("ropeg", [128, 128])
    y = nc.dram_tensor("y", [2, S, D], F32, kind="ExternalOutput").ap()

    A = Arena(nc)
    P = Prog()
    PS2 = [nc.alloc_psum_tensor(f"ps{i}", [128, 1024], F32) for i in range(4)]
    T_ps = [Tile(excl=True) for _ in range(8)]

    def bank(b):
        return PS2[b // 2][:, (b % 2) * 512:(b % 2 + 1) * 512]

    rr = {}
    cur_pool = [tuple(range(8))]

    def nb():
        pool = cur_pool[0]
        k = rr.get(pool, 0)
        rr[pool] = k + 1
        return pool[k % len(pool)]

    cst = A.alloc("cst", [128, 384], BF16)
    cols = A.alloc("cols", [128, NCOLS], F32)
    ropeg = A.alloc("ropeg", [128, 128], F32)
    st = A.alloc("st", [128, 16], F32)
    T_cst, T_cols, T_ropeg = Tile(), Tile(), Tile()
    T_st = [Tile() for _ in range(4)]
    st2 = A.alloc("st2", [128, 48], F32)
    ident, ones, perm = cst[:, 0:128], cst[:, 128:256], cst[:, 256:384]
    P.op("sp", lambda e: e.dma_start(out=cst[:], in_=cst_d), writes=[T_cst], dma=True)
    P.op("sp", lambda e: e.dma_start(out=cols[:], in_=cols_d), writes=[T_cols], dma=True)
    P.op("sp", lambda e: e.dma_start(out=ropeg[:], in_=ropeg_d), writes=[T_ropeg], dma=True)

    def col(c):
        return cols[:, c:c + 1]

    stcnt = [0]

    def mm_group(out_ap, pairs, first=True, last=True):
        def f(e):
            n = len(pairs)
            inst = None
            for i, (l, r) in enumerate(pairs):
                inst = e.matmul(out_ap, lhsT=l, rhs=r, start=(first and i == 0), stop=(last and i == n - 1))
            return inst
        return f

    def rstd_cols(eng_src_ap, width, n_feat, Tsrc, Tdst_list, junk, T_junk):
        pass

    def make_uT(src_aps, src_tiles, grow, T_grow, uT, T_uT, tis, junk, T_junk, ubf, T_ubf, evac_eng, rstd=None):
        for idx, ti in enumerate(tis):
            src, Ts = src_aps[idx], src_tiles[idx]
            k = stcnt[0] % 4
            stcnt[0] += 1
            sl = (stcnt[0] - 1) % len(ubf)
            c0 = 4 * k
            if rstd is None:
                P.op("act", lambda e, src=src, c0=c0: e.activation(out=junk[:], in_=src, func=AF.Square,
                                                                     accum_out=st[:, c0:c0 + 1]),
                     reads=[Ts], writes=[T_junk, T_st[k]])
                P.op("act", lambda e, c0=c0: e.activation(out=st[:, c0 + 1:c0 + 2], in_=st[:, c0:c0 + 1], func=AF.Ln,
                                                          scale=1.0 / D, bias=EPS),
                     reads=[T_st[k]], writes=[T_st[k]])
                P.op("act", lambda e, c0=c0: e.activation(out=st[:, c0 + 2:c0 + 3], in_=st[:, c0 + 1:c0 + 2], func=AF.Exp,
                                                          scale=-0.5),
                     reads=[T_st[k]], writes=[T_st[k]])
                rs_ap, T_rs = st[:, c0 + 2:c0 + 3], T_st[k]
            else:
                rs_ap, T_rs = rstd[idx]
            P.op("dve", lambda e, src=src, rs_ap=rs_ap, sl=sl: e.scalar_tensor_tensor(
                out=ubf[sl][:], in0=src, scalar=rs_ap, in1=grow[:], op0=ALU.mult, op1=ALU.mult),
                reads=[Ts, T_rs, T_grow], writes=[T_ubf[sl]])
            b = nb()
            psT = bank(b).bitcast(BF16)

            def tr(e, sl=sl, psT=psT):
                inst = None
                for c in range(8):
                    inst = e.transpose(out=psT[:, c * 128:(c + 1) * 128], in_=ubf[sl][:, c * 128:(c + 1) * 128],
                                       identity=ident)
                return inst
            P.op("pe", tr, reads=[T_ubf[sl], T_cst], writes=[T_ps[b]])
            t0 = ti * 128
            if evac_eng == "act":
                P.op("act", lambda e, psT=psT, t0=t0: e.activation(
                    out=uT[:, :, t0:t0 + 128], in_=psT.rearrange("p (c t) -> p c t", c=8), func=AF.Copy),
                    reads=[T_ps[b]], writes=[T_uT[ti]])
            else:
                P.op("dve", lambda e, psT=psT, t0=t0: e.tensor_copy(
                    out=uT[:, :, t0:t0 + 128], in_=psT.rearrange("p (c t) -> p c t", c=8)),
                    reads=[T_ps[b]], writes=[T_uT[ti]])

    def bstats(sq_aps, T_sq, nfeat, lnbias, lnt, T_lnt, rdst, T_rdst):
        b = nb()
        P.op("pe", mm_group(bank(b), [(ones, a) for a in sq_aps]), reads=T_sq + [T_cst], writes=[T_ps[b]])
        P.op("act", lambda e, b=b: e.activation(out=lnt[:], in_=bank(b), func=AF.Ln, scale=1.0 / nfeat, bias=EPS),
             reads=[T_ps[b]], writes=[T_lnt])
        P.op("act", lambda e: e.activation(out=rdst[:], in_=lnt[:], func=AF.Exp, scale=-0.5, bias=float(lnbias)),
             reads=[T_lnt], writes=[T_rdst])

    def chk(s_, k_):
        if stop is not None and (s_, k_) == tuple(stop):
            raise _Stop()

    def seq(s):
        w_in_sb = A.alloc("w_in", [128, 8, 1920], BF16)
        uT = A.alloc("uT", [128, 8, S], BF16)
        T_win, T_uT = Tile(), [Tile() for _ in range(16)]
        w_in_v = w_in.rearrange("(c p) n -> p c n", p=128)
        P.op("pool", lambda e: e.dma_start(out=w_in_sb[:, :, 0:704], in_=w_in_v[:, :, 0:704]), writes=[T_win], dma=True)
        P.op("pool", lambda e: e.dma_start(out=w_in_sb[:, :, 704:768], in_=w_in_v[:, :, 640:704]), writes=[T_win], dma=True)
        P.op("pool", lambda e: e.dma_start(out=w_in_sb[:, :, 768:1792], in_=w_in_v[:, :, 704:1728]), writes=[T_win], dma=True)
        for dup in range(2):
            o_ = 1792 + dup * 64
            P.op("pool", lambda e, o_=o_: e.dma_start(out=w_in_sb[:, :, o_:o_ + 32], in_=w_in_v[:, :, 672:704]), writes=[T_win], dma=True)
            P.op("pool", lambda e, o_=o_: e.dma_start(out=w_in_sb[:, :, o_ + 32:o_ + 64], in_=w_in_v[:, :, 640:672]), writes=[T_win], dma=True)

        wq_sb = A.alloc("wq", [128, 3, 1024], BF16)
        wkv_sb = A.alloc("wkv", [128, 2, 1024], BF16)
        rope1 = A.alloc("rope1", [128, 2, S], F32)
        grow = A.alloc("grow", [128, D], F32)
        T_wq, T_wkv, T_rope1, T_grow = Tile(), Tile(), Tile(), Tile()
        wq_v = wq.rearrange("(c p) n -> p c n", p=128)
        P.op("pool", lambda e: e.dma_start(out=wq_sb[:, :, 0:768], in_=wq_v), writes=[T_wq], dma=True)
        for h_ in range(4):
            o_ = 768 + h_ * 64
            s_ = h_ * 192 + 128
            P.op("pool", lambda e, o_=o_, s_=s_: e.dma_start(out=wq_sb[:, :, o_:o_ + 32], in_=wq_v[:, :, s_ + 32:s_ + 64]),
                 writes=[T_wq], dma=True)
            P.op("pool", lambda e, o_=o_, s_=s_: e.dma_start(out=wq_sb[:, :, o_ + 32:o_ + 64], in_=wq_v[:, :, s_:s_ + 32]),
                 writes=[T_wq], dma=True)
        P.op("pool", lambda e: e.dma_start(out=wkv_sb[:], in_=wkv.rearrange("(c p) n -> p c n", p=128)), writes=[T_wkv], dma=True)
        P.op("sp", lambda e: e.dma_start(out=rope1[:], in_=rope1_d.rearrange("a p t -> p a t")), writes=[T_rope1], dma=True)
        P.op("sp", lambda e: e.dma_start(out=grow[:], in_=grows_d[0]), writes=[T_grow], dma=True)
        Qn = A.alloc("Qn", [128, 4, S], BF16)
        Qpe = A.alloc("Qpe", [128, 2, S], BF16)
        Kn = A.alloc("Kn", [128, 4, S], BF16)
        Kpl = A.alloc("Kpl", [128, S], BF16)
        Kph = A.alloc("Kph", [128, S], BF16)
        Vm = A.alloc("Vm", [128, 16, 512], BF16)
        T_Q = [Tile() for _ in range(NB)]
        DBG.update(uT=uT, Qn=Qn, Qpe=Qpe, Kn=Kn, Kpe=Kpl, Vm=Vm, w_in_sb=w_in_sb)
        T_K, T_V = Tile(), Tile()
        T_Kpe = Tile()
        P.op("pool", lambda e: e.memset(Kpl[64:128, :], 0.0), writes=[T_Kpe])
        P.op("pool", lambda e: e.memset(Kph[0:64, :], 0.0), writes=[T_Kpe])
        T_Qpe = [Tile() for _ in range(NB)]
        xt = [A.alloc(f"xt{i}", [128, D], F32) for i in range(2)]
        T_xt = [Tile(), Tile()]
        ubf = [A.alloc(f"ubf{i}", [128, D], BF16) for i in range(2)]
        T_ubf = [Tile(), Tile()]
        junk = A.alloc("junk", [128, D], BF16)
        T_junk = Tile()
        cqs = [A.alloc(f"cq{i}", [128, 3, 512], BF16) for i in range(2)]
        sqs = [A.alloc(f"sq{i}", [128, 3, 512], BF16) for i in range(2)]
        ckvs = [A.alloc(f"ckv{i}", [128, 2, 512], BF16) for i in range(2)]
        sqkvs = [A.alloc(f"sqkv{i}", [128, 2, 512], BF16) for i in range(2)]
        T_cqs, T_sqs, T_ckvs, T_sqkvs = ([Tile(), Tile()] for _ in range(4))
        lnt = A.alloc("lnt", [128, 512], F32)
        rq = A.alloc("rq", [128, 512], F32)
        rkv = A.alloc("rkv", [128, 512], F32)
        T_lnt, T_rq, T_rkv = Tile(), Tile(), Tile()
        rkc = A.alloc("rkc", [128, 8], F32)
        T_rkc = Tile()
        Aq = A.alloc("Aq", [128, 512], F32)
        Bq = A.alloc("Bq", [128, 512], F32)
        T_Aq, T_Bq = Tile(), Tile()
        t1s = [A.alloc("t10", [128, 512], F32)] * 2
        t2s = [A.alloc("t20", [128, 512], F32)] * 2
        T_t1s, T_t2s = [Tile()] * 2, [Tile()] * 2
        xcnt = [0]
        chn = [0]

        def front(n):
            cur_pool[0] = (0, 1, 2)
            blk = slice(n * 512, (n + 1) * 512)
            cq, sq, ckv, sqkv = cqs[n % 2], sqs[n % 2], ckvs[n % 2], sqkvs[n % 2]
            T_cq, T_sq, T_ckv, T_sqkv = T_cqs[n % 2], T_sqs[n % 2], T_ckvs[n % 2], T_sqkvs[n % 2]
            for i in range(4):
                ti = n * 4 + i
                sl = xcnt[0] % 2
                xcnt[0] += 1
                P.op("sp", lambda e, sl=sl, ti=ti: e.dma_start(out=xt[sl][:], in_=x[s, ti * 128:(ti + 1) * 128, :]),
                     writes=[T_xt[sl]], dma=True)
                make_uT([xt[sl][:]], [T_xt[sl]], grow, T_grow, uT, T_uT, [ti], junk, T_junk, ubf, T_ubf, "act")
            uT_blk = [uT[:, k, blk] for k in range(8)]
            T_uTb = [T_uT[n * 4 + i] for i in range(4)]
            for c in range(3):
                b = nb()
                P.op("pe", mm_group(bank(b), [(w_in_sb[:, k, c * 128:(c + 1) * 128], uT_blk[k]) for k in range(8)]),
                     reads=T_uTb + [T_win], writes=[T_ps[b]])
                P.op("dve", lambda e, b=b, c=c: e.tensor_scalar(out=cq[:, c, :], in0=bank(b), scalar1=col(C_GQ + c),
                                                                 scalar2=None, op0=ALU.mult),
                     reads=[T_ps[b], T_cols], writes=[T_cq])
                P.op("act", lambda e, b=b, c=c: e.activation(out=sq[:, c, :], in_=bank(b), func=AF.Square),
                     reads=[T_ps[b]], writes=[T_sq])
            for c in range(2):
                b = nb()
                P.op("pe", mm_group(bank(b), [(w_in_sb[:, k, 384 + c * 128:384 + (c + 1) * 128], uT_blk[k]) for k in range(8)]),
                     reads=T_uTb + [T_win], writes=[T_ps[b]])
                P.op("dve", lambda e, b=b, c=c: e.tensor_scalar(out=ckv[:, c, :], in0=bank(b), scalar1=col(C_GKV + c),
                                                                 scalar2=None, op0=ALU.mult),
                     reads=[T_ps[b], T_cols], writes=[T_ckv])
                P.op("act", lambda e, b=b, c=c: e.activation(out=sqkv[:, c, :], in_=bank(b), func=AF.Square),
                     reads=[T_ps[b]], writes=[T_sqkv])

        def back(n):
            cur_pool[0] = (3, 4, 5, 6, 7)
            blk = slice(n * 512, (n + 1) * 512)
            cq, sq, ckv, sqkv = cqs[n % 2], sqs[n % 2], ckvs[n % 2], sqkvs[n % 2]
            T_cq, T_sq, T_ckv, T_sqkv = T_cqs[n % 2], T_sqs[n % 2], T_ckvs[n % 2], T_sqkvs[n % 2]
            uT_blk = [uT[:, k, blk] for k in range(8)]
            T_uTb = [T_uT[n * 4 + i] for i in range(4)]
            bstats([sq[:, c, :] for c in range(3)], [T_sq], 384, np.log(192.0 ** -0.5), lnt, T_lnt, rq, T_rq)
            bstats([sqkv[:, c, :] for c in range(2)], [T_sqkv], 256, 0.0, lnt, T_lnt, rkv, T_rkv)
            b = nb()

            def colstats(e, b=b):
                inst = None
                for i in range(4):
                    for c in range(2):
                        inst = e.matmul(bank(b)[:, i:i + 1], lhsT=sqkv[:, c, i * 128:(i + 1) * 128], rhs=ones[:, 0:1],
                                        start=(c == 0), stop=(c == 1))
                return inst
            P.op("pe", colstats, reads=[T_sqkv, T_cst], writes=[T_ps[b]])
            P.op("act", lambda e, b=b: e.activation(out=rkc[:, 4:8], in_=bank(b)[:, 0:4], func=AF.Ln, scale=1.0 / 256, bias=EPS),
                 reads=[T_ps[b]], writes=[T_rkc])
            P.op("act", lambda e: e.activation(out=rkc[:, 0:4], in_=rkc[:, 4:8], func=AF.Exp, scale=-0.5),
                 reads=[T_rkc], writes=[T_rkc])
            t1, t2, T_t1, T_t2 = t1s[0], t2s[0], T_t1s[0], T_t2s[0]
            bk = nb()
            P.op("pe", mm_group(bank(bk), [(w_in_sb[:, k, 640:768], uT_blk[k]) for k in range(8)]),
                 reads=T_uTb + [T_win], writes=[T_ps[bk]])
            b2 = nb()
            P.op("pe", mm_group(bank(b2), [(w_in_sb[:, k, 1792:1920], uT_blk[k]) for k in range(8)]),
                 reads=T_uTb + [T_win], writes=[T_ps[b2]])
            P.op("dve", lambda e, bk=bk: e.tensor_tensor(out=t1[:], in0=bank(bk), in1=rope1[:, 0, blk], op=ALU.mult),
                 reads=[T_ps[bk], T_rope1], writes=[T_t1])
            P.op("dve", lambda e, b2=b2: e.tensor_tensor(out=t2[:], in0=bank(b2), in1=rope1[:, 1, blk], op=ALU.mult),
                 reads=[T_ps[b2], T_rope1], writes=[T_t2])
            P.op("dve", lambda e: e.tensor_tensor(out=Kpl[0:64, blk], in0=t1[0:64, :], in1=t2[0:64, :], op=ALU.add),
                 reads=[T_t1, T_t2], writes=[T_Kpe])
            P.op("dve", lambda e: e.tensor_tensor(out=Kph[64:128, blk], in0=t1[64:128, :], in1=t2[64:128, :], op=ALU.add),
                 reads=[T_t1, T_t2], writes=[T_Kpe])
            P.op("dve", lambda e: e.tensor_tensor(out=Aq[:], in0=rq[:], in1=rope1[:, 0, blk], op=ALU.mult),
                 reads=[T_rq, T_rope1], writes=[T_Aq])
            P.op("dve", lambda e: e.tensor_tensor(out=Bq[:], in0=rq[:], in1=rope1[:, 1, blk], op=ALU.mult),
                 reads=[T_rq, T_rope1], writes=[T_Bq])
            for h in range(4):
                b = nb()
                P.op("pe", mm_group(bank(b), [(wq_sb[:, c, h * 192:h * 192 + 128], cq[:, c, :]) for c in range(3)]),
                     reads=[T_cq, T_wq], writes=[T_ps[b]])
                P.op("dve", lambda e, b=b, h=h: e.tensor_tensor(out=Qn[:, h, blk], in0=bank(b), in1=rq[:], op=ALU.mult),
                     reads=[T_ps[b], T_rq], writes=[T_Q[n]])
            for pr in range(2):
                b = nb()
                b2 = nb()

                def qpe_mm(e, b=b, b2=b2, pr=pr):
                    inst = None
                    for half in range(2):
                        h = 2 * pr + half
                        for c in range(3):
                            inst = e.matmul(bank(b)[half * 64:(half + 1) * 64, :],
                                            lhsT=wq_sb[:, c, h * 192 + 128:h * 192 + 192], rhs=cq[:, c, :],
                                            start=(c == 0), stop=(c == 2))
                    for half in range(2):
                        h = 2 * pr + half
                        for c in range(3):
                            inst = e.matmul(bank(b2)[half * 64:(half + 1) * 64, :],
                                            lhsT=wq_sb[:, c, 768 + h * 64:768 + (h + 1) * 64], rhs=cq[:, c, :],
                                            start=(c == 0), stop=(c == 2))
                    return inst
                P.op("pe", qpe_mm, reads=[T_cq, T_wq], writes=[T_ps[b], T_ps[b2]])
                P.op("dve", lambda e, b=b: e.tensor_tensor(out=t1s[0][:], in0=bank(b), in1=Aq[:], op=ALU.mult),
                     reads=[T_ps[b], T_Aq], writes=[T_t1s[0]])
                P.op("dve", lambda e, b2=b2: e.tensor_tensor(out=t2s[0][:], in0=bank(b2), in1=Bq[:], op=ALU.mult),
                     reads=[T_ps[b2], T_Bq], writes=[T_t2s[0]])
                P.op("dve", lambda e, pr=pr: e.tensor_tensor(out=Qpe[:, pr, blk], in0=t1s[0][:], in1=t2s[0][:], op=ALU.add),
                     reads=[T_t1s[0], T_t2s[0]], writes=[T_Qpe[n]])
            for h in range(4):
                b = nb()
                P.op("pe", mm_group(bank(b), [(wkv_sb[:, c, h * 256:h * 256 + 128], ckv[:, c, :]) for c in range(2)]),
                     reads=[T_ckv, T_wkv], writes=[T_ps[b]])
                P.op("dve", lambda e, b=b, h=h: e.tensor_tensor(out=Kn[:, h, blk], in0=bank(b), in1=rkv[:], op=ALU.mult),
                     reads=[T_ps[b], T_rkv], writes=[T_K])
            for i in range(4):
                b = nb()
                vr = [wkv_sb[:, c, :].rearrange("p (h two d) -> p h two d", h=4, two=2)[:, :, 1, :] for c in range(2)]
                P.op("pe", mm_group(bank(b), [(ckv[:, c, i * 128:(i + 1) * 128], vr[c]) for c in range(2)]),
                     reads=[T_ckv, T_wkv], writes=[T_ps[b]])
                P.op("act", lambda e, b=b, i=i: e.activation(out=Vm[:, n * 4 + i, :], in_=bank(b), func=AF.Copy,
                                                             scale=rkc[:, i:i + 1]),
                     reads=[T_ps[b], T_rkc], writes=[T_V])

        P.capture()
        front(0)
        P.replay(P.end_capture())
        for n in range(NB):
            P.capture()
            back(n)
            bl = P.end_capture()
            fl = []
            if n + 1 < NB:
                P.capture()
                front(n + 1)
                fl = P.end_capture()
            P.replay(bl, fl)
        cur_pool[0] = tuple(range(8))
        A.free("wq", "wkv", "rope1", "grow", "xt0", "xt1", "ubf0", "ubf1", "junk", "cq0", "cq1", "sq0", "sq1",
               "ckv0", "ckv1", "sqkv0", "sqkv1", "lnt", "rq", "rkv", "rkc", "Aq", "Bq",
               "t10", "t20")
        P.barrier()
        A.release()
        chk(s, 1)

        def attention(score_pairs, v_of, gg_col0, o_dst, T_o_dst, wfn, extra_reads, recip_dve):
            PT = [A.alloc(f"PT{i}", [128, 1024], BF16) for i in range(4)]
            T_PT = [Tile() for _ in range(4)]
            on = A.alloc("on", [128, 4, 512], F32)
            osq = A.alloc("osq", [128, 4, 512], BF16)
            rinv = A.alloc("rinv", [128, 512], F32)
            glt = A.alloc("glt", [128, 512], F32)
            grs = A.alloc("grs", [128, 512], F32)
            T_on = [Tile() for _ in range(4)]
            T_osq = [Tile() for _ in range(4)]
            T_rinv, T_glt, T_grs = Tile(), Tile(), Tile()
            TS = [A.alloc(f"TS{i}", [128, 512], BF16) for i in range(4)]
            T_TS = [Tile() for _ in range(4)]
            steps = []
            WPOS = [(2, 0), (2, 4), (3, 0), (3, 4)]
            for n in range(NB):
                for h in range(4):
                    for k2 in range(8):
                        steps.append(("a", n, h, k2))
            later = []

            def S_op(i):
                sl = i % 2
                if steps[i][0] == "w":
                    wfn["pe"](steps[i][1], steps[i][2], sl)
                    return
                _, n, h, k2 = steps[i]

                def f(e):
                    inst = None
                    for j in range(2):
                        prs = score_pairs(n, h, 2 * k2 + j)
                        for m, (l, r) in enumerate(prs):
                            inst = e.matmul(PS2[sl][:, j * 512:(j + 1) * 512], lhsT=l, rhs=r,
                                            start=(m == 0), stop=(m == len(prs) - 1))
                    return inst
                P.op("pe", f, reads=[T_Q[n], T_K] + extra_reads(n), writes=[T_ps[2 * sl], T_ps[2 * sl + 1]])

            def EXP_op(i):
                sl = i % 2
                if steps[i][0] == "w":
                    wfn["post"](steps[i][1], steps[i][2], sl)
                    pass
                    return
                P.op("act", lambda e: e.activation(out=PT[i % 4][:], in_=PS2[sl][:], func=AF.Exp),
                     reads=[T_ps[2 * sl], T_ps[2 * sl + 1]], writes=[T_PT[i % 4]])

            def PV_op(i):
                if steps[i][0] == "w":
                    return
                _, n, h, k2 = steps[i]
                par = h % 2
                ob = 4 + par
                P.op("dve", lambda e: e.tensor_tensor(out=TS[i % 4][:], in0=PT[i % 4][:, 0:512], in1=PT[i % 4][:, 512:1024],
                                                      op=ALU.add), reads=[T_PT[i % 4]], writes=[T_TS[i % 4]])

                def f(e):
                    inst = None
                    for j in range(2):
                        kc = 2 * k2 + j
                        inst = e.matmul(bank(ob), lhsT=v_of(h, kc), rhs=PT[i % 4][:, j * 512:(j + 1) * 512],
                                        start=(kc == 0), stop=(kc == 15))
                    return inst
                P.op("pe", f, reads=[T_PT[i % 4], T_V], writes=[T_ps[ob]])

            def SUM_op(i):
                if steps[i][0] == "w":
                    return
                _, n, h, k2 = steps[i]
                sb = 6 + h % 2
                P.op("pe", lambda e: e.matmul(bank(sb), lhsT=ones, rhs=TS[i % 4][:], start=(k2 == 0), stop=(k2 == 7)),
                     reads=[T_TS[i % 4], T_cst], writes=[T_ps[sb]])

            def finalize_head(i):
                _, n, h, k2 = steps[i]
                par = h % 2
                ob, sb = 4 + par, 6 + par
                def mult():
                    P.op("dve", lambda e: e.tensor_tensor(out=on[:, h, :], in0=bank(ob), in1=rinv[:], op=ALU.mult),
                         reads=[T_ps[ob], T_rinv], writes=[T_on[h]])
                if recip_dve:
                    def chunk(c):
                        P.op("dve", lambda e: e.reciprocal(out=rinv[:, c * 128:(c + 1) * 128], in_=bank(sb)[:, c * 128:(c + 1) * 128]),
                             reads=[T_ps[sb]], writes=[T_rinv])
                    chunk(0)
                    chunk(1)
                    later.append((i + 2, lambda: chunk(2)))
                    later.append((i + 2, lambda: chunk(3)))
                    later.append((i + 3, mult))
                    d0 = 3
                else:
                    def recip_act():
                        P.op("act", lambda e: e.activation(out=glt[:], in_=bank(sb), func=AF.Ln), reads=[T_ps[sb]], writes=[T_glt])
                        P.op("act", lambda e: e.activation(out=rinv[:], in_=glt[:], func=AF.Exp, scale=-1.0),
                             reads=[T_glt], writes=[T_rinv])
                    later.append((i + 3, recip_act))
                    later.append((i + 4, mult))
                    d0 = 3
                later.append((i + 3 + d0, lambda: P.op(
                    "act", lambda e: e.activation(out=osq[:, h, :], in_=on[:, h, :], func=AF.Square),
                    reads=[T_on[h]], writes=[T_osq[h]])))
                if h == 3:
                    def gss_pe():
                        P.op("pe", mm_group(bank(7), [(ones, osq[:, hh, :]) for hh in range(4)]),
                             reads=T_osq + [T_cst], writes=[T_ps[7]])

                    def gss_act():
                        P.op("act", lambda e: e.activation(out=glt[:], in_=bank(7), func=AF.Ln, scale=1.0 / 512, bias=EPS),
                             reads=[T_ps[7]], writes=[T_glt])
                        P.op("act", lambda e: e.activation(out=grs[:], in_=glt[:], func=AF.Exp, scale=-0.5),
                             reads=[T_glt], writes=[T_grs])

                    def gss_dve():
                        for hh in range(4):
                            P.op("dve", lambda e, hh=hh: e.scalar_tensor_tensor(
                                out=o_dst(n, hh), in0=on[:, hh, :], scalar=col(gg_col0 + hh), in1=grs[:],
                                op0=ALU.mult, op1=ALU.mult),
                                reads=[T_on[hh], T_cols, T_grs], writes=[T_o_dst(n)])
                    later.append((i + 4 + d0, gss_pe))
                    later.append((i + 6 + d0, gss_act))
                    later.append((i + 7 + d0, gss_dve))

            def run_due(i):
                due = [x for x in later if x[0] <= i]
                for x in due:
                    later.remove(x)
                    x[1]()

            S_op(0)
            for i in range(len(steps)):
                if i + 1 < len(steps):
                    S_op(i + 1)
                EXP_op(i)
                if i >= 2:
                    SUM_op(i - 2)
                PV_op(i)
                if wfn is not None:
                    _, n_, h_, k2_ = steps[i]
                    if n_ >= 1 and h_ >= 1:
                        if k2_ == 0:
                            for t_ in sorted(set(((h_ - 1) * 3 + d) // 2 for d in range(3) if (h_ - 1) * 3 + d < 8)):
                                wfn["load"](n_ - 1, t_)
                        if k2_ in (4, 5, 6):
                            opn = (h_ - 1) * 3 + (k2_ - 4)
                            if opn < 8:
                                q_ = 1 - h_ % 2
                                bk_ = (4 + q_) if (k2_ - 4) % 2 == 0 else (6 + q_)
                                wfn["pe"](n_ - 1, opn // 2, opn % 2, bk_)
                                later.append((i + 1, lambda m=n_ - 1, t=opn // 2, f=opn % 2, bk=bk_: wfn["post"](m, t, f, bk)))
                                if opn % 2 == 1:
                                    later.append((i + 4, lambda m=n_ - 1, t=opn // 2: wfn["stats"](m, t)))
                run_due(i)
                if steps[i][0] == "a" and steps[i][3] == 7:
                    later.append((i + 2, lambda i=i: finalize_head(i)))
            SUM_op(len(steps) - 2)
            SUM_op(len(steps) - 1)
            while later:
                later.sort(key=lambda x: x[0])
                x = later.pop(0)
                x[1]()
            if wfn is not None:
                for ii in range(4):
                    wfn["load"](NB - 1, ii)
                for ii in range(4):
                    for fh in range(2):
                        bk_ = 4 + (2 * ii + fh) % 4
                        wfn["pe"](NB - 1, ii, fh, bk_)
                        wfn["post"](NB - 1, ii, fh, bk_)
                    wfn["stats"](NB - 1, ii)
            A.free("PT0", "PT1", "PT2", "on", "osq", "rinv", "glt", "grs", "TS0", "TS1", "TS2", "TS3", "PT3")

        o_mla = A.alloc("o_mla", [128, 4, S], BF16)
        T_omla = [Tile() for _ in range(NB)]
        DBG.update(o_mla=o_mla)

        def mla_pairs(n, h, kc):
            blk = slice(n * 512, (n + 1) * 512)
            ks = slice(kc * 128, (kc + 1) * 128)
            kp = Kpl if h % 2 == 0 else Kph
            return [(Kn[:, h, ks], Qn[:, h, blk]), (kp[:, ks], Qpe[:, h // 2, blk])]

        attention(mla_pairs, lambda h, kc: Vm[:, kc, h * 128:(h + 1) * 128], C_GGM,
                  lambda n, hh: o_mla[:, hh, n * 512:(n + 1) * 512], lambda n: T_omla[n], None,
                  lambda n: [T_Qpe[n], T_Kpe], False)
        A.free("Qn", "Qpe", "Kn", "Kpl", "Kph", "Vm")
        P.barrier()
        A.release()
        chk(s, 2)

        Qg = A.alloc("Qg", [128, 4, S], BF16)
        Kg = A.alloc("Kg", [128, 2, S], BF16)
        Vg = A.alloc("Vg", [128, 16, 256], BF16)
        DBG.update(Qg=Qg, Kg=Kg, Vg=Vg)
        T_Q = [Tile() for _ in range(NB)]
        T_K, T_V = Tile(), Tile()
        w_out_sb = A.alloc("w_out", [128, 8, D], BF16)
        T_wout = Tile()
        P.op("pool", lambda e: e.dma_start(out=w_out_sb[:], in_=w_out.rearrange("(c p) n -> p c n", p=128)),
             writes=[T_wout], dma=True)
        qg = [A.alloc(f"qg{i}", [128, 512], BF16) for i in range(2)]
        sqg = [A.alloc(f"sqg{i}", [128, 512], BF16) for i in range(2)]
        lng = [A.alloc(f"lng{i}", [128, 512], F32) for i in range(2)]
        rg = [A.alloc(f"rg{i}", [128, 512], F32) for i in range(2)]
        g1 = [A.alloc(f"g1{i}", [128, 512], F32) for i in range(2)]
        g2 = [A.alloc(f"g2{i}", [128, 512], F32) for i in range(2)]
        T_qg, T_sqg, T_lng, T_rg, T_g1, T_g2 = ([Tile(), Tile()] for _ in range(6))
        rstep = ropeg[:].ap[0][0]
        hcnt = 0
        items = []
        for n in range(NB):
            blk = slice(n * 512, (n + 1) * 512)
            uT_blk = [uT[:, k, blk] for k in range(8)]
            T_uTb = [T_uT[n * 4 + i] for i in range(4)]
            cc_top = bass.AP(ropeg, 0 * rstep + n * 8, [[rstep, 64], [1, 8], [0, 64]])
            ss_top = bass.AP(ropeg, 0 * rstep + 64 + n * 8, [[rstep, 64], [1, 8], [0, 64]])
            cc_bot = bass.AP(ropeg, 64 * rstep, [[rstep, 64], [0, 8], [1, 64]])
            ss_bot = bass.AP(ropeg, 64 * rstep + 64, [[rstep, 64], [0, 8], [1, 64]])

            def v3(ap):
                return ap.rearrange("p (a b) -> p a b", a=8)
            for hd in range(6):
                isq = hd < 4
                c0 = 768 + hd * 128 if isq else 1280 + (hd - 4) * 128
                gcol = C_GQQ if isq else C_GQK
                dst = Qg[:, hd, blk] if isq else Kg[:, hd - 4, blk]
                Tdst = T_Q[n] if isq else T_K
                lnb = np.log(128.0 ** -0.5) if isq else 0.0
                u = hcnt % 2
                hcnt += 1
                P.capture()
                b = nb()
                P.op("pe", mm_group(bank(b), [(w_in_sb[:, k, c0:c0 + 128], uT_blk[k]) for k in range(8)]),
                     reads=T_uTb + [T_win], writes=[T_ps[b]])
                P.op("act", lambda e, b=b, u=u, gcol=gcol: e.activation(out=qg[u][:], in_=bank(b), func=AF.Copy, scale=col(gcol)),
                     reads=[T_ps[b], T_cols], writes=[T_qg[u]])
                P.op("act", lambda e, b=b, u=u: e.activation(out=sqg[u][:], in_=bank(b), func=AF.Square),
                     reads=[T_ps[b]], writes=[T_sqg[u]])
                P.op("dve", lambda e, b=b, u=u, gcol=gcol, cc_top=cc_top: e.scalar_tensor_tensor(
                    out=v3(g1[u][0:64, :]), in0=v3(bank(b)[0:64, :]), scalar=cols[0:64, gcol:gcol + 1], in1=cc_top,
                    op0=ALU.mult, op1=ALU.mult), reads=[T_ps[b], T_cols, T_ropeg], writes=[T_g1[u]])
                P.op("dve", lambda e, b=b, u=u, gcol=gcol, cc_bot=cc_bot: e.scalar_tensor_tensor(
                    out=v3(g1[u][64:128, :]), in0=v3(bank(b)[64:128, :]), scalar=cols[64:128, gcol:gcol + 1], in1=cc_bot,
                    op0=ALU.mult, op1=ALU.mult), reads=[T_ps[b], T_cols, T_ropeg], writes=[T_g1[u]])
                bs = nb()
                P.op("pe", mm_group(bank(bs), [(ones, sqg[u][:])]), reads=[T_sqg[u], T_cst], writes=[T_ps[bs]])
                bw = nb()
                P.op("pe", mm_group(bank(bw), [(perm, qg[u][:])]), reads=[T_qg[u], T_cst], writes=[T_ps[bw]])
                P.op("act", lambda e, bs=bs, u=u: e.activation(out=lng[u][:], in_=bank(bs), func=AF.Ln, scale=1.0 / 128, bias=EPS),
                     reads=[T_ps[bs]], writes=[T_lng[u]])
                P.op("act", lambda e, u=u, lnb=lnb: e.activation(out=rg[u][:], in_=lng[u][:], func=AF.Exp, scale=-0.5, bias=float(lnb)),
                     reads=[T_lng[u]], writes=[T_rg[u]])
                P.op("dve", lambda e, bw=bw, u=u, ss_top=ss_top: e.tensor_tensor(
                    out=v3(g2[u][0:64, :]), in0=v3(bank(bw)[0:64, :]), in1=ss_top, op=ALU.mult),
                    reads=[T_ps[bw], T_ropeg], writes=[T_g2[u]])
                P.op("dve", lambda e, bw=bw, u=u, ss_bot=ss_bot: e.tensor_tensor(
                    out=v3(g2[u][64:128, :]), in0=v3(bank(bw)[64:128, :]), in1=ss_bot, op=ALU.mult),
                    reads=[T_ps[bw], T_ropeg], writes=[T_g2[u]])
                P.op("pool", lambda e, u=u: e.tensor_tensor(out=g1[u][:], in0=g1[u][:], in1=g2[u][:], op=ALU.add),
                     reads=[T_g2[u]], writes=[T_g1[u]])
                P.op("pool", lambda e, u=u, dst=dst: e.tensor_tensor(out=dst, in0=g1[u][:], in1=rg[u][:], op=ALU.mult),
                     reads=[T_g1[u], T_rg[u]], writes=[Tdst])
                L = P.end_capture()
                items.append((L[:5], L[5:]))
            for i in range(4):
                P.capture()
                b = nb()
                t0 = (n * 4 + i) * 128
                P.op("pe", mm_group(bank(b)[:, 0:256], [(uT[:, k, t0:t0 + 128], w_in_sb[:, k, 1536:1792]) for k in range(8)]),
                     reads=[T_uT[n * 4 + i], T_win], writes=[T_ps[b]])
                P.op("act", lambda e, b=b, i=i, n=n: e.activation(out=Vg[:, n * 4 + i, :], in_=bank(b)[:, 0:256], func=AF.Copy),
                     reads=[T_ps[b]], writes=[T_V])
                items[-1 - i % 2] = (items[-1 - i % 2][0], items[-1 - i % 2][1] + P.end_capture())
        for i in range(len(items)):
            P.replay(items[i][0])
            P.replay(items[i][1])
        A.free("w_in", "uT", "qg0", "qg1", "sqg0", "sqg1", "lng0", "lng1", "rg0", "rg1", "g10", "g11", "g20", "g21")
        P.barrier()
        A.release()
        chk(s, 3)

        hres = A.alloc("h", [128, 16, D], F32)
        T_h = [Tile() for _ in range(16)]
        DBG.update(h=hres)
        og = A.alloc("og", [128, 4, 512], BF16)
        T_og = Tile()
        xr = [A.alloc(f"xr{i}", [128, D], F32) for i in range(4)]
        T_xr = [Tile() for _ in range(4)]
        junk2 = A.alloc("junk2", [128, D], BF16)
        T_junk2 = Tile()
        T_ss2 = [Tile() for _ in range(16)]
        T_r2 = Tile()

        def w_load(n, i):
            ti = n * 4 + i
            t0 = ti * 128
            P.op("sp", lambda e: e.dma_start(out=xr[i][:], in_=x[s, t0:t0 + 128, :]), writes=[T_xr[i]], dma=True)

        def w_pe(n, i, fh, bk):
            ti = n * 4 + i
            t0 = ti * 128
            pairs = [(o_mla[:, c, t0:t0 + 128], w_out_sb[:, c, fh * 512:(fh + 1) * 512]) for c in range(4)]
            pairs += [(og[:, c, i * 128:(i + 1) * 128], w_out_sb[:, 4 + c, fh * 512:(fh + 1) * 512]) for c in range(4)]
            P.op("pe", mm_group(bank(bk), pairs), reads=[T_omla[n], T_og, T_wout], writes=[T_ps[bk]])

        def w_post(n, i, fh, bk):
            ti = n * 4 + i
            P.op("dve", lambda e: e.tensor_tensor(out=hres[:, ti, fh * 512:(fh + 1) * 512], in0=bank(bk),
                                                  in1=xr[i][:, fh * 512:(fh + 1) * 512], op=ALU.add),
                 reads=[T_ps[bk], T_xr[i]], writes=[T_h[ti]])

        def w_stats(n, i):
            ti = n * 4 + i
            P.op("act", lambda e: e.activation(out=junk2[:], in_=hres[:, ti, :], func=AF.Square, accum_out=st2[:, ti:ti + 1]),
                 reads=[T_h[ti]], writes=[T_junk2, T_ss2[ti]])

        def gqa_pairs(n, h, kc):
            return [(Kg[:, h // 2, kc * 128:(kc + 1) * 128], Qg[:, h, n * 512:(n + 1) * 512])]

        attention(gqa_pairs, lambda h, kc: Vg[:, kc, (h // 2) * 128:(h // 2 + 1) * 128], C_GGG,
                  lambda n, hh: og[:, hh, :], lambda n: T_og, {"pe": w_pe, "post": w_post, "stats": w_stats, "load": w_load}, lambda n: [], False)
        A.free("Qg", "Kg", "Vg", "w_out", "o_mla", "og", "xr0", "xr1", "xr2", "xr3", "junk2")
        P.barrier()
        A.release()
        chk(s, 4)

        u2T = A.alloc("u2T", [128, 8, S], BF16)
        T_u2T = [Tile() for _ in range(16)]
        DBG.update(u2T=u2T)
        DBGP3 = True
        grow2 = A.alloc("grow2", [128, D], F32)
        T_grow2 = Tile()
        P.op("sp", lambda e: e.dma_start(out=grow2[:], in_=grows_d[1]), writes=[T_grow2], dma=True)
        ubf = [A.alloc(f"ubf{i}", [128, D], BF16) for i in range(2)]
        T_ubf = [Tile() for _ in range(2)]
        junk = A.alloc("junk", [128, D], BF16)
        T_junk = Tile()
        NWU = 3
        wg = [A.alloc(f"wg{i}", [128, 8, 128], BF16) for i in range(NWU)]
        wv = [A.alloc(f"wv{i}", [128, 8, 128], BF16) for i in range(NWU)]
        T_wu = [Tile() for _ in range(NWU)]
        NWD = 8
        wd = [A.alloc(f"wd{i}", [128, D], BF16) for i in range(NWD)]
        T_wd = [Tile() for _ in range(NWD)]
        actT = [A.alloc(f"actT{i}", [128, S], BF16) for i in range(6)]
        T_actT = [Tile() for _ in range(6)]
        hug = [A.alloc(f"hug{i}", [128, S + 2], BF16) for i in range(2)]
        huv = [A.alloc(f"huv{i}", [128, S + 2], F32) for i in range(2)]
        cvs = [A.alloc(f"cvs{i}", [128, 512], F32) for i in range(2)]
        T_cv = [Tile(), Tile()]
        T_hu = [[Tile() for _ in range(NB)] for _ in range(2)]
        T_hv = [[Tile() for _ in range(NB)] for _ in range(2)]
        dg = [A.alloc(f"dg{i}", [128, 3, 128], BF16) for i in range(2)]
        T_dg = [Tile(), Tile()]
        sg = [A.alloc(f"sg{i}", [128, 512], BF16) for i in range(2)]
        T_sg = [Tile(), Tile()]
        DBG.update(actT0=actT[0], actT4=actT[4], hug0=hug[0], hug1=hug[1], huv0=huv[0], huv1=huv[1])
        for u in range(2):
            for buf, TT in ((hug[u], T_hu[u]), (huv[u], T_hv[u])):
                P.op("pool", lambda e, buf=buf: e.memset(buf[:, 0:1], 0.0), writes=[TT[0]])
                P.op("pool", lambda e, buf=buf: e.memset(buf[:, S + 1:S + 2], 0.0), writes=[TT[NB - 1]])
        w_up_v = w_up.rearrange("(c p) n -> p c n", p=128)
        jlist = []
        for gi, gsz in enumerate(GROUPS):
            j0 = sum(GROUPS[:gi])
            jlist.append(list(range(j0, j0 + gsz)))

        def load_w(j):
            su, sd = j % NWU, j % NWD
            P.op("pool", lambda e: e.dma_start(out=wg[su][:], in_=w_up_v[:, :, j * 128:(j + 1) * 128]), writes=[T_wu[su]], dma=True)
            P.op("pool", lambda e: e.dma_start(out=wv[su][:], in_=w_up_v[:, :, DFF + j * 128:DFF + (j + 1) * 128]),
                 writes=[T_wu[su]], dma=True)
            P.op("pool", lambda e: e.dma_start(out=wd[sd][:], in_=w_down[j * 128:(j + 1) * 128, :]), writes=[T_wd[sd]], dma=True)

        def dg_build(j):
            u = j % 2
            for k in range(3):
                P.op("act", lambda e, k=k: e.activation(out=dg[u][:, k, :], in_=ident, func=AF.Copy, scale=col(C_CW + k * 44 + j)),
                     reads=[T_cst, T_cols], writes=[T_dg[u]])

        def hu_block(j, n):
            su, u = j % NWU, j % 2
            blk = slice(n * 512, (n + 1) * 512)
            rd = [T_u2T[n * 4 + i] for i in range(4)] + [T_wu[su]]
            b = n % 2
            P.op("pe", mm_group(bank(b), [(wg[su][:, k, :], u2T[:, k, blk]) for k in range(8)]), reads=rd, writes=[T_ps[b]])
            P.op("act", lambda e, b=b: e.activation(out=hug[u][:, 1 + n * 512:1 + (n + 1) * 512], in_=bank(b), func=AF.Copy),
                 reads=[T_ps[b]], writes=[T_hu[u][n]])
            b = (2, 3, 6, 7)[n % 4]
            P.op("pe", mm_group(bank(b), [(wv[su][:, k, :], u2T[:, k, blk]) for k in range(8)]), reads=rd, writes=[T_ps[b]])
            P.op("act", lambda e, b=b: e.activation(out=huv[u][:, 1 + n * 512:1 + (n + 1) * 512], in_=bank(b), func=AF.Copy),
                 reads=[T_ps[b]], writes=[T_hv[u][n]])

        def hu_ops(j):
            dg_build(j)
            for n in range(NB):
                hu_block(j, n)

        def conv_ops(j, jj):
            u = j % 2
            cw0, cw1, cw2, cbv = (col(C_CW + k * 44 + 22 + j) for k in range(3)), None, None, col(C_CB + 22 + j)
            cw0, cw1, cw2 = col(C_CW + 22 + j), col(C_CW + 44 + 22 + j), col(C_CW + 88 + 22 + j)
            for n in range(NB):
                nbrs = [T_hu[u][m] for m in (n - 1, n, n + 1) if 0 <= m < NB]
                nbrv = [T_hv[u][m] for m in (n - 1, n, n + 1) if 0 <= m < NB]
                bg = 4 + n % 2
                v = n % 2
                o0 = n * 512
                P.op("pe", mm_group(bank(bg), [(dg[u][:, k, :], hug[u][:, o0 + k:o0 + k + 512]) for k in range(3)]),
                     reads=nbrs + [T_dg[u]], writes=[T_ps[bg]])
                P.op("dve", lambda e, v=v, o0=o0: e.tensor_scalar(out=cvs[v][:], in0=huv[u][:, o0 + 1:o0 + 513], scalar1=cw1,
                                                                 scalar2=cbv, op0=ALU.mult, op1=ALU.add),
                     reads=nbrv + [T_cols], writes=[T_cv[v]])
                P.op("dve", lambda e, v=v, o0=o0: e.scalar_tensor_tensor(out=cvs[v][:], in0=huv[u][:, o0:o0 + 512], scalar=cw0,
                                                                        in1=cvs[v][:], op0=ALU.mult, op1=ALU.add),
                     reads=nbrv + [T_cols], writes=[T_cv[v]])
                P.op("dve", lambda e, v=v, o0=o0: e.scalar_tensor_tensor(out=cvs[v][:], in0=huv[u][:, o0 + 2:o0 + 514], scalar=cw2,
                                                                        in1=cvs[v][:], op0=ALU.mult, op1=ALU.add),
                     reads=nbrv + [T_cols], writes=[T_cv[v]])
                P.op("act", lambda e, bg=bg, v=v: e.activation(out=sg[v][:], in_=bank(bg), func=AF.Silu, bias=col(C_CB + j)),
                     reads=[T_ps[bg], T_cols], writes=[T_sg[v]])
                P.op("dve", lambda e, v=v, o0=o0: e.tensor_tensor(out=actT[jj][:, o0:o0 + 512], in0=cvs[v][:], in1=sg[v][:], op=ALU.mult),
                     reads=[T_cv[v], T_sg[v]], writes=[T_actT[jj]])

        yt = None
        allj = [j for g in jlist for j in g]
        load_w(0)
        load_w(1)
        dg_build(0)
        dg_build(1)
        P.op("act", lambda e: e.activation(out=st2[:, 16:32], in_=st2[:, 0:16], func=AF.Ln, scale=1.0 / D, bias=EPS),
             reads=T_ss2, writes=[T_r2])
        P.op("act", lambda e: e.activation(out=st2[:, 32:48], in_=st2[:, 16:32], func=AF.Exp, scale=-0.5),
             reads=[T_r2], writes=[T_r2])
        for n in range(NB):
            for i in range(4):
                ti = n * 4 + i
                make_uT([hres[:, ti, :]], [T_h[ti]], grow2, T_grow2, u2T, T_u2T, [ti], junk, T_junk, ubf, T_ubf,
                        "act", rstd=[(st2[:, 32 + ti:33 + ti], T_r2)])
            hu_block(0, n)
            hu_block(1, n)
        def down_ops(gi):
            js = jlist[gi]
            lastg = gi == len(jlist) - 1
            if lastg:
                growf = A.alloc("growf", [128, D], F32)
                T_growf = Tile()
                P.op("sp", lambda e: e.dma_start(out=growf[:], in_=grows_d[2]), writes=[T_growf], dma=True)
                yt = [A.alloc(f"yt{i}", [128, D], F32) for i in range(2)]
                T_yt = [Tile(), Tile()]
            fin = []
            for ti in range(16):
                t0 = ti * 128
                for fh in range(2):
                    b = (2 * ti + fh) % 4
                    P.op("pe", mm_group(bank(b), [(actT[jj][:, t0:t0 + 128], wd[j % NWD][:, fh * 512:(fh + 1) * 512])
                                                  for jj, j in enumerate(js)]),
                         reads=[T_actT[jj] for jj in range(len(js))] + [T_wd[j % NWD] for j in js], writes=[T_ps[b]])
                    P.op("dve", lambda e, b=b, ti=ti, fh=fh: e.tensor_tensor(
                        out=hres[:, ti, fh * 512:(fh + 1) * 512], in0=bank(b), in1=hres[:, ti, fh * 512:(fh + 1) * 512], op=ALU.add),
                        reads=[T_ps[b], T_h[ti]], writes=[T_h[ti]])
                if lastg:
                    k = stcnt[0] % 4
                    stcnt[0] += 1
                    c0 = 4 * k
                    sl = ti % 2
                    P.op("act", lambda e, ti=ti, c0=c0: e.activation(out=junk[:], in_=hres[:, ti, :], func=AF.Square,
                                                                    accum_out=st[:, c0:c0 + 1]),
                         reads=[T_h[ti]], writes=[T_junk, T_st[k]])
                    P.op("act", lambda e, c0=c0: e.activation(out=st[:, c0 + 1:c0 + 2], in_=st[:, c0:c0 + 1], func=AF.Ln,
                                                              scale=1.0 / D, bias=EPS), reads=[T_st[k]], writes=[T_st[k]])
                    P.op("act", lambda e, c0=c0: e.activation(out=st[:, c0 + 2:c0 + 3], in_=st[:, c0 + 1:c0 + 2], func=AF.Exp,
                                                              scale=-0.5), reads=[T_st[k]], writes=[T_st[k]])
                    def finish(ti=ti, c0=c0, sl=sl, k=k, t0=t0):
                        P.op("dve", lambda e: e.scalar_tensor_tensor(
                            out=yt[sl][:], in0=hres[:, ti, :], scalar=st[:, c0 + 2:c0 + 3], in1=growf[:], op0=ALU.mult, op1=ALU.mult),
                            reads=[T_h[ti], T_st[k], T_growf], writes=[T_yt[sl]])
                        P.op("sp", lambda e: e.dma_start(out=y[s, t0:t0 + 128, :], in_=yt[sl][:]),
                             reads=[T_yt[sl]], dma=True)
                    fin.append(finish)
                    if len(fin) >= 2:
                        fin.pop(0)()
            for f_ in fin:
                f_()

        flat = [(gi, jj, j) for gi, js in enumerate(jlist) for jj, j in enumerate(js)]
        for idx, (gi, jj, j) in enumerate(flat):
            if j >= 2:
                hu_ops(j)
            if idx > 0:
                pgi, pjj, pj = flat[idx - 1]
                conv_ops(pj, pjj)
                if pjj == len(jlist[pgi]) - 1:
                    down_ops(pgi)
            if j + 2 < NJ:
                load_w(j + 2)
        pgi, pjj, pj = flat[-1]
        conv_ops(pj, pjj)
        down_ops(pgi)
        names = ["u2T", "grow2", "growf", "ubf0", "ubf1", "cvs0", "cvs1", "junk", "h", "yt0", "yt1"]
        names += [f"wg{i}" for i in range(NWU)] + [f"wv{i}" for i in range(NWU)] + [f"wd{i}" for i in range(NWD)]
        names += [f"actT{i}" for i in range(6)] + ["hug0", "hug1", "huv0", "huv1", "dg0", "dg1", "sg0", "sg1"]
        A.free(*names)
        P.barrier()
        A.release()
        chk(s, 5)

    try:
        for s_ in range(2):
            seq(s_)
    except _Stop:
        P.barrier()
        for name in (dbg or []):
            t = DBG[name]
            shp = list(t[:].shape)
            dd = nc.dram_tensor("dbg_" + name, shp, t[:].dtype, kind="ExternalOutput").ap()
            P.op("sp", lambda e, t=t, dd=dd: e.dma_start(out=dd, in_=t[:]), dma=True)
        P.barrier()

    semnames = list(Prog.ENGS) + [(e, i) for e in Prog.ENGS for i in range(Prog.NRING)]
    with contextlib.ExitStack() as es:
        sems = {k: es.enter_context(nc.semaphore("s_" + str(k).replace(" ", "").replace("'", "").replace("(", "")
                                                 .replace(")", "").replace(",", "_"))) for k in semnames}
        P.finalize(sems)
        block = es.enter_context(nc.Block())
        block.tensor(lambda e: P.emit_engine("pe", e))
        block.scalar(lambda e: P.emit_engine("act", e))
        block.vector(lambda e: P.emit_engine("dve", e))
        block.gpsimd(lambda e: P.emit_engine("pool", e))
        block.sync(lambda e: P.emit_engine("sp", e))
    return nc


def _rope_tables():
    inv = (10000.0 ** (-np.arange(0, 64, 2, dtype=np.float32) / 64.0)).astype(np.float32)
    p = np.arange(128)
    f = p % 32
    sign = np.where((p % 64) < 32, -1.0, 1.0).astype(np.float32)
    t = np.arange(S, dtype=np.float32)
    ang1 = (t[None, :] * inv[f][:, None]).astype(np.float32)
    rope1 = np.stack([np.cos(ang1), sign[:, None] * np.sin(ang1)], 0).astype(np.float32)
    pos = np.arange(64, dtype=np.float32)
    angg = (pos[None, :] * inv[f][:, None]).astype(np.float32)
    ropeg = np.concatenate([np.cos(angg), sign[:, None] * np.sin(angg)], 1).astype(np.float32)
    return rope1, ropeg


def _consts():
    ident = np.eye(128, dtype=np.float32)
    ones = np.ones((128, 128), np.float32)
    perm = np.zeros((128, 128), np.float32)
    for m in range(128):
        k = m + 32 if (m % 64) < 32 else m - 32
        perm[k, m] = 1.0
    return np.concatenate([ident, ones, perm], 1).astype(ml_dtypes.bfloat16)


_NC_CACHE = {}


def kernel(x, norm1_g, w_in, mla_q_norm_g, mla_w_q_up, mla_kv_norm_g, mla_w_kv_up,
           gqa_q_norm_g, gqa_k_norm_g, group_norm_mla_g, group_norm_gqa_g, w_out,
           norm2_g, ffn_w_up, ffn_conv_w, ffn_conv_b, ffn_w_down, final_norm_g):
    f32 = np.float32
    x = np.asarray(x, f32)

    def colify(v, n):
        return np.asarray(v, f32).reshape(n, 128).T

    cols = np.concatenate([
        colify(mla_q_norm_g[0], 3), colify(mla_kv_norm_g[0], 2),
        colify(gqa_q_norm_g[0], 1), colify(gqa_k_norm_g[0], 1),
        colify(group_norm_mla_g[0], 4), colify(group_norm_gqa_g[0], 4),
        colify(ffn_conv_w[0, 0], 44), colify(ffn_conv_w[0, 1], 44), colify(ffn_conv_w[0, 2], 44),
        colify(ffn_conv_b[0], 44)], 1)
    cols = np.ascontiguousarray(cols, f32)
    assert cols.shape == (128, NCOLS)
    grows = np.ascontiguousarray(np.stack([
        np.broadcast_to(np.asarray(norm1_g[0], f32), (128, D)),
        np.broadcast_to(np.asarray(norm2_g[0], f32), (128, D)),
        np.broadcast_to(np.asarray(final_norm_g, f32), (128, D))], 0))
    rope1, ropeg = _rope_tables()
    cst = _consts()
    if "nc" not in _NC_CACHE:
        _NC_CACHE["nc"] = build_nc()
    nc = _NC_CACHE["nc"]
    shared = {
        "w_in": np.ascontiguousarray(w_in[0], f32), "wq": np.ascontiguousarray(mla_w_q_up[0], f32),
        "wkv": np.ascontiguousarray(mla_w_kv_up[0], f32), "w_out": np.ascontiguousarray(w_out[0], f32),
        "w_up": np.ascontiguousarray(ffn_w_up[0], f32), "w_down": np.ascontiguousarray(ffn_w_down[0], f32),
        "cst": cst, "cols": cols, "grows": grows, "rope1": rope1, "ropeg": ropeg,
    }
    in_maps = []
    for c in range(8):
        m = dict(shared)
        m["x"] = np.ascontiguousarray(x[2 * c:2 * c + 2])
        in_maps.append(m)
    res = run_bass_kernel_spmd(nc, in_maps, core_ids=list(range(8)))
    out = np.concatenate([np.asarray(r["y"], f32) for r in res.results], 0)
    return out
```
